# Optimizing a Trainium2 kernel written in Bass

```python
import math
import jax, jax.numpy as jnp
from jax import lax
import numpy as np

D_MODEL = 2048
BATCH = 32
SEQ = 256
DEPTH = 1
DEC_BATCH = 2
DEC_SEQ = 2048
PAST_LEN = 256

GRID_W = 64
D_A = 1024
HEAD = 64
H_A = D_A // HEAD
LORA_W = 64
LORA_A = 64
LORA_G = 160
D_B = 1024
CHUNK = 128
G_B = 8
C_B = D_B // G_B
D_FF = 5504
N_SHIFT = 3 * D_A + LORA_W + LORA_A + LORA_G
N_IN = N_SHIFT + 2 * D_B + 2 * D_MODEL
RW_SPLITS = (D_A, 2 * D_A, 3 * D_A, 3 * D_A + LORA_W, 3 * D_A + LORA_W + LORA_A)
ALPHA = (2.0 * DEPTH) ** 0.25
BETA = (8.0 * DEPTH) ** -0.25
LN_EPS = 1e-5
GN_EPS = 64e-5
NRM_EPS = 1e-12

kernel_name = 'hybrid_rwkv7_gmlp_diffusion_step'


def _layernorm(x, g, b, eps=LN_EPS):
    xf = x.astype(jnp.float32)
    mu = jnp.mean(xf, -1, keepdims=True)
    var = jnp.mean(jnp.square(xf - mu), -1, keepdims=True)
    return ((xf - mu) * lax.rsqrt(var + eps)).astype(x.dtype) * g + b


def _swiglu(h, w_in, w_out):
    a, b = jnp.split(h @ w_in, 2, axis=-1)
    return (jax.nn.silu(a) * b) @ w_out


def _neighbour_mean(s, on_grid):
    if on_grid:
        B, T, C = s.shape
        rows = T // GRID_W
        p = jnp.pad(s.reshape(B, rows, GRID_W, C), ((0, 0), (1, 1), (1, 1), (0, 0)))
        nb = (p[:, :-2, 1:-1] + p[:, 2:, 1:-1] + p[:, 1:-1, :-2] + p[:, 1:-1, 2:]) * 0.25
        return nb.reshape(B, T, C)
    p = jnp.pad(s, ((0, 0), (1, 1), (0, 0)))
    return (p[:, :-2] + p[:, 2:]) * 0.5


def _rwkv_scan(S0, r, w, k, v, kk, b, reverse):
    def step(S, inp):
        r_t, w_t, k_t, v_t, kk_t, b_t = inp
        Skk = jnp.einsum('bhvk,bhk->bhv', S, kk_t)
        S = S * w_t[:, :, None, :] - Skk[..., None] * b_t[:, :, None, :] + v_t[..., None] * k_t[:, :, None, :]
        return S, jnp.einsum('bhvk,bhk->bhv', S, r_t)
    xs = tuple(jnp.swapaxes(a, 0, 1) for a in (r, w, k, v, kk, b))
    S_T, ys = lax.scan(step, S0, xs, reverse=reverse)
    return S_T, jnp.swapaxes(ys, 0, 1)


def _rwkv_branch(z, S0f, S0b, w0, w2, a0, a2, g2, k_k, k_a, r_k, gn_g, gn_b):
    z = z.astype(jnp.float32)
    B, T, _ = z.shape
    hd = lambda t: t.reshape(B, T, H_A, HEAD)
    r, k, v, xw, xa, xg = jnp.split(z, RW_SPLITS, axis=-1)
    g = jax.nn.sigmoid(xg) @ g2
    kk = hd(k * k_k)
    kk = kk / jnp.maximum(jnp.sqrt(jnp.sum(kk * kk, -1, keepdims=True)), NRM_EPS)
    r4, v4 = hd(r), hd(v)
    inits = (S0f.astype(jnp.float32), S0b.astype(jnp.float32))
    y = jnp.zeros_like(r4)
    bonus = jnp.zeros_like(r4)
    finals = []
    for d in range(2):
        wlog = -jax.nn.softplus(-(w0[d] + jnp.tanh(xw) @ w2[d])) - 0.5
        decay = jnp.exp(-jnp.exp(wlog))
        a = jax.nn.sigmoid(a0[d] + xa @ a2[d])
        kd = hd(k * (1.0 + (a - 1.0) * k_a))
        S_T, yd = _rwkv_scan(inits[d], r4, hd(decay), kd, v4, kk, kk * hd(a), reverse=(d == 1))
        y = y + yd
        bonus = bonus + jnp.sum(r4 * kd * r_k, -1, keepdims=True) * v4
        finals.append(S_T)
    mu = jnp.mean(y, -1, keepdims=True)
    var = jnp.mean(jnp.square(y - mu), -1, keepdims=True)
    yn = ((y - mu) * lax.rsqrt(var + GN_EPS)).reshape(B, T, D_A) * gn_g + gn_b
    out = (yn + bonus.reshape(B, T, D_A)) * g
    return out, finals[0], finals[1]


def _spatial_gating(uv, ln_g, ln_b, w_s, b_s):
    u, v = jnp.split(uv, 2, axis=-1)
    v = _layernorm(v, ln_g, ln_b)
    B, T, _ = v.shape
    vc = v.reshape(B, T // CHUNK, CHUNK, G_B, C_B)
    s = jnp.einsum('gpq,bnqgc->bnpgc', w_s, vc) + jnp.transpose(b_s)[None, None, :, :, None]
    return u * s.reshape(B, T, D_B)


def _layer(x, cond, on_grid, S0f, S0b, p):
    mod = (jax.nn.silu(cond) @ p['w_ada'] + p['b_ada']).reshape(cond.shape[0], 1, 9, D_MODEL)
    shift = lambda i: mod[:, :, 3 * i]
    scale = lambda i: mod[:, :, 3 * i + 1]
    gate = lambda i: mod[:, :, 3 * i + 2]
    modulate = lambda t, i: t * (1.0 + scale(i)) + shift(i)
    f1 = _swiglu(modulate(x, 0), p['ffn_w_in'][0], p['ffn_w_out'][0])
    x = _layernorm(ALPHA * x + 0.5 * gate(0) * f1, p['ln_g'][0], p['ln_b'][0])
    h = modulate(x, 1)
    proj = h @ p['w_in']
    zs = proj[..., :N_SHIFT]
    zs = zs + p['shift_mu'] * (_neighbour_mean(zs, on_grid) - zs)
    yA, Sf, Sb = _rwkv_branch(zs, S0f, S0b, p['rw_w0'], p['rw_w2'], p['rw_a0'], p['rw_a2'], p['rw_g2'],
                              p['rw_k_k'], p['rw_k_a'], p['rw_r_k'], p['rw_gn_g'], p['rw_gn_b'])
    uv = jax.nn.gelu(proj[..., N_SHIFT:N_SHIFT + 2 * D_B], approximate=False)
    yB = _spatial_gating(uv, p['sg_ln_g'], p['sg_ln_b'], p['sg_w'], p['sg_b'])
    gts = jax.nn.sigmoid(proj[..., N_SHIFT + 2 * D_B:])
    merged = gts[..., :D_MODEL] * (yA.astype(x.dtype) @ p['w_pa']) + gts[..., D_MODEL:] * (yB @ p['w_pb'])
    x = _layernorm(ALPHA * x + gate(1) * (merged @ p['w_o']), p['ln_g'][1], p['ln_b'][1])
    f2 = _swiglu(modulate(x, 2), p['ffn_w_in'][1], p['ffn_w_out'][1])
    x = _layernorm(ALPHA * x + 0.5 * gate(2) * f2, p['ln_g'][2], p['ln_b'][2])
    return x, Sf, Sb


def setup_inputs(seed: int = 0) -> dict:
    key = jax.random.key(seed)
    ks = jax.random.split(key, 40)
    nrm = lambda k, shape, s: jax.random.normal(k, shape, jnp.float32) * s
    L = DEPTH
    return {
        'x_prompt': nrm(ks[0], (BATCH, SEQ, D_MODEL), 1.0),
        'x_sample': nrm(ks[1], (DEC_BATCH, DEC_SEQ, D_MODEL), 1.0),
        'c': nrm(ks[2], (DEC_BATCH, D_MODEL), 1.0),
        'state_rwkv_fwd': nrm(ks[3], (DEC_BATCH, L, H_A, HEAD, HEAD), 0.1),
        'state_rwkv_bwd': nrm(ks[4], (DEC_BATCH, L, H_A, HEAD, HEAD), 0.1),
        'c_ctx': nrm(ks[5], (D_MODEL,), 1.0),
        'w_ada': nrm(ks[6], (L, D_MODEL, 9 * D_MODEL), 0.5 * D_MODEL ** -0.5),
        'b_ada': nrm(ks[7], (L, 9 * D_MODEL), 0.01),
        'ln_g': 1.0 + nrm(ks[8], (L, 3, D_MODEL), 0.01),
        'ln_b': nrm(ks[9], (L, 3, D_MODEL), 0.01),
        'ffn_w_in': nrm(ks[10], (L, 2, D_MODEL, 2 * D_FF), D_MODEL ** -0.5),
        'ffn_w_out': nrm(ks[11], (L, 2, D_FF, D_MODEL), BETA * D_FF ** -0.5),
        'w_in': nrm(ks[12], (L, D_MODEL, N_IN), D_MODEL ** -0.5),
        'shift_mu': jax.random.uniform(ks[13], (L, N_SHIFT), jnp.float32),
        'rw_w0': jax.random.uniform(ks[14], (L, 2, D_A), jnp.float32, minval=-3.0, maxval=0.0),
        'rw_w2': nrm(ks[15], (L, 2, LORA_W, D_A), 0.1 * LORA_W ** -0.5),
        'rw_a0': nrm(ks[16], (L, 2, D_A), 0.1),
        'rw_a2': nrm(ks[17], (L, 2, LORA_A, D_A), 0.1 * LORA_A ** -0.5),
        'rw_g2': nrm(ks[18], (L, LORA_G, D_A), LORA_G ** -0.5),
        'rw_k_k': 0.85 + nrm(ks[19], (L, D_A), 0.02),
        'rw_k_a': 1.0 + nrm(ks[20], (L, D_A), 0.02),
        'rw_r_k': nrm(ks[21], (L, H_A, HEAD), 0.1),
        'rw_gn_g': 1.0 + nrm(ks[22], (L, D_A), 0.01),
        'rw_gn_b': nrm(ks[23], (L, D_A), 0.01),
        'sg_ln_g': 1.0 + nrm(ks[24], (L, D_B), 0.01),
        'sg_ln_b': nrm(ks[25], (L, D_B), 0.01),
        'sg_w': nrm(ks[26], (L, G_B, CHUNK, CHUNK), 0.5 * CHUNK ** -0.5),
        'sg_b': 1.0 + nrm(ks[27], (L, G_B, CHUNK), 0.01),
        'w_pa': nrm(ks[28], (L, D_A, D_MODEL), D_A ** -0.5),
        'w_pb': nrm(ks[29], (L, D_B, D_MODEL), D_B ** -0.5),
        'w_o': nrm(ks[30], (L, D_MODEL, D_MODEL), BETA * D_MODEL ** -0.5),
    }


def reference(x_prompt, x_sample, c, state_rwkv_fwd, state_rwkv_bwd, c_ctx, w_ada, b_ada, ln_g, ln_b,
              ffn_w_in, ffn_w_out, w_in, shift_mu, rw_w0, rw_w2, rw_a0, rw_a2, rw_g2, rw_k_k, rw_k_a,
              rw_r_k, rw_gn_g, rw_gn_b, sg_ln_g, sg_ln_b, sg_w, sg_b, w_pa, w_pb, w_o):
    xp, xs = x_prompt, x_sample
    zero_state = jnp.zeros((x_prompt.shape[0], H_A, HEAD, HEAD), jnp.float32)
    cond_ctx = c_ctx[None, :]
    new_f, new_b = [], []
    for l in range(DEPTH):
        p = {'w_ada': w_ada[l], 'b_ada': b_ada[l], 'ln_g': ln_g[l], 'ln_b': ln_b[l],
             'ffn_w_in': ffn_w_in[l], 'ffn_w_out': ffn_w_out[l], 'w_in': w_in[l], 'shift_mu': shift_mu[l],
             'rw_w0': rw_w0[l], 'rw_w2': rw_w2[l], 'rw_a0': rw_a0[l], 'rw_a2': rw_a2[l], 'rw_g2': rw_g2[l],
             'rw_k_k': rw_k_k[l], 'rw_k_a': rw_k_a[l], 'rw_r_k': rw_r_k[l], 'rw_gn_g': rw_gn_g[l],
             'rw_gn_b': rw_gn_b[l], 'sg_ln_g': sg_ln_g[l], 'sg_ln_b': sg_ln_b[l], 'sg_w': sg_w[l],
             'sg_b': sg_b[l], 'w_pa': w_pa[l], 'w_pb': w_pb[l], 'w_o': w_o[l]}
        xp, sf, sb = _layer(xp, cond_ctx, False, zero_state, zero_state, p)
        new_f.append(sf)
        new_b.append(sb)
        xs, _, _ = _layer(xs, c, True, state_rwkv_fwd[:, l], state_rwkv_bwd[:, l], p)
    new_state_fwd = jnp.stack(new_f, axis=1).astype(x_prompt.dtype)
    new_state_bwd = jnp.stack(new_b, axis=1).astype(x_prompt.dtype)
    return (xp, xs, new_state_fwd, new_state_bwd)
```

```python
import math
from contextlib import ExitStack
from types import SimpleNamespace
import numpy as np
import concourse.bass as bass
import concourse.mybir as mybir
from concourse.bass_utils import run_bass_kernel_spmd

F32 = mybir.dt.float32
F32R = mybir.dt.float32r
BF16 = mybir.dt.bfloat16
AF = mybir.ActivationFunctionType
ALU = mybir.AluOpType

D = 2048; DFF = 5504; DA = 1024; HD = 64; NH = 16; DB = 1024; NSH = 3360; NIN = 9504
LW = 64; LA = 64; LG = 160
ALPHA = 2.0 ** 0.25
LN_EPS = 1e-5; GN_EPS = 64e-5
TG = 256
NG = 12
NPG = 4
TD = 512
NSG = 6
C = 128
PADZ = 64
EXPC = math.exp(-0.5)


class Res:
    __slots__ = ("name", "w", "readers", "excl")

    def __init__(self, name="", excl=False):
        self.name = name
        self.w = None
        self.readers = {}
        self.excl = excl


class Sched:
    ENGS = ("pe", "act", "dve", "pool", "sp")

    def __init__(self, nc, n_dma_sems=40):
        self.nc = nc
        self.prog = {e: [] for e in self.ENGS}
        self.sems = {e: nc.alloc_semaphore(f"s_{e}") for e in self.ENGS}
        self.count = {e: 0 for e in self.ENGS}
        self.known = {e: {} for e in self.ENGS}
        self.dma_sems = []
        for i in range(n_dma_sems):
            k = f"d{i}"
            self.sems[k] = nc.alloc_semaphore(f"s_{k}")
            self.dma_sems.append(k)
        self.dma_cnt = {k: 0 for k in self.dma_sems}
        self.dma_rr = 0
        self.out_tokens = []

    def _deps(self, eng, reads, writes):
        deps = {}
        ex = [r for r in reads if r.excl]
        if ex:
            writes = list(writes) + ex

        def add(tok):
            if tok is not None and deps.get(tok[0], 0) < tok[1]:
                deps[tok[0]] = tok[1]

        for r in reads:
            add(r.w)
        for w in writes:
            add(w.w)
            for k, v in w.readers.items():
                add((k, v))
        waits = []
        kn = self.known[eng]
        for k, v in deps.items():
            if k == eng and eng == "pe":
                continue
            if kn.get(k, 0) < v:
                kn[k] = v
                waits.append((k, v))
        return waits

    def _mark(self, tok, reads, writes):
        k, v = tok
        ex = [r for r in reads if r.excl]
        if ex:
            writes = list(writes) + ex
        for r in reads:
            if r.readers.get(k, 0) < v:
                r.readers[k] = v
        for w in writes:
            w.w = tok
            w.readers = {}

    def op(self, eng, fn, reads=(), writes=()):
        waits = self._deps(eng, reads, writes)
        self.count[eng] += 1
        tok = (eng, self.count[eng])
        self.prog[eng].append((waits, fn, (eng, 1)))
        self._mark(tok, reads, writes)
        return tok

    def pe_group(self, fns, reads=(), writes=()):
        waits = self._deps("pe", reads, writes)
        self.count["pe"] += 1
        tok = ("pe", self.count["pe"])
        n = len(fns)
        for i, fn in enumerate(fns):
            self.prog["pe"].append((waits if i == 0 else [], fn, ("pe", 1) if i == n - 1 else None))
        self._mark(tok, reads, writes)
        return tok

    def dma(self, eng, fn, reads=(), writes=(), is_output=False):
        k = self.dma_sems[self.dma_rr % len(self.dma_sems)]
        self.dma_rr += 1
        waits = self._deps(eng, reads, writes)
        prev = 16 * self.dma_cnt[k]
        kn = self.known[eng]
        if prev > 0 and kn.get(k, 0) < prev:
            kn[k] = prev
            waits.append((k, prev))
        self.dma_cnt[k] += 1
        tok = (k, 16 * self.dma_cnt[k])
        self.prog[eng].append((waits, fn, (k, 16)))
        self._mark(tok, reads, writes)
        if is_output:
            self.out_tokens.append(tok)
        return tok

    def barrier(self):
        tg = {e: self.count[e] for e in self.ENGS if self.count[e] > 0}
        tg.update({k: 16 * c for k, c in self.dma_cnt.items() if c > 0})
        for e in self.ENGS:
            waits = []
            kn = self.known[e]
            for k, v in tg.items():
                if k == e:
                    continue
                if kn.get(k, 0) < v:
                    kn[k] = v
                    waits.append((k, v))
            if waits:
                self.prog[e].append((waits, None, None))

    def emit(self):
        nc = self.nc
        fin = {}
        for k, v in self.out_tokens:
            fin[k] = max(fin.get(k, 0), v)
        final_waits = list(fin.items())
        names = {"pe": "tensor", "act": "scalar", "dve": "vector", "pool": "gpsimd", "sp": "sync"}
        with nc.Block() as block:
            for e in self.ENGS:
                prog = self.prog[e]
                if not prog and not (e == "sp" and final_waits):
                    continue

                def body(engine, prog=prog, e=e):
                    for waits, fn, inc in prog:
                        for k, v in waits:
                            engine.wait_ge(self.sems[k], v)
                        if fn is None:
                            continue
                        ins = fn(engine)
                        if inc is not None:
                            ins.then_inc(self.sems[inc[0]], inc[1])
                    if e == "sp":
                        for k, v in final_waits:
                            engine.wait_ge(self.sems[k], v)

                getattr(block, names[e])(body)


class Prog:
    pass


def build_program():
    nc = bass.Bass("TRN2", target_bir_lowering=False)
    S = Sched(nc)
    cur = [None]

    def din(name, shape, dt=F32):
        return nc.dram_tensor(name, list(shape), dt, kind="ExternalInput").ap()

    def dout(name, shape, dt=F32):
        return nc.dram_tensor(name, list(shape), dt, kind="ExternalOutput").ap()

    def dscr(name, shape, dt=F32):
        return nc.dram_tensor(name, list(shape), dt, kind="Internal").ap()

    uid = [0]

    def sb(name, shape, dt=F32):
        uid[0] += 1
        name = f"{name}_{uid[0]}"
        if cur[0] is not None:
            return cur[0].enter_context(nc.sbuf_tensor(name, list(shape), dt))
        return nc.alloc_sbuf_tensor(name, list(shape), dt)

    xT = din("xT", [NSG, 128, 16, TD])
    condT = din("condT", [128, 16, 2])
    h0in = din("h0in", [2, NH, HD, HD])
    w_ada = din("w_ada", [144, 128, 16, 128])
    b_adaT = din("b_adaT", [128, 144])
    lnT = din("lnT", [128, 6, 16])
    w_f1 = din("w_f1", [86, 128, 16, 128])
    w_f1o = din("w_f1o", [16, 128, 43, 128])
    w_f2 = din("w_f2", [86, 128, 16, 128])
    w_f2o = din("w_f2o", [16, 128, 43, 128])
    w_rkv = din("w_rkv", [24, 128, 16, 128])
    w_lora = din("w_lora", [3, 128, 16, 128])
    w_u = din("w_u", [8, 128, 16, 128])
    w_v = din("w_v", [4, 128, 16, 256])
    w_gate = din("w_gate", [32, 128, 16, 128])
    p64 = din("p64", [64, 12, NH])
    NP64 = 12
    w2T = din("w2T", [64, 2, DA])
    a2T = din("a2T", [64, 2, DA])
    g2a = din("g2a", [128, DA])
    g2b = din("g2b", [32, DA])
    mu_l = din("mu_l", [128, 3])
    sgln = din("sgln", [128, 2, DB])
    sgwT = din("sgwT", [128, 8, 128])
    sgb = din("sgb", [128, 8 * 128])
    w_pa = din("w_pa", [16, 64, 16, 128])
    w_pb = din("w_pb", [16, 128, 8, 128])
    w_o = din("w_o", [16, 128, 16, 128])
    masks = din("masks", [128, 2, 3, 128])
    ident_in = din("ident_in", [128, 128])

    yT = dout("yT", [NSG, 128, 16, TD])
    st_out = dout("st_out", [2, 4, NH, HD, HD])

    a0in = din("a0in", [64, 2, NH])
    X1 = dscr("X1", [NSG, 128, 16, TD])
    ZR = [dscr(f"ZR{s}", [48, 64, (256 if s < 4 else 2048) + 2 * PADZ]) for s in range(5)]
    ZLP = [dscr(f"ZLP{s}", [3, 128, (256 if s < 4 else 2048) + 2 * PADZ]) for s in range(5)]
    GTs = dscr("GTs", [NSG, 128, 32, TD], BF16)
    UTs = dscr("UTs", [NSG, 128, 8, TD], BF16)
    VNs = dscr("VNs", [NSG, 4, 128, DB], BF16)
    OA = dscr("OA", [NSG, 64, NH, TD], BF16)
    PREP = dscr("PREP", [NG, NH, 4, 64, TG])
    YF = dscr("YF", [NG, NH, 64, TG])
    RKF = dscr("RKF", [NG, NH, 64, TG])

    ident = sb("ident", [128, 128]); r_const = Res("const")
    ones = sb("ones", [128, 128])
    onesb = sb("onesb", [128, 128], BF16)
    zeros = sb("zeros", [128, 128])
    msk = sb("msk", [128, 2, 3, 128])
    p64s = sb("p64s", [64, NP64, NH])
    lns = sb("lns", [128, 6, 16])
    mus = sb("mus", [128, 3])
    S.dma("sp", lambda e: e.dma_start(out=ident[:], in_=ident_in[:, :]), writes=[r_const])
    S.dma("sp", lambda e: e.dma_start(out=msk[:], in_=masks[:, :, :, :]), writes=[r_const])
    S.dma("sp", lambda e: e.dma_start(out=p64s[:], in_=p64[:, :, :]), writes=[r_const])
    S.dma("sp", lambda e: e.dma_start(out=lns[:], in_=lnT[:, :, :]), writes=[r_const])
    S.dma("sp", lambda e: e.dma_start(out=mus[:], in_=mu_l[:, :]), writes=[r_const])
    S.op("dve", lambda e: e.memset(ones[:], 1.0), writes=[r_const])
    S.op("dve", lambda e: e.memset(onesb[:], 1.0), writes=[r_const])
    S.op("dve", lambda e: e.memset(zeros[:], 0.0), writes=[r_const])

    r_zpad = Res("zpad")
    for s in range(5):
        L = 256 if s < 4 else 2048
        for side in (0, 1):
            off = 0 if side == 0 else PADZ + L
            for q in range(48):
                S.dma("sp", lambda e, s=s, q=q, off=off: e.dma_start(out=ZR[s][q, :, off:off + PADZ], in_=zeros[0:64, 0:PADZ]),
                      reads=[r_const], writes=[r_zpad])
            for q in range(3):
                S.dma("sp", lambda e, s=s, q=q, off=off: e.dma_start(out=ZLP[s][q, :, off:off + PADZ], in_=zeros[:, 0:PADZ]),
                      reads=[r_const], writes=[r_zpad])

    pd = [nc.alloc_psum_tensor(f"pd{i}", [128, 512], F32) for i in range(4)]
    r_pd = [Res(f"pd{i}", excl=True) for i in range(4)]
    pr = [nc.alloc_psum_tensor(f"pr{i}", [128, 512], F32) for i in range(4)]
    pdi = [0]

    def next_pd():
        i = pdi[0] % 4
        pdi[0] += 1
        return pd[i], r_pd[i]

    mod = sb("mod", [128, 144, 2]); r_modS = [Res(f"mod{i}") for i in range(3)]
    modd = sb("modd", [128, 9, 16, 2])
    a0s = sb("a0s", [64, 2, NH])
    S.dma("sp", lambda e: e.dma_start(out=a0s[:], in_=a0in[:, :, :]), writes=[r_const])

    def mcol(i, which, kc, cnd):
        return modd[:, 3 * i + which, kc, cnd:cnd + 1]

    def sg_cnd(sg):
        return 0 if sg < 2 else 1

    def sg_segs(sg):
        if sg < 2:
            return [(2 * sg, 0, 256, 0), (2 * sg + 1, 0, 256, 256)]
        return [(4, (sg - 2) * TD, TD, 0)]

    def g_seg(g):
        if g < NPG:
            return (g, 0)
        return (4, (g - NPG) * TG)

    def dense_env():
        NSLOT = 4
        wslot = [sb(f"wslot{i}", [128, 43 * 128], BF16) for i in range(NSLOT)]
        r_wslot = [Res(f"wslot{i}") for i in range(NSLOT)]
        wsi = [0]
        xa_t = sb("xa_t", [128, 16, TD]); r_xa = Res("xa")
        h_t = sb("h_t", [128, 16, TD], BF16); r_h = Res("h")
        GBIG = sb("GBIG", [128, 43 * TD], BF16); r_G = Res("G")
        G_t = GBIG[:].rearrange("p (k t) -> p k t", t=TD)
        sq_t = [sb(f"sq_t{i}", [128, TD]) for i in range(2)]; r_sq = [Res(f"sq{i}") for i in range(2)]
        sa_t = sb("sa_t", [128, TD]); r_sa = Res("sa")
        st_mean = sb("st_mean", [128, TD]); st_rstd = sb("st_rstd", [128, TD]); st_tmp = sb("st_tmp", [128, TD]); r_st = Res("st")

        def load_w(src_ap, kp, kc, cols):
            i = wsi[0] % NSLOT
            wsi[0] += 1
            view = wslot[i][0:kp, 0:kc * cols].rearrange("p (k c) -> p k c", c=cols)
            S.dma("pool", lambda e: e.dma_start(out=view, in_=src_ap), writes=[r_wslot[i]])
            return view, r_wslot[i]

        def modulate(i, cnd):
            for kc in range(16):
                S.op("act", lambda e, kc=kc: e.activation(out=h_t[:, kc, :], in_=xa_t[:, kc, :], func=AF.Identity,
                                                          scale=mcol(i, 1, kc, cnd), bias=mcol(i, 0, kc, cnd)),
                     reads=[r_xa, r_modS[i]], writes=[r_h])

        def layernorm(li):
            x, rx = xa_t, r_xa
            p1, rp1 = next_pd()
            p2, rp2 = next_pd()
            S.pe_group([(lambda e, kc=kc: e.matmul(p1[:, :], lhsT=ones[:, :], rhs=x[:, kc, :], start=(kc == 0), stop=(kc == 15))) for kc in range(16)],
                       reads=[rx, r_const], writes=[rp1])
            for kc in range(16):
                i = kc % 2
                S.op("act", lambda e, kc=kc, i=i: e.activation(out=sq_t[i][:], in_=x[:, kc, :], func=AF.Square), reads=[rx], writes=[r_sq[i]])
                S.pe_group([lambda e, kc=kc, i=i: e.matmul(p2[:, :], lhsT=ones[:, :], rhs=sq_t[i][:], start=(kc == 0), stop=(kc == 15))],
                           reads=[r_sq[i], r_const], writes=[rp2])
            S.op("dve", lambda e: e.tensor_scalar(out=st_mean[:], in0=p1[:, :], scalar1=1.0 / D, scalar2=None, op0=ALU.mult), reads=[rp1], writes=[r_st])
            S.op("dve", lambda e: e.tensor_tensor(out=st_tmp[:], in0=st_mean[:], in1=st_mean[:], op=ALU.mult), reads=[r_st], writes=[r_st])
            S.op("dve", lambda e: e.scalar_tensor_tensor(out=st_tmp[:], in0=p2[:, :], scalar=1.0 / D, in1=st_tmp[:], op0=ALU.mult, op1=ALU.subtract),
                 reads=[rp2, r_st], writes=[r_st])
            S.op("dve", lambda e: e.tensor_scalar(out=st_tmp[:], in0=st_tmp[:], scalar1=LN_EPS, scalar2=None, op0=ALU.add), reads=[r_st], writes=[r_st])
            S.op("act", lambda e: e.activation(out=st_tmp[:], in_=st_tmp[:], func=AF.Sqrt), reads=[r_st], writes=[r_st])
            S.op("dve", lambda e: e.reciprocal(out=st_rstd[:], in_=st_tmp[:]), reads=[r_st], writes=[r_st])
            for kc in range(16):
                S.op("dve", lambda e, kc=kc: e.tensor_tensor(out=x[:, kc, :], in0=x[:, kc, :], in1=st_mean[:], op=ALU.subtract), reads=[rx, r_st], writes=[rx])
                S.op("dve", lambda e, kc=kc: e.tensor_tensor(out=x[:, kc, :], in0=x[:, kc, :], in1=st_rstd[:], op=ALU.mult), reads=[r_st, rx], writes=[rx])
                S.op("act", lambda e, kc=kc: e.activation(out=x[:, kc, :], in_=x[:, kc, :], func=AF.Identity,
                                                          scale=lns[:, li, kc:kc + 1], bias=lns[:, 3 + li, kc:kc + 1]), reads=[rx, r_const], writes=[rx])

        def ffn(w_in_d, w_out_d, stage, cnd):
            x, rx = xa_t, r_xa
            for j in range(43):
                wa, rwa = load_w(w_in_d[2 * j, :, :, :], 128, 16, 128)
                wb, rwb = load_w(w_in_d[2 * j + 1, :, :, :], 128, 16, 128)
                pa, rpa = next_pd()
                pb, rpb = next_pd()
                S.pe_group([(lambda e, k=k, wa=wa, pa=pa: e.matmul(pa[:, :], lhsT=wa[:, k, :], rhs=h_t[:, k, :], start=(k == 0), stop=(k == 15))) for k in range(16)],
                           reads=[rwa, r_h], writes=[rpa])
                S.pe_group([(lambda e, k=k, wb=wb, pb=pb: e.matmul(pb[:, :], lhsT=wb[:, k, :], rhs=h_t[:, k, :], start=(k == 0), stop=(k == 15))) for k in range(16)],
                           reads=[rwb, r_h], writes=[rpb])
                S.op("act", lambda e, pa=pa: e.activation(out=sa_t[:], in_=pa[:, :], func=AF.Silu), reads=[rpa], writes=[r_sa])
                S.op("dve", lambda e, pb=pb, j=j: e.tensor_tensor(out=G_t[:, j, :], in0=sa_t[:], in1=pb[:, :], op=ALU.mult), reads=[r_sa, rpb], writes=[r_G])
            for dc in range(16):
                wo, rwo = load_w(w_out_d[dc, :, :, :], 128, 43, 128)
                po, rpo = next_pd()
                S.pe_group([(lambda e, k=k, wo=wo, po=po: e.matmul(po[:, :], lhsT=wo[:, k, :], rhs=G_t[:, k, :], start=(k == 0), stop=(k == 42))) for k in range(43)],
                           reads=[rwo, r_G], writes=[rpo])
                S.op("act", lambda e, dc=dc: e.activation(out=x[:, dc, :], in_=x[:, dc, :], func=AF.Identity, scale=ALPHA), reads=[rx], writes=[rx])
                S.op("dve", lambda e, dc=dc, po=po: e.scalar_tensor_tensor(out=x[:, dc, :], in0=po[:, :], scalar=mcol(stage, 2, dc, cnd), in1=x[:, dc, :],
                                                                          op0=ALU.mult, op1=ALU.add), reads=[rpo, rx, r_modS[stage]], writes=[rx])

        def proj_fm(w_d, q, ncols, epi):
            wv, rw = load_w(w_d[q, :, :, :], 128, 16, ncols)
            pt, rp = next_pd()
            S.pe_group([(lambda e, k=k, wv=wv, pt=pt: e.matmul(pt[0:ncols, :], lhsT=wv[:, k, :], rhs=h_t[:, k, :], start=(k == 0), stop=(k == 15))) for k in range(16)],
                       reads=[rw, r_h], writes=[rp])
            epi(pt, rp)

        return SimpleNamespace(**locals())

    def rwkv_phase():
        KS = 4
        TP = TG + 2 * PADZ
        NCH = TG // C
        RW = F32
        lin_big = sb("lin_big", [128, 3, TP]); r_lin = Res("lin")
        lsh = sb("lsh", [128, 3, TG]); r_lsh = Res("lsh")
        nbl = sb("nbl", [128, TG]); r_nbl = Res("nbl")
        xaf = sb("xaf", [128, TG], BF16); r_xaf = Res("xaf")
        txwL = [sb("txw", [64, TG], BF16) for _ in range(2)]; xatL = [sb("xat", [64, TG], BF16) for _ in range(2)]; r_txL = [Res("tx0"), Res("tx1")]
        sxgL = [sb("sxg", [128, 2, TG], BF16) for _ in range(2)]; r_sxgL = [Res("sxg0"), Res("sxg1")]
        w2s = sb("w2s", [64, 2, DA], BF16); a2s = sb("a2s", [64, 2, DA], BF16); g2as = sb("g2as", [128, DA], BF16); g2bs = sb("g2bs", [32, DA], BF16); r_lw = Res("lw")
        S.dma("pool", lambda e: e.dma_start(out=w2s[:], in_=w2T[:, :, :]), writes=[r_lw])
        S.dma("pool", lambda e: e.dma_start(out=a2s[:], in_=a2T[:, :, :]), writes=[r_lw])
        S.dma("pool", lambda e: e.dma_start(out=g2as[:], in_=g2a[:, :]), writes=[r_lw])
        S.dma("pool", lambda e: e.dma_start(out=g2bs[:], in_=g2b[:, :]), writes=[r_lw])
        Hs = sb("Hs", [64, NH, 64]); r_H = [Res(f"H{h}") for h in range(NH)]
        r_prep = [[[Res() for i in range(4)] for h in range(NH)] for g in range(NG)]
        r_yf = [[[Res() for i in range(2)] for h in range(NH)] for g in range(NG)]
        P_MUR, P_KK, P_KA, P_1MKA, P_RK, P_GNG, P_GNB = 0, 3, 4, 5, 6, 7, 8

        def pcol(idx, h):
            return p64s[:, idx, h:h + 1]

        def R(ap):
            return ap

        def RA(ap):
            return ap.bitcast(F32R)

        banks = [(pr[0], pr[1]), (pr[2], pr[3]), (pd[0], pd[1]), (pd[2], pd[3])]
        msk4 = sb("msk4", [128, 2, 4, 128]); msk2 = sb("msk2", [128, 2, 2, 128])
        for dd in range(2):
            for a_ in range(4):
                S.op("dve", lambda e, dd=dd, a_=a_: e.tensor_copy(out=msk4[:, dd, a_, :], in_=msk[:, dd, a_ % 2, :]), reads=[r_const], writes=[r_const])
            for a_ in range(2):
                S.op("dve", lambda e, dd=dd, a_=a_: e.tensor_copy(out=msk2[:, dd, a_, :], in_=msk[:, dd, 2, :]), reads=[r_const], writes=[r_const])

        def shift_tile(src, dst, is_grid, n_part, mu_ap, nb, rsrc, rdst, rnb):
            c0 = PADZ
            n = TG
            if is_grid:
                yield S.op("dve", lambda e: e.tensor_tensor(out=nb[0:n_part, 0:n], in0=src[0:n_part, c0 - 64:c0 - 64 + n], in1=src[0:n_part, c0 + 64:c0 + 64 + n], op=ALU.add),
                           reads=[rsrc], writes=[rnb])
                nb3 = nb[0:n_part, 0:n].rearrange("p (r c) -> p r c", c=64)
                s3 = src[0:n_part, c0:c0 + n].rearrange("p (r c) -> p r c", c=64)
                yield S.op("dve", lambda e: e.tensor_tensor(out=nb3[:, :, 1:64], in0=nb3[:, :, 1:64], in1=s3[:, :, 0:63], op=ALU.add), reads=[rsrc, rnb], writes=[rnb])
                yield S.op("dve", lambda e: e.tensor_tensor(out=nb3[:, :, 0:63], in0=nb3[:, :, 0:63], in1=s3[:, :, 1:64], op=ALU.add), reads=[rsrc, rnb], writes=[rnb])
                sc = 0.25
            else:
                yield S.op("dve", lambda e: e.tensor_tensor(out=nb[0:n_part, 0:n], in0=src[0:n_part, c0 - 1:c0 - 1 + n], in1=src[0:n_part, c0 + 1:c0 + 1 + n], op=ALU.add),
                           reads=[rsrc], writes=[rnb])
                sc = 0.5
            yield S.op("dve", lambda e: e.scalar_tensor_tensor(out=nb[0:n_part, 0:n], in0=nb[0:n_part, 0:n], scalar=sc, in1=src[0:n_part, c0:c0 + n], op0=ALU.mult, op1=ALU.subtract),
                       reads=[rsrc, rnb], writes=[rnb])
            yield S.op("dve", lambda e: e.scalar_tensor_tensor(out=dst[0:n_part, 0:n], in0=nb[0:n_part, 0:n], scalar=mu_ap, in1=src[0:n_part, c0:c0 + n], op0=ALU.mult, op1=ALU.add),
                       reads=[rsrc, rnb, r_const], writes=[rdst])

        def make_stream(sid):
            bk0, bk1 = banks[sid]
            rb0, rb1 = Res(f"bk0_{sid}", excl=True), Res(f"bk1_{sid}", excl=True)
            zin_big = [sb(f"zinb{i}", [64, TP]) for i in range(3)]; r_zin = [Res() for i in range(3)]
            zsh = [sb(f"zsh{i}", [64, TG]) for i in range(3)]; r_zsh = [Res() for i in range(3)]
            kkt = sb("kkt", [64, TG]); r_kk = Res()
            wdt = sb("wdt", [64, TG]); r_wd = Res()
            alr = sb("alr", [64, TG]); r_alr = Res()
            kdt = sb("kdt", [64, TG]); r_kd = Res()
            pin = sb("pin", [64, TG]); pex = sb("pex", [64, TG]); r_p = Res()
            sinc = sb("sinc", [64, TG]); sexc = sb("sexc", [64, TG]); rinc = sb("rinc", [64, TG]); r_s = r_p
            tmp64 = sb("tmp64", [64, TG]); r_tmp = Res()
            nbt, r_nbt = tmp64, r_tmp
            arT = sb("arT", [64, NCH, 2, C]); r_ar = Res()
            btT = sb("btT", [64, TG]); ktT = sb("ktT", [64, TG]); r_bk = Res()
            yacc = sb("yacc", [64, TG]); r_yacc = Res()
            oA = sb("oA", [64, TG], BF16); r_oA = Res()
            rkd = sb("rkd", [64, TG]); r_rkd = Res()
            yf_in, rk_in, r_yfin = sinc, sexc, r_s
            gn1, gn2, r_gn = pex, rinc, r_s
            ALLA = sb("ALLA", [128, NCH, 2, 128], RW); r_ALLA = Res()
            ALLB = sb("ALLB", [128, NCH, 2, 128], RW); r_ALLB = Res()
            LL = [sb(f"LL{i}", [128, 2, NCH, 128], RW) for i in range(2)]; r_LL = [Res() for i in range(2)]
            Xb = [sb(f"X{i}", [128, NCH, 128], RW) for i in range(2)]; r_X = [Res() for i in range(2)]
            Xfin = sb("Xfin", [128, NCH, 128], RW); r_Xfin = Res()
            tokm = sb("tokm", [128, NCH, 4, 64], RW); r_tokm = Res()
            MN = sb("MN", [64, NCH, 2, 64]); r_MN = Res()
            QT = sb("QT", [64, NCH, 128]); r_QT = Res()
            Ht = sb("Ht", [64, 64]); r_Ht = Res()
            arv = arT[:].rearrange("p c two t -> p c (two t)")
            rr, kr, vr = zsh

            def head_final(g, h, lb):
                sxg, r_sxg = sxgL[lb], r_sxgL[lb]
                yield S.dma("sp", lambda e: e.dma_start(out=yf_in[:], in_=YF[g, h, :, :]), reads=[r_yf[g][h][0]], writes=[r_yfin])
                yield S.dma("sp", lambda e: e.dma_start(out=rk_in[:], in_=RKF[g, h, :, :]), reads=[r_yf[g][h][1]], writes=[r_yfin])
                yield S.op("dve", lambda e: e.tensor_tensor(out=yacc[:], in0=yacc[:], in1=yf_in[:], op=ALU.add), reads=[r_yfin, r_yacc], writes=[r_yacc])
                yield S.op("dve", lambda e: e.tensor_tensor(out=rkd[:], in0=rkd[:], in1=rk_in[:], op=ALU.add), reads=[r_yfin, r_rkd], writes=[r_rkd])
                p1, p2 = bk0[0:64, 0:TG], bk0[0:64, TG:2 * TG]
                yield S.pe_group([lambda e: e.matmul(p1, lhsT=ones[0:64, 0:64], rhs=yacc[:], start=True, stop=True)], reads=[r_yacc, r_const], writes=[rb0])
                yield S.op("act", lambda e: e.activation(out=gn1[:], in_=yacc[:], func=AF.Square), reads=[r_yacc], writes=[r_gn])
                yield S.pe_group([lambda e: e.matmul(p2, lhsT=ones[0:64, 0:64], rhs=gn1[:], start=True, stop=True)], reads=[r_gn, r_const], writes=[rb0])
                yield S.op("dve", lambda e: e.tensor_scalar(out=gn1[:], in0=p1, scalar1=1.0 / 64, scalar2=None, op0=ALU.mult), reads=[rb0], writes=[r_gn])
                yield S.op("dve", lambda e: e.tensor_tensor(out=gn2[:], in0=gn1[:], in1=gn1[:], op=ALU.mult), reads=[r_gn], writes=[r_gn])
                yield S.op("dve", lambda e: e.scalar_tensor_tensor(out=gn2[:], in0=p2, scalar=1.0 / 64, in1=gn2[:], op0=ALU.mult, op1=ALU.subtract), reads=[rb0, r_gn], writes=[r_gn])
                yield S.op("dve", lambda e: e.tensor_scalar(out=gn2[:], in0=gn2[:], scalar1=GN_EPS, scalar2=None, op0=ALU.add), reads=[r_gn], writes=[r_gn])
                yield S.op("act", lambda e: e.activation(out=gn2[:], in_=gn2[:], func=AF.Sqrt), reads=[r_gn], writes=[r_gn])
                yield S.op("dve", lambda e: e.reciprocal(out=gn2[:], in_=gn2[:]), reads=[r_gn], writes=[r_gn])
                yield S.op("dve", lambda e: e.tensor_tensor(out=yacc[:], in0=yacc[:], in1=gn1[:], op=ALU.subtract), reads=[r_gn, r_yacc], writes=[r_yacc])
                yield S.op("dve", lambda e: e.tensor_tensor(out=yacc[:], in0=yacc[:], in1=gn2[:], op=ALU.mult), reads=[r_gn, r_yacc], writes=[r_yacc])
                yield S.op("act", lambda e: e.activation(out=yacc[:], in_=yacc[:], func=AF.Identity, scale=pcol(P_GNG, h), bias=pcol(P_GNB, h)), reads=[r_yacc, r_const], writes=[r_yacc])
                p3, p4 = bk1[0:64, 0:TG], bk1[0:64, TG:2 * TG]
                yield S.pe_group([lambda e: e.matmul(p3, lhsT=ones[0:64, 0:64], rhs=rkd[:], start=True, stop=True)], reads=[r_rkd, r_const], writes=[rb1])
                yield S.op("dve", lambda e: e.tensor_tensor(out=gn1[:], in0=p3, in1=vr[:], op=ALU.mult), reads=[rb1, r_zsh[2]], writes=[r_gn])
                yield S.op("dve", lambda e: e.tensor_tensor(out=yacc[:], in0=yacc[:], in1=gn1[:], op=ALU.add), reads=[r_gn, r_yacc], writes=[r_yacc])
                yield S.pe_group([lambda e: e.matmul(p4, lhsT=g2as[:, h * 64:(h + 1) * 64], rhs=sxg[:, 0, :], start=True, stop=False),
                                  lambda e: e.matmul(p4, lhsT=g2bs[:, h * 64:(h + 1) * 64], rhs=sxg[0:32, 1, :], start=False, stop=True)],
                                 reads=[r_sxg, r_lw], writes=[rb1])
                yield S.op("dve", lambda e: e.tensor_tensor(out=oA[:], in0=yacc[:], in1=p4, op=ALU.mult), reads=[rb1, r_yacc], writes=[r_oA])
                yield S.dma("sp", lambda e: e.dma_start(out=OA[g // 2, :, h, (g % 2) * TG:(g % 2 + 1) * TG], in_=oA[:]), reads=[r_oA])

            def head_body(g, d, final, h, lb):
                is_grid = g >= NPG
                sq, st = g_seg(g)
                chunks = list(range(NCH))
                txw, xat, r_tx = txwL[lb], xatL[lb], r_txL[lb]
                if g < NPG:
                    yield S.op("dve", lambda e: e.memset(Hs[:, h, :], 0.0), writes=[r_H[h]])
                elif (d == 0 and g == NPG) or (d == 1 and g == NG - 1):
                    yield S.dma("sp", lambda e: e.dma_start(out=Hs[:, h, :], in_=h0in[d, h, :, :]), writes=[r_H[h]])
                pa_, pb_, pc_ = bk0[0:64, 0:TG], bk0[0:64, TG:2 * TG], bk1[0:64, 0:TG]
                if d == 0:
                    for i in range(3):
                        yield S.dma("sp", lambda e, i=i: e.dma_start(out=zin_big[i][:], in_=ZR[sq][i * NH + h, :, st:st + TP]), reads=[r_zpad], writes=[r_zin[i]])
                    for i in range(3):
                        yield from shift_tile(zin_big[i], zsh[i], is_grid, 64, pcol(P_MUR + i, h), nbt, r_zin[i], r_zsh[i], r_nbt)
                        yield S.dma("sp", lambda e, i=i: e.dma_start(out=PREP[g, h, i, :, :], in_=zsh[i][:]), reads=[r_zsh[i]], writes=[r_prep[g][h][i]])
                    yield S.op("act", lambda e: e.activation(out=kkt[:], in_=kr[:], func=AF.Identity, scale=pcol(P_KK, h)), reads=[r_zsh[1], r_const], writes=[r_kk])
                    yield S.op("act", lambda e: e.activation(out=tmp64[:], in_=kkt[:], func=AF.Square), reads=[r_kk], writes=[r_tmp])
                    yield S.pe_group([lambda e: e.matmul(pa_, lhsT=ones[0:64, 0:64], rhs=tmp64[:], start=True, stop=True)], reads=[r_tmp, r_const], writes=[rb0])
                    yield S.op("act", lambda e: e.activation(out=tmp64[:], in_=pa_, func=AF.Sqrt), reads=[rb0], writes=[r_tmp])
                    yield S.op("dve", lambda e: e.tensor_scalar(out=tmp64[:], in0=tmp64[:], scalar1=1e-12, scalar2=None, op0=ALU.max), reads=[r_tmp], writes=[r_tmp])
                    yield S.op("dve", lambda e: e.reciprocal(out=tmp64[:], in_=tmp64[:]), reads=[r_tmp], writes=[r_tmp])
                    yield S.op("dve", lambda e: e.tensor_tensor(out=kkt[:], in0=kkt[:], in1=tmp64[:], op=ALU.mult), reads=[r_tmp, r_kk], writes=[r_kk])
                    yield S.dma("sp", lambda e: e.dma_start(out=PREP[g, h, 3, :, :], in_=kkt[:]), reads=[r_kk], writes=[r_prep[g][h][3]])
                else:
                    for i in range(3):
                        yield S.dma("sp", lambda e, i=i: e.dma_start(out=zsh[i][:], in_=PREP[g, h, i, :, :]), reads=[r_prep[g][h][i]], writes=[r_zsh[i]])
                    yield S.dma("sp", lambda e: e.dma_start(out=kkt[:], in_=PREP[g, h, 3, :, :]), reads=[r_prep[g][h][3]], writes=[r_kk])
                yield S.pe_group([lambda e: e.matmul(pb_, lhsT=w2s[:, d, h * 64:(h + 1) * 64], rhs=txw[:], start=True, stop=True)], reads=[r_tx, r_lw], writes=[rb0])
                yield S.op("act", lambda e: e.activation(out=wdt[:], in_=pb_, func=AF.Sigmoid, bias=pcol(9 + d, h)), reads=[rb0, r_const], writes=[r_wd])
                yield S.op("act", lambda e: e.activation(out=wdt[:], in_=wdt[:], func=AF.Exp, scale=-EXPC), reads=[r_wd], writes=[r_wd])
                yield S.pe_group([lambda e: e.matmul(pc_, lhsT=a2s[:, d, h * 64:(h + 1) * 64], rhs=xat[:], start=True, stop=True)], reads=[r_tx, r_lw], writes=[rb1])
                yield S.op("act", lambda e: e.activation(out=alr[:], in_=pc_, func=AF.Sigmoid, bias=a0s[:, d, h:h + 1]), reads=[rb1, r_const], writes=[r_alr])
                yield S.op("act", lambda e: e.activation(out=kdt[:], in_=alr[:], func=AF.Identity, scale=pcol(P_KA, h), bias=pcol(P_1MKA, h)),
                           reads=[r_alr, r_const], writes=[r_kd])
                yield S.op("dve", lambda e: e.tensor_tensor(out=kdt[:], in0=kdt[:], in1=kr[:], op=ALU.mult), reads=[r_kd, r_zsh[1]], writes=[r_kd])
                yield S.op("dve", lambda e: e.scalar_tensor_tensor(out=rkd[:], in0=kdt[:], scalar=pcol(P_RK, h), in1=rr[:], op0=ALU.mult, op1=ALU.mult),
                           reads=[r_kd, r_zsh[0], r_const], writes=[r_rkd])
                for c in chunks:
                    yield S.op("dve", lambda e, c=c: e.tensor_tensor_scan(out=pin[:, c * C:(c + 1) * C], data0=wdt[:, c * C:(c + 1) * C], data1=zeros[0:64, 0:C], initial=1.0,
                                                                          op0=ALU.mult, op1=ALU.add), reads=[r_wd, r_const], writes=[r_p])
                yield S.op("dve", lambda e: e.reciprocal(out=tmp64[:], in_=wdt[:]), reads=[r_wd], writes=[r_tmp])
                yield S.op("dve", lambda e: e.tensor_tensor(out=pex[:], in0=pin[:], in1=tmp64[:], op=ALU.mult), reads=[r_p, r_tmp], writes=[r_p])
                if d == 0:
                    sincv, sexcv = pin, pex
                else:
                    sincv, sexcv = sinc, sexc
                    yield S.op("dve", lambda e: e.reciprocal(out=sinc[:], in_=pex[:]), reads=[r_p], writes=[r_s])
                    yield S.op("dve", lambda e: e.reciprocal(out=sexc[:], in_=pin[:]), reads=[r_p], writes=[r_s])
                    for c in chunks:
                        yield S.op("act", lambda e, c=c: e.activation(out=sinc[:, c * C:(c + 1) * C], in_=sinc[:, c * C:(c + 1) * C], func=AF.Identity,
                                                                      scale=pin[:, (c + 1) * C - 1:(c + 1) * C]), reads=[r_p, r_s], writes=[r_s])
                        yield S.op("act", lambda e, c=c: e.activation(out=sexc[:, c * C:(c + 1) * C], in_=sexc[:, c * C:(c + 1) * C], func=AF.Identity,
                                                                      scale=pin[:, (c + 1) * C - 1:(c + 1) * C]), reads=[r_p, r_s], writes=[r_s])
                yield S.op("dve", lambda e: e.reciprocal(out=rinc[:], in_=sincv[:]), reads=[r_s], writes=[r_s])
                for c in chunks:
                    yield S.op("dve", lambda e, c=c: e.scalar_tensor_tensor(out=RA(arT[:, c, 0, :]), in0=kkt[:, c * C:(c + 1) * C], scalar=-1.0, in1=sexcv[:, c * C:(c + 1) * C],
                                                                            op0=ALU.mult, op1=ALU.mult), reads=[r_kk, r_s], writes=[r_ar])
                    yield S.op("dve", lambda e, c=c: e.tensor_tensor(out=RA(arT[:, c, 1, :]), in0=rr[:, c * C:(c + 1) * C], in1=sincv[:, c * C:(c + 1) * C], op=ALU.mult),
                               reads=[r_zsh[0], r_s], writes=[r_ar])
                yield S.op("dve", lambda e: e.tensor_tensor(out=tmp64[:], in0=kkt[:], in1=alr[:], op=ALU.mult), reads=[r_kk, r_alr], writes=[r_tmp])
                yield S.op("dve", lambda e: e.tensor_tensor(out=RA(btT[:]), in0=tmp64[:], in1=rinc[:], op=ALU.mult), reads=[r_s, r_tmp], writes=[r_bk])
                yield S.op("dve", lambda e: e.tensor_tensor(out=RA(ktT[:]), in0=kdt[:], in1=rinc[:], op=ALU.mult), reads=[r_s, r_kd], writes=[r_bk])
                corder = chunks if d == 0 else chunks[::-1]
                Hv = Hs[:, h, :]
                rH = r_H[h]
                m4 = msk4[:, d, :, :].rearrange("p a t -> p (a t)")
                fns = []
                for c in chunks:
                    cs = slice(c * C, (c + 1) * C)
                    for i, src in enumerate([arT[:, c, 0, :], btT[:, cs], ktT[:, cs], vr[:, cs]]):
                        fns.append(lambda e, c=c, i=i, src=src: e.transpose(out=bk1[:, c * 256 + i * 64:c * 256 + (i + 1) * 64], in_=src, identity=ident[0:64, 0:64]))
                yield S.pe_group(fns, reads=[r_ar, r_bk, r_zsh[2], r_const], writes=[rb1])
                yield S.op("act", lambda e: e.activation(out=RA(tokm[:].rearrange("p c a k -> p (c a k)")), in_=bk1[:, :], func=AF.Copy), reads=[rb1], writes=[r_tokm])
                yield S.pe_group([(lambda e, c=c: e.matmul(bk0[:, c * 256:(c + 1) * 256], lhsT=RA(btT[:, c * C:(c + 1) * C]), rhs=RA(arv[:, c, :]), start=True, stop=True)) for c in chunks],
                                 reads=[r_bk, r_ar], writes=[rb0])
                yield S.op("dve", lambda e: e.tensor_tensor(out=RA(ALLA[:].rearrange("p c a t -> p (c a t)")), in0=bk0[:, :], in1=m4, op=ALU.mult), reads=[rb0, r_const], writes=[r_ALLA])
                yield S.pe_group([(lambda e, c=c: e.matmul(bk1[:, c * 256:(c + 1) * 256], lhsT=RA(ktT[:, c * C:(c + 1) * C]), rhs=RA(arv[:, c, :]), start=True, stop=True)) for c in chunks],
                                 reads=[r_bk, r_ar], writes=[rb1])
                yield S.op("dve", lambda e: e.tensor_tensor(out=RA(ALLB[:].rearrange("p c a t -> p (c a t)")), in0=bk1[:, :], in1=m4, op=ALU.mult), reads=[rb1, r_const], writes=[r_ALLB])
                yield S.pe_group([(lambda e, c=c: e.matmul(bk0[:, c * 128:(c + 1) * 128], lhsT=RA(arT[:, c, 0, :]), rhs=RA(btT[:, c * C:(c + 1) * C]), start=True, stop=True)) for c in chunks],
                                 reads=[r_bk, r_ar], writes=[rb0])
                yield S.op("dve", lambda e: e.tensor_tensor(out=R(LL[0][:, 1, :, :].rearrange("p c t -> p (c t)")), in0=bk0[:, 0:256], in1=msk2[:, d, :, :].rearrange("p a t -> p (a t)"), op=ALU.mult),
                           reads=[rb0, r_const], writes=[r_LL[0]])
                yield S.pe_group([(lambda e, c=c: e.matmul(bk1[:, c * 64:(c + 1) * 64], lhsT=RA(ALLB[:, c, 0, :]), rhs=RA(tokm[:, c, 3, :]), start=True, stop=True)) for c in chunks],
                                 reads=[r_ALLB, r_tokm], writes=[rb1])
                yield S.op("act", lambda e: e.activation(out=R(Xb[0][:, :, 64:128]), in_=bk1[:, 0:128].rearrange("p (c k) -> p c k", k=64), func=AF.Copy), reads=[rb1], writes=[r_X[0]])
                yield S.op("act", lambda e: e.activation(out=R(Xb[0][:, :, 0:64]), in_=tokm[:, :, 0, :], func=AF.Copy), reads=[r_tokm], writes=[r_X[0]])
                for j in range(7):
                    i0, i1 = j % 2, (j + 1) % 2

                    def LtA(c, j=j, i0=i0):
                        return ALLA[:, c, 0, :] if j == 0 else LL[i0][:, 0, c, :]

                    def LA(c, i0=i0):
                        return LL[i0][:, 1, c, :]
                    rLt = [r_ALLA, r_LL[0]] if j == 0 else [r_LL[i0]]
                    yield S.pe_group([(lambda e, c=c, LtA=LtA, i0=i0: e.matmul(bk0[:, c * 128:(c + 1) * 128], lhsT=R(LtA(c)), rhs=R(Xb[i0][:, c, :]), start=True, stop=True)) for c in chunks],
                                     reads=rLt + [r_X[i0]], writes=[rb0])
                    if j < 6:
                        fns = []
                        for c in chunks:
                            fns.append(lambda e, c=c, LtA=LtA, LA=LA: e.matmul(bk1[:, c * 128:(c + 1) * 128], lhsT=R(LA(c)), rhs=R(LtA(c)), start=True, stop=True))
                        for c in chunks:
                            fns.append(lambda e, c=c, LtA=LtA, LA=LA: e.matmul(bk1[:, 256 + c * 128:256 + (c + 1) * 128], lhsT=R(LtA(c)), rhs=R(LA(c)), start=True, stop=True))
                        yield S.pe_group(fns, reads=rLt + [r_LL[i0]], writes=[rb1])
                    if j < 6:
                        yield S.op("dve", lambda e, i0=i0, i1=i1: e.tensor_tensor(out=R(Xb[i1][:].rearrange("p c t -> p (c t)")), in0=bk0[:, 0:256], in1=Xb[i0][:].rearrange("p c t -> p (c t)"), op=ALU.add),
                                   reads=[rb0, r_X[i0]], writes=[r_X[i1]])
                    else:
                        yield S.op("dve", lambda e, i0=i0: e.tensor_tensor(out=RA(Xfin[:].rearrange("p c t -> p (c t)")), in0=bk0[:, 0:256], in1=Xb[i0][:].rearrange("p c t -> p (c t)"), op=ALU.add),
                                   reads=[rb0, r_X[i0]], writes=[r_Xfin])
                    if j < 6:
                        yield S.op("act", lambda e, i1=i1: e.activation(out=R(LL[i1][:].rearrange("p a c t -> p (a c t)")), in_=bk1[:, :], func=AF.Copy), reads=[rb1], writes=[r_LL[i1]])
                Xf = Xfin; rXf = r_Xfin
                fns = []
                for c in chunks:
                    fns.append(lambda e, c=c: e.matmul(bk0[0:64, c * 128:c * 128 + 64], lhsT=RA(Xf[:, c, 0:64]), rhs=RA(tokm[:, c, 1, :]), start=True, stop=True))
                    fns.append(lambda e, c=c: e.matmul(bk0[0:64, c * 128 + 64:c * 128 + 128], lhsT=RA(tokm[:, c, 1, :]), rhs=RA(Xf[:, c, 64:128]), start=True, stop=False))
                    fns.append(lambda e, c=c: e.matmul(bk0[0:64, c * 128 + 64:c * 128 + 128], lhsT=RA(tokm[:, c, 2, :]), rhs=RA(tokm[:, c, 3, :]), start=False, stop=True))
                    fns.append(lambda e, c=c: e.matmul(bk0[0:64, 256 + c * 128:256 + (c + 1) * 128], lhsT=RA(Xf[:, c, 0:64]), rhs=RA(ALLA[:, c, 1, :]), start=True, stop=True))
                yield S.pe_group(fns, reads=[rXf, r_tokm, r_ALLA], writes=[rb0])
                yield S.op("act", lambda e: e.activation(out=MN[:].rearrange("p c a k -> p (c a k)"), in_=bk0[0:64, 0:256], func=AF.Copy), reads=[rb0], writes=[r_MN])
                yield S.op("dve", lambda e: e.tensor_tensor(out=QT[:], in0=bk0[0:64, 256:512].rearrange("p (c t) -> p c t", t=128), in1=arT[:, :, 1, :], op=ALU.add),
                           reads=[rb0, r_ar], writes=[r_QT])
                for c in corder:
                    cs = slice(c * C, (c + 1) * C)
                    yield S.pe_group([lambda e, c=c: e.matmul(bk1[0:64, c * 128:(c + 1) * 128], lhsT=RA(Xf[:, c, 64:128]), rhs=RA(ALLA[:, c, 1, :]), start=True, stop=False),
                                      lambda e, c=c: e.matmul(bk1[0:64, c * 128:(c + 1) * 128], lhsT=RA(tokm[:, c, 3, :]), rhs=RA(ALLB[:, c, 1, :]), start=False, stop=False),
                                      lambda e, c=c: e.matmul(bk1[0:64, c * 128:(c + 1) * 128], lhsT=Hv, rhs=QT[:, c, :], start=False, stop=True),
                                      lambda e, c=c: e.matmul(bk1[0:64, 256:320], lhsT=MN[:, c, 0, :], rhs=Hv, start=True, stop=True)],
                                     reads=[rXf, r_tokm, r_ALLA, r_ALLB, r_QT, rH, r_MN], writes=[rb1])
                    pc = pin[:, (c + 1) * C - 1:(c + 1) * C]
                    yield S.op("dve", lambda e, c=c: e.tensor_tensor(out=Ht[:], in0=bk1[0:64, 256:320], in1=MN[:, c, 1, :], op=ALU.add), reads=[rb1, r_MN], writes=[r_Ht])
                    yield S.op("dve", lambda e: e.tensor_tensor(out=Ht[:], in0=Ht[:], in1=Hv, op=ALU.add), reads=[r_Ht, rH], writes=[r_Ht])
                    yield S.op("act", lambda e, pc=pc: e.activation(out=Hv, in_=Ht[:], func=AF.Identity, scale=pc), reads=[r_Ht, r_p], writes=[rH])
                yield S.op("act", lambda e: e.activation(out=yacc[:], in_=bk1[0:64, 0:256], func=AF.Copy), reads=[rb1], writes=[r_yacc])
                if not final:
                    yield S.dma("sp", lambda e: e.dma_start(out=YF[g, h, :, :], in_=yacc[:]), reads=[r_yacc], writes=[r_yf[g][h][0]])
                    yield S.dma("sp", lambda e: e.dma_start(out=RKF[g, h, :, :], in_=rkd[:]), reads=[r_rkd], writes=[r_yf[g][h][1]])
                else:
                    yield from head_final(g, h, lb)
                if g < NPG:
                    yield S.dma("sp", lambda e: e.dma_start(out=st_out[d, g, h, :, :], in_=Hs[:, h, :]), reads=[r_H[h]], is_output=True)

            return head_body

        streams = [make_stream(sid) for sid in range(KS)]

        def run_streams(gens, stagger=24):
            gens = list(gens)
            active = []
            for i, gen in enumerate(gens):
                if i > 0:
                    for _ in range(stagger):
                        for g_ in list(active):
                            try:
                                next(g_)
                            except StopIteration:
                                active.remove(g_)
                active.append(gen)
            while active:
                for g_ in list(active):
                    try:
                        next(g_)
                    except StopIteration:
                        active.remove(g_)

        def drain(gen):
            for _ in gen:
                pass

        def lora_prep(g, lb):
            is_grid = g >= NPG
            sq, st = g_seg(g)
            txw, xat, r_tx, sxg, r_sxg = txwL[lb], xatL[lb], r_txL[lb], sxgL[lb], r_sxgL[lb]
            for q in range(3):
                yield S.dma("sp", lambda e, q=q: e.dma_start(out=lin_big[:, q, :], in_=ZLP[sq][q, :, st:st + TP]), reads=[r_zpad], writes=[r_lin])
            for q in range(3):
                yield from shift_tile(lin_big[:, q, :], lsh[:, q, :], is_grid, 128, mus[:, q:q + 1], nbl, r_lin, r_lsh, r_nbl)
            yield S.op("act", lambda e: e.activation(out=txw[:], in_=lsh[0:64, 0, :], func=AF.Tanh), reads=[r_lsh], writes=[r_tx])
            yield S.op("act", lambda e: e.activation(out=xaf[:], in_=lsh[:, 0, :], func=AF.Copy), reads=[r_lsh], writes=[r_xaf])
            yield S.dma("sp", lambda e: e.dma_start(out=xat[:], in_=xaf[64:128, :]), reads=[r_xaf], writes=[r_tx])
            yield S.op("act", lambda e: e.activation(out=sxg[:, 0, :], in_=lsh[:, 1, :], func=AF.Sigmoid), reads=[r_lsh], writes=[r_sxg])
            yield S.op("act", lambda e: e.activation(out=sxg[0:32, 1, :], in_=lsh[0:32, 2, :], func=AF.Sigmoid), reads=[r_lsh], writes=[r_sxg])

        def rwkv_pass(d, final, order):
            lora_done = [-1]
            fin_cnt = [0] * len(order)

            def groups_done():
                k = 0
                while k < len(order) and fin_cnt[k] == KS:
                    k += 1
                return k

            def aux_gen():
                for n, g in enumerate(order):
                    while groups_done() < n - 1:
                        yield None
                    yield from lora_prep(g, n % 2)
                    lora_done[0] = n

            def stream_gen(sid):
                for n, g in enumerate(order):
                    while lora_done[0] < n:
                        yield None
                    lb = n % 2
                    for h in range(sid, NH, KS):
                        yield from streams[sid](g, d, final, h, lb)
                    fin_cnt[n] += 1

            run_streams([aux_gen()] + [stream_gen(sid) for sid in range(KS)])

        rwkv_pass(0, False, list(range(NG)))
        rwkv_pass(1, True, list(range(NG - 1, -1, -1)))

    with ExitStack() as stk:
        cur[0] = stk
        EA = dense_env()
        scond = sb("scond", [128, 16, 2]); r_scond = Res("scond")
        scondb = sb("scondb", [128, 16, 2], BF16)
        badas = sb("badas", [128, 144])
        S.dma("sp", lambda e: e.dma_start(out=scond[:], in_=condT[:, :, :]), writes=[r_scond])
        S.dma("sp", lambda e: e.dma_start(out=badas[:], in_=b_adaT[:, :]), writes=[r_const])
        S.op("act", lambda e: e.activation(out=scondb[:], in_=scond[:], func=AF.Silu), reads=[r_scond], writes=[r_scond])
        def ada_stage(i):
            for j in range(48 * i, 48 * (i + 1)):
                wv, rw = EA.load_w(w_ada[j, :, :, :], 128, 16, 128)
                pt, rp = next_pd()
                S.pe_group([(lambda e, k=k, wv=wv, pt=pt: e.matmul(pt[:, 0:2], lhsT=wv[:, k, :], rhs=scondb[:, k, :], start=(k == 0), stop=(k == 15)))
                            for k in range(16)], reads=[rw, r_scond], writes=[rp])
                S.op("dve", lambda e, j=j, pt=pt: e.tensor_scalar(out=mod[:, j, :], in0=pt[:, 0:2], scalar1=badas[:, j:j + 1], scalar2=None, op0=ALU.add),
                     reads=[rp, r_const], writes=[r_modS[i]])
            S.op("dve", lambda e: e.tensor_copy(out=modd[:, 3 * i + 0, :, :], in_=mod[:, (3 * i) * 16:(3 * i + 1) * 16, :]), reads=[r_modS[i]], writes=[r_modS[i]])
            S.op("dve", lambda e: e.tensor_scalar(out=modd[:, 3 * i + 1, :, :], in0=mod[:, (3 * i + 1) * 16:(3 * i + 2) * 16, :], scalar1=1.0, scalar2=None, op0=ALU.add),
                 reads=[r_modS[i]], writes=[r_modS[i]])
            S.op("dve", lambda e: e.tensor_scalar(out=modd[:, 3 * i + 2, :, :], in0=mod[:, (3 * i + 2) * 16:(3 * i + 3) * 16, :], scalar1=(1.0 if i == 1 else 0.5), scalar2=None, op0=ALU.mult),
                 reads=[r_modS[i]], writes=[r_modS[i]])

        ada_stage(0)

        stg = sb("stg", [128, TD]); r_stg = Res("stg")
        stgb = sb("stgb", [128, 4, TD], BF16); r_stgb = Res("stgb")
        vtm = sb("vtm", [128, DB]); r_vtm = Res("vtm")
        vnb = sb("vnb", [128, DB], BF16); r_vnb = Res("vnb")
        bnst = sb("bnst", [128, 2, 6]); bnag = sb("bnag", [128, 2]); r_bn = Res("bn")
        sgl = sb("sgl", [128, 2, DB]); r_sgl = Res("sgl")
        S.dma("sp", lambda e: e.dma_start(out=sgl[:], in_=sgln[:, :, :]), writes=[r_sgl])

        def phase_a(sg):
            cnd = sg_cnd(sg)
            segs = sg_segs(sg)
            xa_t, h_t, r_xa, r_h = EA.xa_t, EA.h_t, EA.r_xa, EA.r_h
            S.dma("sp", lambda e: e.dma_start(out=xa_t[:], in_=xT[sg, :, :, :]), writes=[r_xa])
            EA.modulate(0, cnd)
            EA.ffn(w_f1, w_f1o, 0, cnd)
            if sg == 0:
                ada_stage(1)
                ada_stage(2)
            EA.layernorm(0)
            S.dma("sp", lambda e: e.dma_start(out=X1[sg, :, :, :], in_=xa_t[:]), reads=[r_xa])
            EA.modulate(1, cnd)
            for q in range(24):
                def epi(pt, rp, q=q):
                    S.op("act", lambda e: e.activation(out=stg[:, :], in_=pt[:, :], func=AF.Copy), reads=[rp], writes=[r_stg])
                    for (sq, st, n, co) in segs:
                        for half in range(2):
                            dst = ZR[sq][2 * q + half, :, PADZ + st:PADZ + st + n]
                            S.dma("sp", lambda e, dst=dst, co=co, n=n, half=half: e.dma_start(out=dst, in_=stg[half * 64:(half + 1) * 64, co:co + n]), reads=[r_stg], writes=[r_zpad])
                EA.proj_fm(w_rkv, q, 128, epi)
            for q in range(3):
                def epi(pt, rp, q=q):
                    S.op("dve", lambda e: e.tensor_copy(out=stg[:, :], in_=pt[:, :]), reads=[rp], writes=[r_stg])
                    for (sq, st, n, co) in segs:
                        dst = ZLP[sq][q, :, PADZ + st:PADZ + st + n]
                        S.dma("sp", lambda e, dst=dst, co=co, n=n: e.dma_start(out=dst, in_=stg[:, co:co + n]), reads=[r_stg], writes=[r_zpad])
                EA.proj_fm(w_lora, q, 128, epi)
            for q in range(8):
                def epi(pt, rp, q=q):
                    S.op("act", lambda e: e.activation(out=stgb[:, q % 4, :], in_=pt[:, :], func=AF.Gelu), reads=[rp], writes=[r_stgb])
                    if q % 4 == 3:
                        S.dma("sp", lambda e: e.dma_start(out=UTs[sg, :, q - 3:q + 1, :], in_=stgb[:]), reads=[r_stgb])
                EA.proj_fm(w_u, q, 128, epi)
            for q in range(32):
                def epi(pt, rp, q=q):
                    S.op("act", lambda e: e.activation(out=stgb[:, q % 4, :], in_=pt[:, :], func=AF.Sigmoid), reads=[rp], writes=[r_stgb])
                    if q % 4 == 3:
                        S.dma("sp", lambda e: e.dma_start(out=GTs[sg, :, q - 3:q + 1, :], in_=stgb[:]), reads=[r_stgb])
                EA.proj_fm(w_gate, q, 128, epi)
            for tc in range(TD // 128):
                for cb in range(4):
                    wv, rw = EA.load_w(w_v[cb, :, :, :], 128, 16, 256)
                    pt, rp = next_pd()
                    S.pe_group([(lambda e, k=k, wv=wv, pt=pt, tc=tc: e.matmul(pt[:, 0:256], lhsT=h_t[:, k, tc * 128:(tc + 1) * 128], rhs=wv[:, k, :], start=(k == 0), stop=(k == 15)))
                                for k in range(16)], reads=[rw, r_h], writes=[rp])
                    S.op("act", lambda e, pt=pt, cb=cb: e.activation(out=vtm[:, cb * 256:(cb + 1) * 256], in_=pt[:, 0:256], func=AF.Gelu), reads=[rp], writes=[r_vtm])
                for cb in range(2):
                    S.op("dve", lambda e, cb=cb: e.bn_stats(out=bnst[:, cb, :], in_=vtm[:, cb * 512:(cb + 1) * 512]), reads=[r_vtm], writes=[r_bn])
                S.op("dve", lambda e: e.bn_aggr(out=bnag[:], in_=bnst[:].rearrange("p a b -> p (a b)")), reads=[r_bn], writes=[r_bn])
                S.op("dve", lambda e: e.tensor_scalar(out=bnag[:, 1:2], in0=bnag[:, 1:2], scalar1=LN_EPS, scalar2=None, op0=ALU.add), reads=[r_bn], writes=[r_bn])
                S.op("act", lambda e: e.activation(out=bnag[:, 1:2], in_=bnag[:, 1:2], func=AF.Sqrt), reads=[r_bn], writes=[r_bn])
                S.op("dve", lambda e: e.reciprocal(out=bnag[:, 1:2], in_=bnag[:, 1:2]), reads=[r_bn], writes=[r_bn])
                S.op("dve", lambda e: e.tensor_scalar(out=vtm[:], in0=vtm[:], scalar1=bnag[:, 0:1], scalar2=bnag[:, 1:2], op0=ALU.subtract, op1=ALU.mult),
                     reads=[r_bn, r_vtm], writes=[r_vtm])
                S.op("dve", lambda e: e.tensor_tensor(out=vtm[:], in0=vtm[:], in1=sgl[:, 0, :], op=ALU.mult), reads=[r_vtm, r_sgl], writes=[r_vtm])
                S.op("dve", lambda e: e.tensor_tensor(out=vnb[:], in0=vtm[:], in1=sgl[:, 1, :], op=ALU.add), reads=[r_vtm, r_sgl], writes=[r_vnb])
                S.dma("sp", lambda e, tc=tc: e.dma_start(out=VNs[sg, tc, :, :], in_=vnb[:]), reads=[r_vnb])

        for sg in range(NSG):
            phase_a(sg)
        S.barrier()
        cur[0] = None

    with ExitStack() as stk:
        cur[0] = stk
        rwkv_phase()
        S.barrier()
        cur[0] = None

    with ExitStack() as stk:
        cur[0] = stk
        EC = dense_env()
        outA = EC.GBIG[0:64, 0:NH * TD].rearrange("p (h t) -> p h t", t=TD)
        mrg = EC.GBIG[:, NH * TD:NH * TD + 16 * TD].rearrange("p (k t) -> p k t", t=TD)
        ut = EC.GBIG[:, 32 * TD:40 * TD].rearrange("p (k t) -> p k t", t=TD)
        r_G = EC.r_G
        gA = sb("gA", [128, TD], BF16); gB = sb("gB", [128, TD], BF16); r_gAB = Res("gAB")
        vn_in = sb("vn_in", [128, TD // 128, DB], BF16); r_vnin = Res("vnin")
        yB = sb("yB", [128, 8, TD], BF16); r_yB = Res("yB")
        sgws = sb("sgws", [128, 8, 128], BF16); r_sgw = Res("sgw")
        sgbs = sb("sgbs", [128, 8 * 128])
        S.dma("pool", lambda e: e.dma_start(out=sgws[:], in_=sgwT[:, :, :]), writes=[r_sgw])
        S.dma("sp", lambda e: e.dma_start(out=sgbs[:], in_=sgb[:, :]), writes=[r_sgw])
        mtmp = sb("mtmp", [128, TD]); r_mtmp = Res("mtmp")

        def phase_c(sg):
            cnd = sg_cnd(sg)
            xa_t, r_xa, sa_t, r_sa = EC.xa_t, EC.r_xa, EC.sa_t, EC.r_sa
            S.dma("sp", lambda e: e.dma_start(out=outA[:], in_=OA[sg, :, :, :]), writes=[r_G])
            S.dma("sp", lambda e: e.dma_start(out=ut[:], in_=UTs[sg, :, :, :]), writes=[r_G])
            S.dma("sp", lambda e: e.dma_start(out=vn_in[:], in_=VNs[sg, :, :, :].rearrange("t p c -> p t c")), writes=[r_vnin])
            for tc in range(TD // 128):
                for gg in range(8):
                    pt, rp = next_pd()
                    S.pe_group([lambda e, pt=pt, tc=tc, gg=gg: e.matmul(pt[:, 0:128], lhsT=vn_in[:, tc, gg * 128:(gg + 1) * 128], rhs=sgws[:, gg, :], start=True, stop=True)],
                               reads=[r_vnin, r_sgw], writes=[rp])
                    S.op("dve", lambda e, pt=pt, gg=gg: e.tensor_tensor(out=mtmp[:, 0:128], in0=pt[:, 0:128], in1=sgbs[:, gg * 128:(gg + 1) * 128], op=ALU.add),
                         reads=[rp, r_sgw], writes=[r_mtmp])
                    S.op("dve", lambda e, tc=tc, gg=gg: e.tensor_tensor(out=yB[:, gg, tc * 128:(tc + 1) * 128], in0=mtmp[:, 0:128], in1=ut[:, gg, tc * 128:(tc + 1) * 128], op=ALU.mult),
                         reads=[r_mtmp, r_G], writes=[r_yB])
            for dc in range(16):
                wfull, rwa = EC.load_w(w_pa[dc, :, :, :], 64, 16, 128)
                wpa_v = wfull
                wpb_v, rwb = EC.load_w(w_pb[dc, :, :, :], 128, 8, 128)
                S.dma("sp", lambda e, dc=dc: e.dma_start(out=gA[:], in_=GTs[sg, :, dc, :]), writes=[r_gAB])
                S.dma("sp", lambda e, dc=dc: e.dma_start(out=gB[:], in_=GTs[sg, :, 16 + dc, :]), writes=[r_gAB])
                pa, rpa = next_pd()
                pb, rpb = next_pd()
                S.pe_group([(lambda e, k=k, pa=pa, wpa_v=wpa_v: e.matmul(pa[:, :], lhsT=wpa_v[:, k, :], rhs=outA[:, k, :], start=(k == 0), stop=(k == 15))) for k in range(16)],
                           reads=[rwa, r_G], writes=[rpa])
                S.op("dve", lambda e, pa=pa: e.tensor_tensor(out=mtmp[:], in0=pa[:, :], in1=gA[:], op=ALU.mult), reads=[rpa, r_gAB], writes=[r_mtmp])
                S.pe_group([(lambda e, k=k, pb=pb, wpb_v=wpb_v: e.matmul(pb[:, :], lhsT=wpb_v[:, k, :], rhs=yB[:, k, :], start=(k == 0), stop=(k == 7))) for k in range(8)],
                           reads=[rwb, r_yB], writes=[rpb])
                S.op("dve", lambda e, pb=pb: e.tensor_tensor(out=sa_t[:], in0=pb[:, :], in1=gB[:], op=ALU.mult), reads=[rpb, r_gAB], writes=[r_sa])
                S.op("dve", lambda e, dc=dc: e.tensor_tensor(out=mrg[:, dc, :], in0=mtmp[:], in1=sa_t[:], op=ALU.add), reads=[r_mtmp, r_sa], writes=[r_G])
            S.dma("sp", lambda e: e.dma_start(out=xa_t[:], in_=X1[sg, :, :, :]), writes=[r_xa])
            for dc in range(16):
                wo, rwo = EC.load_w(w_o[dc, :, :, :], 128, 16, 128)
                po, rpo = next_pd()
                S.pe_group([(lambda e, k=k, wo=wo, po=po: e.matmul(po[:, :], lhsT=wo[:, k, :], rhs=mrg[:, k, :], start=(k == 0), stop=(k == 15))) for k in range(16)],
                           reads=[rwo, r_G], writes=[rpo])
                S.op("act", lambda e, dc=dc: e.activation(out=xa_t[:, dc, :], in_=xa_t[:, dc, :], func=AF.Identity, scale=ALPHA), reads=[r_xa], writes=[r_xa])
                S.op("dve", lambda e, dc=dc, po=po: e.scalar_tensor_tensor(out=xa_t[:, dc, :], in0=po[:, :], scalar=mcol(1, 2, dc, cnd), in1=xa_t[:, dc, :], op0=ALU.mult, op1=ALU.add),
                     reads=[rpo, r_xa, r_modS[1]], writes=[r_xa])
            EC.layernorm(1)
            EC.modulate(2, cnd)
            EC.ffn(w_f2, w_f2o, 2, cnd)
            EC.layernorm(2)
            S.dma("sp", lambda e: e.dma_start(out=yT[sg, :, :, :], in_=xa_t[:]), reads=[r_xa], is_output=True)

        for sg in range(NSG):
            phase_c(sg)
        cur[0] = None

    S.emit()
    return nc


def _chunk_w(W, cols=128):
    K, N = W.shape
    return np.ascontiguousarray(W.reshape(K // 128, 128, N // cols, cols).transpose(2, 1, 0, 3))


def _fm(x):
    T = x.shape[0]
    return np.ascontiguousarray(x.T.reshape(16, 128, T).transpose(1, 0, 2))


_NC_CACHE = {}


def kernel(x_prompt, x_sample, c, state_rwkv_fwd, state_rwkv_bwd, c_ctx, w_ada, b_ada, ln_g, ln_b,
           ffn_w_in, ffn_w_out, w_in, shift_mu, rw_w0, rw_w2, rw_a0, rw_a2, rw_g2, rw_k_k, rw_k_a,
           rw_r_k, rw_gn_g, rw_gn_b, sg_ln_g, sg_ln_b, sg_w, sg_b, w_pa, w_pb, w_o):
    f = lambda a: np.asarray(a, dtype=np.float32)
    x_prompt, x_sample, c, c_ctx = f(x_prompt), f(x_sample), f(c), f(c_ctx)
    w_in0 = f(w_in)[0]
    shared = {}
    shared["w_ada"] = _chunk_w(f(w_ada)[0])
    shared["b_adaT"] = np.ascontiguousarray(f(b_ada)[0].reshape(144, 128).T)
    lg, lb = f(ln_g)[0], f(ln_b)[0]
    shared["lnT"] = np.ascontiguousarray(np.concatenate([lg, lb], 0).reshape(6, 16, 128).transpose(2, 0, 1))
    for nm, i in (("w_f1", 0), ("w_f2", 1)):
        W = f(ffn_w_in)[0, i]
        Wc = _chunk_w(W)
        inter = np.empty_like(Wc)
        inter[0::2] = Wc[:43]
        inter[1::2] = Wc[43:]
        shared[nm] = inter
        Wo = f(ffn_w_out)[0, i]
        shared[nm + "o"] = np.ascontiguousarray(Wo.reshape(43, 128, 16, 128).transpose(2, 1, 0, 3))
    shared["w_rkv"] = _chunk_w(w_in0[:, :3072], 128)
    wl = np.zeros((2048, 384), np.float32)
    wl[:, :288] = w_in0[:, 3072:3360]
    shared["w_lora"] = _chunk_w(wl)
    shared["w_u"] = _chunk_w(w_in0[:, 3360:4384])
    shared["w_v"] = _chunk_w(w_in0[:, 4384:5408], 256)
    shared["w_gate"] = _chunk_w(w_in0[:, 5408:])
    mu = f(shift_mu)[0]
    p64 = np.zeros((64, 12, NH), np.float32)
    hp = lambda v: np.ascontiguousarray(v.reshape(NH, 64).T)
    p64[:, 0] = hp(mu[0:1024]); p64[:, 1] = hp(mu[1024:2048]); p64[:, 2] = hp(mu[2048:3072])
    kk_, ka_ = f(rw_k_k)[0], f(rw_k_a)[0]
    p64[:, 3] = hp(kk_); p64[:, 4] = hp(ka_)
    p64[:, 5] = hp(np.float32(1.0) - ka_)
    p64[:, 6] = hp(f(rw_r_k)[0].reshape(-1)); p64[:, 7] = hp(f(rw_gn_g)[0]); p64[:, 8] = hp(f(rw_gn_b)[0])
    p64[:, 9] = hp(f(rw_w0)[0, 0]); p64[:, 10] = hp(f(rw_w0)[0, 1])
    shared["p64"] = p64
    a0 = f(rw_a0)[0]
    shared["a0in"] = np.ascontiguousarray(np.stack([hp(a0[0]), hp(a0[1])], 1))
    shared["w2T"] = np.ascontiguousarray(f(rw_w2)[0].transpose(1, 0, 2))
    shared["a2T"] = np.ascontiguousarray(f(rw_a2)[0].transpose(1, 0, 2))
    g2 = f(rw_g2)[0]
    shared["g2a"] = np.ascontiguousarray(g2[:128]); shared["g2b"] = np.ascontiguousarray(g2[128:160])
    mul = np.zeros((384,), np.float32); mul[:288] = mu[3072:3360]
    shared["mu_l"] = np.ascontiguousarray(mul.reshape(3, 128).T)
    shared["sgln"] = np.ascontiguousarray(np.broadcast_to(np.stack([f(sg_ln_g)[0], f(sg_ln_b)[0]], 0)[None], (128, 2, DB)))
    shared["sgwT"] = np.ascontiguousarray(f(sg_w)[0].transpose(2, 0, 1))
    shared["sgb"] = np.ascontiguousarray(np.broadcast_to(f(sg_b)[0].reshape(1, -1), (128, 1024)))
    wpa = f(w_pa)[0]
    shared["w_pa"] = np.ascontiguousarray(wpa.reshape(NH, 64, 16, 128).transpose(2, 1, 0, 3))
    shared["w_pb"] = _chunk_w(f(w_pb)[0])
    shared["w_o"] = _chunk_w(f(w_o)[0])
    idx = np.arange(128)
    m = np.zeros((128, 2, 3, 128), np.float32)
    m[:, 0, 0] = (idx[:, None] < idx[None, :]); m[:, 0, 1] = (idx[:, None] <= idx[None, :]); m[:, 0, 2] = (idx[:, None] > idx[None, :])
    m[:, 1, 0] = (idx[:, None] > idx[None, :]); m[:, 1, 1] = (idx[:, None] >= idx[None, :]); m[:, 1, 2] = (idx[:, None] < idx[None, :])
    shared["masks"] = m
    shared["ident_in"] = np.eye(128, dtype=np.float32)

    sf, sbw = f(state_rwkv_fwd), f(state_rwkv_bwd)
    in_maps = []
    for core in range(8):
        b = core % 2
        xs = []
        for gi in range(2):
            xs.append(_fm(x_prompt[4 * core + 2 * gi: 4 * core + 2 * gi + 2].reshape(TD, D)))
        for gi in range(4):
            xs.append(_fm(x_sample[b, gi * TD:(gi + 1) * TD]))
        mp = dict(shared)
        mp["xT"] = np.ascontiguousarray(np.stack(xs, 0))
        cond = np.stack([c_ctx, c[b]], 1)
        mp["condT"] = np.ascontiguousarray(cond.reshape(16, 128, 2).transpose(1, 0, 2))
        h0 = np.stack([sf[b, 0].transpose(0, 2, 1), sbw[b, 0].transpose(0, 2, 1)], 0)
        mp["h0in"] = np.ascontiguousarray(h0)
        in_maps.append(mp)

    if "nc" not in _NC_CACHE:
        _NC_CACHE["nc"] = build_program()
    nc = _NC_CACHE["nc"]
    res = run_bass_kernel_spmd(nc, in_maps, core_ids=list(range(8)))

    def unfm(y):
        return y.transpose(1, 0, 2).reshape(D, -1).T

    y_prompt = np.zeros((32, 256, D), np.float32)
    y_sample = np.zeros((2, 2048, D), np.float32)
    nsf = np.zeros((32, 1, NH, HD, HD), np.float32)
    nsb = np.zeros((32, 1, NH, HD, HD), np.float32)
    for core in range(8):
        r = res.results[core]
        yt = np.asarray(r["yT"])
        for gi in range(2):
            y_prompt[4 * core + 2 * gi: 4 * core + 2 * gi + 2] = unfm(yt[gi]).reshape(2, 256, D)
        if core < 2:
            for gi in range(4):
                y_sample[core, gi * TD:(gi + 1) * TD] = unfm(yt[2 + gi])
        so = np.asarray(r["st_out"])
        nsf[4 * core:4 * core + 4, 0] = so[0].transpose(0, 1, 3, 2)
        nsb[4 * core:4 * core + 4, 0] = so[1].transpose(0, 1, 3, 2)
    return (y_prompt, y_sample, nsf, nsb)
```

```python
import math
from contextlib import ExitStack
from types import SimpleNamespace
import numpy as np
import concourse.bass as bass
import concourse.mybir as mybir
from concourse.bass_utils import run_bass_kernel_spmd

F32 = mybir.dt.float32
F32R = mybir.dt.float32r
BF16 = mybir.dt.bfloat16
AF = mybir.ActivationFunctionType
ALU = mybir.AluOpType

D = 2048; DFF = 5504; DA = 1024; HD = 64; NH = 16; DB = 1024; NSH = 3360; NIN = 9504
LW = 64; LA = 64; LG = 160
ALPHA = 2.0 ** 0.25
LN_EPS = 1e-5; GN_EPS = 64e-5
TG = 256
NG = 12
NPG = 4
TD = 512
NSG = 6
C = 128
PADZ = 64
EXPC = math.exp(-0.5)


class Res:
    __slots__ = ("name", "w", "readers", "excl")

    def __init__(self, name="", excl=False):
        self.name = name
        self.w = None
        self.readers = {}
        self.excl = excl


class Sched:
    ENGS = ("pe", "act", "dve", "pool", "sp")

    def __init__(self, nc, n_dma_sems=40):
        self.nc = nc
        self.prog = {e: [] for e in self.ENGS}
        self.sems = {e: nc.alloc_semaphore(f"s_{e}") for e in self.ENGS}
        self.count = {e: 0 for e in self.ENGS}
        self.known = {e: {} for e in self.ENGS}
        self.dma_sems = []
        for i in range(n_dma_sems):
            k = f"d{i}"
            self.sems[k] = nc.alloc_semaphore(f"s_{k}")
            self.dma_sems.append(k)
        self.dma_cnt = {k: 0 for k in self.dma_sems}
        self.dma_rr = 0
        self.out_tokens = []

    def _deps(self, eng, reads, writes):
        deps = {}
        ex = [r for r in reads if r.excl]
        if ex:
            writes = list(writes) + ex

        def add(tok):
            if tok is not None and deps.get(tok[0], 0) < tok[1]:
                deps[tok[0]] = tok[1]

        for r in reads:
            add(r.w)
        for w in writes:
            add(w.w)
            for k, v in w.readers.items():
                add((k, v))
        waits = []
        kn = self.known[eng]
        for k, v in deps.items():
            if k == eng and eng == "pe":
                continue
            if kn.get(k, 0) < v:
                kn[k] = v
                waits.append((k, v))
        return waits

    def _mark(self, tok, reads, writes):
        k, v = tok
        ex = [r for r in reads if r.excl]
        if ex:
            writes = list(writes) + ex
        for r in reads:
            if r.readers.get(k, 0) < v:
                r.readers[k] = v
        for w in writes:
            w.w = tok
            w.readers = {}

    def op(self, eng, fn, reads=(), writes=()):
        waits = self._deps(eng, reads, writes)
        self.count[eng] += 1
        tok = (eng, self.count[eng])
        self.prog[eng].append((waits, fn, (eng, 1)))
        self._mark(tok, reads, writes)
        return tok

    def pe_group(self, fns, reads=(), writes=()):
        waits = self._deps("pe", reads, writes)
        self.count["pe"] += 1
        tok = ("pe", self.count["pe"])
        n = len(fns)
        for i, fn in enumerate(fns):
            self.prog["pe"].append((waits if i == 0 else [], fn, ("pe", 1) if i == n - 1 else None))
        self._mark(tok, reads, writes)
        return tok

    def dma(self, eng, fn, reads=(), writes=(), is_output=False):
        k = self.dma_sems[self.dma_rr % len(self.dma_sems)]
        self.dma_rr += 1
        waits = self._deps(eng, reads, writes)
        prev = 16 * self.dma_cnt[k]
        kn = self.known[eng]
        if prev > 0 and kn.get(k, 0) < prev:
            kn[k] = prev
            waits.append((k, prev))
        self.dma_cnt[k] += 1
        tok = (k, 16 * self.dma_cnt[k])
        self.prog[eng].append((waits, fn, (k, 16)))
        self._mark(tok, reads, writes)
        if is_output:
            self.out_tokens.append(tok)
        return tok

    def barrier(self):
        tg = {e: self.count[e] for e in self.ENGS if self.count[e] > 0}
        tg.update({k: 16 * c for k, c in self.dma_cnt.items() if c > 0})
        for e in self.ENGS:
            waits = []
            kn = self.known[e]
            for k, v in tg.items():
                if k == e:
                    continue
                if kn.get(k, 0) < v:
                    kn[k] = v
                    waits.append((k, v))
            if waits:
                self.prog[e].append((waits, None, None))

    def emit(self):
        nc = self.nc
        fin = {}
        for k, v in self.out_tokens:
            fin[k] = max(fin.get(k, 0), v)
        final_waits = list(fin.items())
        names = {"pe": "tensor", "act": "scalar", "dve": "vector", "pool": "gpsimd", "sp": "sync"}
        with nc.Block() as block:
            for e in self.ENGS:
                prog = self.prog[e]
                if not prog and not (e == "sp" and final_waits):
                    continue

                def body(engine, prog=prog, e=e):
                    for waits, fn, inc in prog:
                        for k, v in waits:
                            engine.wait_ge(self.sems[k], v)
                        if fn is None:
                            continue
                        ins = fn(engine)
                        if inc is not None:
                            ins.then_inc(self.sems[inc[0]], inc[1])
                    if e == "sp":
                        for k, v in final_waits:
                            engine.wait_ge(self.sems[k], v)

                getattr(block, names[e])(body)


class Prog:
    pass


def build_program():
    nc = bass.Bass("TRN2", target_bir_lowering=False)
    S = Sched(nc)
    cur = [None]

    def din(name, shape, dt=F32):
        return nc.dram_tensor(name, list(shape), dt, kind="ExternalInput").ap()

    def dout(name, shape, dt=F32):
        return nc.dram_tensor(name, list(shape), dt, kind="ExternalOutput").ap()

    def dscr(name, shape, dt=F32):
        return nc.dram_tensor(name, list(shape), dt, kind="Internal").ap()

    uid = [0]

    def sb(name, shape, dt=F32):
        uid[0] += 1
        name = f"{name}_{uid[0]}"
        if cur[0] is not None:
            return cur[0].enter_context(nc.sbuf_tensor(name, list(shape), dt))
        return nc.alloc_sbuf_tensor(name, list(shape), dt)

    xT = din("xT", [NSG, 128, 16, TD])
    condT = din("condT", [128, 16, 2])
    h0in = din("h0in", [2, NH, HD, HD])
    w_ada = din("w_ada", [144, 128, 16, 128])
    b_adaT = din("b_adaT", [128, 144])
    lnT = din("lnT", [128, 6, 16])
    w_f1 = din("w_f1", [86, 128, 16, 128])
    w_f1o = din("w_f1o", [16, 128, 43, 128])
    w_f2 = din("w_f2", [86, 128, 16, 128])
    w_f2o = din("w_f2o", [16, 128, 43, 128])
    w_rkv = din("w_rkv", [24, 128, 16, 128])
    w_lora = din("w_lora", [3, 128, 16, 128])
    w_u = din("w_u", [8, 128, 16, 128])
    w_v = din("w_v", [4, 128, 16, 256])
    w_gate = din("w_gate", [32, 128, 16, 128])
    p64 = din("p64", [64, 12, NH])
    NP64 = 12
    w2T = din("w2T", [64, 2, DA])
    a2T = din("a2T", [64, 2, DA])
    g2a = din("g2a", [128, DA])
    g2b = din("g2b", [32, DA])
    mu_l = din("mu_l", [128, 3])
    sgln = din("sgln", [128, 2, DB])
    sgwT = din("sgwT", [128, 8, 128])
    sgb = din("sgb", [128, 8 * 128])
    w_pa = din("w_pa", [16, 64, 16, 128])
    w_pb = din("w_pb", [16, 128, 8, 128])
    w_o = din("w_o", [16, 128, 16, 128])
    masks = din("masks", [128, 2, 3, 128])
    ident_in = din("ident_in", [128, 128])

    yT = dout("yT", [NSG, 128, 16, TD])
    st_out = dout("st_out", [2, 4, NH, HD, HD])

    a0in = din("a0in", [64, 2, NH])
    X1 = dscr("X1", [NSG, 128, 16, TD])
    ZR = [dscr(f"ZR{s}", [48, 64, (256 if s < 4 else 2048) + 2 * PADZ]) for s in range(5)]
    ZLP = [dscr(f"ZLP{s}", [3, 128, (256 if s < 4 else 2048) + 2 * PADZ]) for s in range(5)]
    GTs = dscr("GTs", [NSG, 128, 32, TD], BF16)
    UTs = dscr("UTs", [NSG, 128, 8, TD], BF16)
    VNs = dscr("VNs", [NSG, 4, 128, DB], BF16)
    OA = dscr("OA", [NSG, 64, NH, TD], BF16)
    PREP = dscr("PREP", [NG, NH, 4, 64, TG])
    YF = dscr("YF", [NG, NH, 64, TG])
    RKF = dscr("RKF", [NG, NH, 64, TG])

    ident = sb("ident", [128, 128]); r_const = Res("const")
    ones = sb("ones", [128, 128])
    onesb = sb("onesb", [128, 128], BF16)
    zeros = sb("zeros", [128, 128])
    msk = sb("msk", [128, 2, 3, 128])
    p64s = sb("p64s", [64, NP64, NH])
    lns = sb("lns", [128, 6, 16])
    mus = sb("mus", [128, 3])
    S.dma("sp", lambda e: e.dma_start(out=ident[:], in_=ident_in[:, :]), writes=[r_const])
    S.dma("sp", lambda e: e.dma_start(out=msk[:], in_=masks[:, :, :, :]), writes=[r_const])
    S.dma("sp", lambda e: e.dma_start(out=p64s[:], in_=p64[:, :, :]), writes=[r_const])
    S.dma("sp", lambda e: e.dma_start(out=lns[:], in_=lnT[:, :, :]), writes=[r_const])
    S.dma("sp", lambda e: e.dma_start(out=mus[:], in_=mu_l[:, :]), writes=[r_const])
    S.op("dve", lambda e: e.memset(ones[:], 1.0), writes=[r_const])
    S.op("dve", lambda e: e.memset(onesb[:], 1.0), writes=[r_const])
    S.op("dve", lambda e: e.memset(zeros[:], 0.0), writes=[r_const])

    r_zpad = Res("zpad")
    for s in range(5):
        L = 256 if s < 4 else 2048
        for side in (0, 1):
            off = 0 if side == 0 else PADZ + L
            for q in range(48):
                S.dma("sp", lambda e, s=s, q=q, off=off: e.dma_start(out=ZR[s][q, :, off:off + PADZ], in_=zeros[0:64, 0:PADZ]),
                      reads=[r_const], writes=[r_zpad])
            for q in range(3):
                S.dma("sp", lambda e, s=s, q=q, off=off: e.dma_start(out=ZLP[s][q, :, off:off + PADZ], in_=zeros[:, 0:PADZ]),
                      reads=[r_const], writes=[r_zpad])

    pd = [nc.alloc_psum_tensor(f"pd{i}", [128, 512], F32) for i in range(4)]
    r_pd = [Res(f"pd{i}", excl=True) for i in range(4)]
    pr = [nc.alloc_psum_tensor(f"pr{i}", [128, 512], F32) for i in range(4)]
    pdi = [0]

    def next_pd():
        i = pdi[0] % 4
        pdi[0] += 1
        return pd[i], r_pd[i]

    mod = sb("mod", [128, 144, 2]); r_modS = [Res(f"mod{i}") for i in range(3)]
    modd = sb("modd", [128, 9, 16, 2])
    a0s = sb("a0s", [64, 2, NH])
    S.dma("sp", lambda e: e.dma_start(out=a0s[:], in_=a0in[:, :, :]), writes=[r_const])

    def mcol(i, which, kc, cnd):
        return modd[:, 3 * i + which, kc, cnd:cnd + 1]

    def sg_cnd(sg):
        return 0 if sg < 2 else 1

    def sg_segs(sg):
        if sg < 2:
            return [(2 * sg, 0, 256, 0), (2 * sg + 1, 0, 256, 256)]
        return [(4, (sg - 2) * TD, TD, 0)]

    def g_seg(g):
        if g < NPG:
            return (g, 0)
        return (4, (g - NPG) * TG)

    def dense_env():
        NSLOT = 4
        wslot = [sb(f"wslot{i}", [128, 43 * 128], BF16) for i in range(NSLOT)]
        r_wslot = [Res(f"wslot{i}") for i in range(NSLOT)]
        wsi = [0]
        xa_t = sb("xa_t", [128, 16, TD]); r_xa = Res("xa")
        h_t = sb("h_t", [128, 16, TD], BF16); r_h = Res("h")
        GBIG = sb("GBIG", [128, 43 * TD], BF16); r_G = Res("G")
        G_t = GBIG[:].rearrange("p (k t) -> p k t", t=TD)
        sq_t = [sb(f"sq_t{i}", [128, TD]) for i in range(2)]; r_sq = [Res(f"sq{i}") for i in range(2)]
        sa_t = sb("sa_t", [128, TD]); r_sa = Res("sa")
        st_mean = sb("st_mean", [128, TD]); st_rstd = sb("st_rstd", [128, TD]); st_tmp = sb("st_tmp", [128, TD]); r_st = Res("st")

        def load_w(src_ap, kp, kc, cols):
            i = wsi[0] % NSLOT
            wsi[0] += 1
            view = wslot[i][0:kp, 0:kc * cols].rearrange("p (k c) -> p k c", c=cols)
            S.dma("pool", lambda e: e.dma_start(out=view, in_=src_ap), writes=[r_wslot[i]])
            return view, r_wslot[i]

        def modulate(i, cnd):
            for kc in range(16):
                S.op("act", lambda e, kc=kc: e.activation(out=h_t[:, kc, :], in_=xa_t[:, kc, :], func=AF.Identity,
                                                          scale=mcol(i, 1, kc, cnd), bias=mcol(i, 0, kc, cnd)),
                     reads=[r_xa, r_modS[i]], writes=[r_h])

        def layernorm(li):
            x, rx = xa_t, r_xa
            p1, rp1 = next_pd()
            p2, rp2 = next_pd()
            S.pe_group([(lambda e, kc=kc: e.matmul(p1[:, :], lhsT=ones[:, :], rhs=x[:, kc, :], start=(kc == 0), stop=(kc == 15))) for kc in range(16)],
                       reads=[rx, r_const], writes=[rp1])
            for kc in range(16):
                i = kc % 2
                S.op("act", lambda e, kc=kc, i=i: e.activation(out=sq_t[i][:], in_=x[:, kc, :], func=AF.Square), reads=[rx], writes=[r_sq[i]])
                S.pe_group([lambda e, kc=kc, i=i: e.matmul(p2[:, :], lhsT=ones[:, :], rhs=sq_t[i][:], start=(kc == 0), stop=(kc == 15))],
                           reads=[r_sq[i], r_const], writes=[rp2])
            S.op("dve", lambda e: e.tensor_scalar(out=st_mean[:], in0=p1[:, :], scalar1=1.0 / D, scalar2=None, op0=ALU.mult), reads=[rp1], writes=[r_st])
            S.op("dve", lambda e: e.tensor_tensor(out=st_tmp[:], in0=st_mean[:], in1=st_mean[:], op=ALU.mult), reads=[r_st], writes=[r_st])
            S.op("dve", lambda e: e.scalar_tensor_tensor(out=st_tmp[:], in0=p2[:, :], scalar=1.0 / D, in1=st_tmp[:], op0=ALU.mult, op1=ALU.subtract),
                 reads=[rp2, r_st], writes=[r_st])
            S.op("dve", lambda e: e.tensor_scalar(out=st_tmp[:], in0=st_tmp[:], scalar1=LN_EPS, scalar2=None, op0=ALU.add), reads=[r_st], writes=[r_st])
            S.op("act", lambda e: e.activation(out=st_tmp[:], in_=st_tmp[:], func=AF.Sqrt), reads=[r_st], writes=[r_st])
            S.op("dve", lambda e: e.reciprocal(out=st_rstd[:], in_=st_tmp[:]), reads=[r_st], writes=[r_st])
            for kc in range(16):
                S.op("dve", lambda e, kc=kc: e.tensor_tensor(out=x[:, kc, :], in0=x[:, kc, :], in1=st_mean[:], op=ALU.subtract), reads=[rx, r_st], writes=[rx])
                S.op("dve", lambda e, kc=kc: e.tensor_tensor(out=x[:, kc, :], in0=x[:, kc, :], in1=st_rstd[:], op=ALU.mult), reads=[r_st, rx], writes=[rx])
                S.op("act", lambda e, kc=kc: e.activation(out=x[:, kc, :], in_=x[:, kc, :], func=AF.Identity,
                                                          scale=lns[:, li, kc:kc + 1], bias=lns[:, 3 + li, kc:kc + 1]), reads=[rx, r_const], writes=[rx])

        def ffn(w_in_d, w_out_d, stage, cnd):
            x, rx = xa_t, r_xa
            for j in range(43):
                wa, rwa = load_w(w_in_d[2 * j, :, :, :], 128, 16, 128)
                wb, rwb = load_w(w_in_d[2 * j + 1, :, :, :], 128, 16, 128)
                pa, rpa = next_pd()
                pb, rpb = next_pd()
                S.pe_group([(lambda e, k=k, wa=wa, pa=pa: e.matmul(pa[:, :], lhsT=wa[:, k, :], rhs=h_t[:, k, :], start=(k == 0), stop=(k == 15))) for k in range(16)],
                           reads=[rwa, r_h], writes=[rpa])
                S.pe_group([(lambda e, k=k, wb=wb, pb=pb: e.matmul(pb[:, :], lhsT=wb[:, k, :], rhs=h_t[:, k, :], start=(k == 0), stop=(k == 15))) for k in range(16)],
                           reads=[rwb, r_h], writes=[rpb])
                S.op("act", lambda e, pa=pa: e.activation(out=sa_t[:], in_=pa[:, :], func=AF.Silu), reads=[rpa], writes=[r_sa])
                S.op("dve", lambda e, pb=pb, j=j: e.tensor_tensor(out=G_t[:, j, :], in0=sa_t[:], in1=pb[:, :], op=ALU.mult), reads=[r_sa, rpb], writes=[r_G])
            for dc in range(16):
                wo, rwo = load_w(w_out_d[dc, :, :, :], 128, 43, 128)
                po, rpo = next_pd()
                S.pe_group([(lambda e, k=k, wo=wo, po=po: e.matmul(po[:, :], lhsT=wo[:, k, :], rhs=G_t[:, k, :], start=(k == 0), stop=(k == 42))) for k in range(43)],
                           reads=[rwo, r_G], writes=[rpo])
                S.op("act", lambda e, dc=dc: e.activation(out=x[:, dc, :], in_=x[:, dc, :], func=AF.Identity, scale=ALPHA), reads=[rx], writes=[rx])
                S.op("dve", lambda e, dc=dc, po=po: e.scalar_tensor_tensor(out=x[:, dc, :], in0=po[:, :], scalar=mcol(stage, 2, dc, cnd), in1=x[:, dc, :],
                                                                          op0=ALU.mult, op1=ALU.add), reads=[rpo, rx, r_modS[stage]], writes=[rx])

        def proj_fm(w_d, q, ncols, epi):
            wv, rw = load_w(w_d[q, :, :, :], 128, 16, ncols)
            pt, rp = next_pd()
            S.pe_group([(lambda e, k=k, wv=wv, pt=pt: e.matmul(pt[0:ncols, :], lhsT=wv[:, k, :], rhs=h_t[:, k, :], start=(k == 0), stop=(k == 15))) for k in range(16)],
                       reads=[rw, r_h], writes=[rp])
            epi(pt, rp)

        return SimpleNamespace(**locals())

    def rwkv_phase():
        KS = 4
        TP = TG + 2 * PADZ
        NCH = TG // C
        RW = F32
        lin_big = sb("lin_big", [128, 3, TP]); r_lin = Res("lin")
        lsh = sb("lsh", [128, 3, TG]); r_lsh = Res("lsh")
        nbl = sb("nbl", [128, TG]); r_nbl = Res("nbl")
        xaf = sb("xaf", [128, TG], BF16); r_xaf = Res("xaf")
        txwL = [sb("txw", [64, TG], BF16) for _ in range(2)]; xatL = [sb("xat", [64, TG], BF16) for _ in range(2)]; r_txL = [Res("tx0"), Res("tx1")]
        sxgL = [sb("sxg", [128, 2, TG], BF16) for _ in range(2)]; r_sxgL = [Res("sxg0"), Res("sxg1")]
        w2s = sb("w2s", [64, 2, DA], BF16); a2s = sb("a2s", [64, 2, DA], BF16); g2as = sb("g2as", [128, DA], BF16); g2bs = sb("g2bs", [32, DA], BF16); r_lw = Res("lw")
        S.dma("pool", lambda e: e.dma_start(out=w2s[:], in_=w2T[:, :, :]), writes=[r_lw])
        S.dma("pool", lambda e: e.dma_start(out=a2s[:], in_=a2T[:, :, :]), writes=[r_lw])
        S.dma("pool", lambda e: e.dma_start(out=g2as[:], in_=g2a[:, :]), writes=[r_lw])
        S.dma("pool", lambda e: e.dma_start(out=g2bs[:], in_=g2b[:, :]), writes=[r_lw])
        Hs = sb("Hs", [64, NH, 64]); r_H = [Res(f"H{h}") for h in range(NH)]
        r_prep = [[[Res() for i in range(4)] for h in range(NH)] for g in range(NG)]
        r_yf = [[[Res() for i in range(2)] for h in range(NH)] for g in range(NG)]
        P_MUR, P_KK, P_KA, P_1MKA, P_RK, P_GNG, P_GNB = 0, 3, 4, 5, 6, 7, 8

        def pcol(idx, h):
            return p64s[:, idx, h:h + 1]

        def R(ap):
            return ap

        def RA(ap):
            return ap.bitcast(F32R)

        banks = [(pr[0], pr[1]), (pr[2], pr[3]), (pd[0], pd[1]), (pd[2], pd[3])]
        msk4 = sb("msk4", [128, 2, 4, 128]); msk2 = sb("msk2", [128, 2, 2, 128])
        for dd in range(2):
            for a_ in range(4):
                S.op("dve", lambda e, dd=dd, a_=a_: e.tensor_copy(out=msk4[:, dd, a_, :], in_=msk[:, dd, a_ % 2, :]), reads=[r_const], writes=[r_const])
            for a_ in range(2):
                S.op("dve", lambda e, dd=dd, a_=a_: e.tensor_copy(out=msk2[:, dd, a_, :], in_=msk[:, dd, 2, :]), reads=[r_const], writes=[r_const])

        def shift_tile(src, dst, is_grid, n_part, mu_ap, nb, rsrc, rdst, rnb):
            c0 = PADZ
            n = TG
            if is_grid:
                yield S.op("dve", lambda e: e.tensor_tensor(out=nb[0:n_part, 0:n], in0=src[0:n_part, c0 - 64:c0 - 64 + n], in1=src[0:n_part, c0 + 64:c0 + 64 + n], op=ALU.add),
                           reads=[rsrc], writes=[rnb])
                nb3 = nb[0:n_part, 0:n].rearrange("p (r c) -> p r c", c=64)
                s3 = src[0:n_part, c0:c0 + n].rearrange("p (r c) -> p r c", c=64)
                yield S.op("dve", lambda e: e.tensor_tensor(out=nb3[:, :, 1:64], in0=nb3[:, :, 1:64], in1=s3[:, :, 0:63], op=ALU.add), reads=[rsrc, rnb], writes=[rnb])
                yield S.op("dve", lambda e: e.tensor_tensor(out=nb3[:, :, 0:63], in0=nb3[:, :, 0:63], in1=s3[:, :, 1:64], op=ALU.add), reads=[rsrc, rnb], writes=[rnb])
                sc = 0.25
            else:
                yield S.op("dve", lambda e: e.tensor_tensor(out=nb[0:n_part, 0:n], in0=src[0:n_part, c0 - 1:c0 - 1 + n], in1=src[0:n_part, c0 + 1:c0 + 1 + n], op=ALU.add),
                           reads=[rsrc], writes=[rnb])
                sc = 0.5
            yield S.op("dve", lambda e: e.scalar_tensor_tensor(out=nb[0:n_part, 0:n], in0=nb[0:n_part, 0:n], scalar=sc, in1=src[0:n_part, c0:c0 + n], op0=ALU.mult, op1=ALU.subtract),
                       reads=[rsrc, rnb], writes=[rnb])
            yield S.op("dve", lambda e: e.scalar_tensor_tensor(out=dst[0:n_part, 0:n], in0=nb[0:n_part, 0:n], scalar=mu_ap, in1=src[0:n_part, c0:c0 + n], op0=ALU.mult, op1=ALU.add),
                       reads=[rsrc, rnb, r_const], writes=[rdst])

        def make_stream(sid):
            bk0, bk1 = banks[sid]
            rb0, rb1 = Res(f"bk0_{sid}", excl=True), Res(f"bk1_{sid}", excl=True)
            zin_big = [sb(f"zinb{i}", [64, TP]) for i in range(3)]; r_zin = [Res() for i in range(3)]
            zsh = [sb(f"zsh{i}", [64, TG]) for i in range(3)]; r_zsh = [Res() for i in range(3)]
            kkt = sb("kkt", [64, TG]); r_kk = Res()
            wdt = sb("wdt", [64, TG]); r_wd = Res()
            alr = sb("alr", [64, TG]); r_alr = Res()
            kdt = sb("kdt", [64, TG]); r_kd = Res()
            pin = sb("pin", [64, TG]); pex = sb("pex", [64, TG]); r_p = Res()
            sinc = sb("sinc", [64, TG]); sexc = sb("sexc", [64, TG]); rinc = sb("rinc", [64, TG]); r_s = r_p
            tmp64 = sb("tmp64", [64, TG]); r_tmp = Res()
            nbt, r_nbt = tmp64, r_tmp
            arT = sb("arT", [64, NCH, 2, C]); r_ar = Res()
            btT = sb("btT", [64, TG]); ktT = sb("ktT", [64, TG]); r_bk = Res()
            yacc = sb("yacc", [64, TG]); r_yacc = Res()
            oA = sb("oA", [64, TG], BF16); r_oA = Res()
            rkd = sb("rkd", [64, TG]); r_rkd = Res()
            yf_in, rk_in, r_yfin = sinc, sexc, r_s
            gn1, gn2, r_gn = pex, rinc, r_s
            ALLA = sb("ALLA", [128, NCH, 2, 128], RW); r_ALLA = Res()
            ALLB = sb("ALLB", [128, NCH, 2, 128], RW); r_ALLB = Res()
            LL = [sb(f"LL{i}", [128, 2, NCH, 128], RW) for i in range(2)]; r_LL = [Res() for i in range(2)]
            Xb = [sb(f"X{i}", [128, NCH, 128], RW) for i in range(2)]; r_X = [Res() for i in range(2)]
            Xfin = sb("Xfin", [128, NCH, 128], RW); r_Xfin = Res()
            tokm = sb("tokm", [128, NCH, 4, 64], RW); r_tokm = Res()
            MN = sb("MN", [64, NCH, 2, 64]); r_MN = Res()
            QT = sb("QT", [64, NCH, 128]); r_QT = Res()
            Ht = sb("Ht", [64, 64]); r_Ht = Res()
            arv = arT[:].rearrange("p c two t -> p c (two t)")
            rr, kr, vr = zsh

            def head_final(g, h, lb):
                sxg, r_sxg = sxgL[lb], r_sxgL[lb]
                yield S.dma("sp", lambda e: e.dma_start(out=yf_in[:], in_=YF[g, h, :, :]), reads=[r_yf[g][h][0]], writes=[r_yfin])
                yield S.dma("sp", lambda e: e.dma_start(out=rk_in[:], in_=RKF[g, h, :, :]), reads=[r_yf[g][h][1]], writes=[r_yfin])
                yield S.op("dve", lambda e: e.tensor_tensor(out=yacc[:], in0=yacc[:], in1=yf_in[:], op=ALU.add), reads=[r_yfin, r_yacc], writes=[r_yacc])
                yield S.op("dve", lambda e: e.tensor_tensor(out=rkd[:], in0=rkd[:], in1=rk_in[:], op=ALU.add), reads=[r_yfin, r_rkd], writes=[r_rkd])
                p1, p2 = bk0[0:64, 0:TG], bk0[0:64, TG:2 * TG]
                yield S.pe_group([lambda e: e.matmul(p1, lhsT=ones[0:64, 0:64], rhs=yacc[:], start=True, stop=True)], reads=[r_yacc, r_const], writes=[rb0])
                yield S.op("act", lambda e: e.activation(out=gn1[:], in_=yacc[:], func=AF.Square), reads=[r_yacc], writes=[r_gn])
                yield S.pe_group([lambda e: e.matmul(p2, lhsT=ones[0:64, 0:64], rhs=gn1[:], start=True, stop=True)], reads=[r_gn, r_const], writes=[rb0])
                yield S.op("dve", lambda e: e.tensor_scalar(out=gn1[:], in0=p1, scalar1=1.0 / 64, scalar2=None, op0=ALU.mult), reads=[rb0], writes=[r_gn])
                yield S.op("dve", lambda e: e.tensor_tensor(out=gn2[:], in0=gn1[:], in1=gn1[:], op=ALU.mult), reads=[r_gn], writes=[r_gn])
                yield S.op("dve", lambda e: e.scalar_tensor_tensor(out=gn2[:], in0=p2, scalar=1.0 / 64, in1=gn2[:], op0=ALU.mult, op1=ALU.subtract), reads=[rb0, r_gn], writes=[r_gn])
                yield S.op("dve", lambda e: e.tensor_scalar(out=gn2[:], in0=gn2[:], scalar1=GN_EPS, scalar2=None, op0=ALU.add), reads=[r_gn], writes=[r_gn])
                yield S.op("act", lambda e: e.activation(out=gn2[:], in_=gn2[:], func=AF.Sqrt), reads=[r_gn], writes=[r_gn])
                yield S.op("dve", lambda e: e.reciprocal(out=gn2[:], in_=gn2[:]), reads=[r_gn], writes=[r_gn])
                yield S.op("dve", lambda e: e.tensor_tensor(out=yacc[:], in0=yacc[:], in1=gn1[:], op=ALU.subtract), reads=[r_gn, r_yacc], writes=[r_yacc])
                yield S.op("dve", lambda e: e.tensor_tensor(out=yacc[:], in0=yacc[:], in1=gn2[:], op=ALU.mult), reads=[r_gn, r_yacc], writes=[r_yacc])
                yield S.op("act", lambda e: e.activation(out=yacc[:], in_=yacc[:], func=AF.Identity, scale=pcol(P_GNG, h), bias=pcol(P_GNB, h)), reads=[r_yacc, r_const], writes=[r_yacc])
                p3, p4 = bk1[0:64, 0:TG], bk1[0:64, TG:2 * TG]
                yield S.pe_group([lambda e: e.matmul(p3, lhsT=ones[0:64, 0:64], rhs=rkd[:], start=True, stop=True)], reads=[r_rkd, r_const], writes=[rb1])
                yield S.op("dve", lambda e: e.tensor_tensor(out=gn1[:], in0=p3, in1=vr[:], op=ALU.mult), reads=[rb1, r_zsh[2]], writes=[r_gn])
                yield S.op("dve", lambda e: e.tensor_tensor(out=yacc[:], in0=yacc[:], in1=gn1[:], op=ALU.add), reads=[r_gn, r_yacc], writes=[r_yacc])
                yield S.pe_group([lambda e: e.matmul(p4, lhsT=g2as[:, h * 64:(h + 1) * 64], rhs=sxg[:, 0, :], start=True, stop=False),
                                  lambda e: e.matmul(p4, lhsT=g2bs[:, h * 64:(h + 1) * 64], rhs=sxg[0:32, 1, :], start=False, stop=True)],
                                 reads=[r_sxg, r_lw], writes=[rb1])
                yield S.op("dve", lambda e: e.tensor_tensor(out=oA[:], in0=yacc[:], in1=p4, op=ALU.mult), reads=[rb1, r_yacc], writes=[r_oA])
                yield S.dma("sp", lambda e: e.dma_start(out=OA[g // 2, :, h, (g % 2) * TG:(g % 2 + 1) * TG], in_=oA[:]), reads=[r_oA])

            def head_body(g, d, final, h, lb):
                is_grid = g >= NPG
                sq, st = g_seg(g)
                chunks = list(range(NCH))
                txw, xat, r_tx = txwL[lb], xatL[lb], r_txL[lb]
                if g < NPG:
                    yield S.op("dve", lambda e: e.memset(Hs[:, h, :], 0.0), writes=[r_H[h]])
                elif (d == 0 and g == NPG) or (d == 1 and g == NG - 1):
                    yield S.dma("sp", lambda e: e.dma_start(out=Hs[:, h, :], in_=h0in[d, h, :, :]), writes=[r_H[h]])
                pa_, pb_, pc_ = bk0[0:64, 0:TG], bk0[0:64, TG:2 * TG], bk1[0:64, 0:TG]
                if d == 0:
                    for i in range(3):
                        yield S.dma("sp", lambda e, i=i: e.dma_start(out=zin_big[i][:], in_=ZR[sq][i * NH + h, :, st:st + TP]), reads=[r_zpad], writes=[r_zin[i]])
                    for i in range(3):
                        yield from shift_tile(zin_big[i], zsh[i], is_grid, 64, pcol(P_MUR + i, h), nbt, r_zin[i], r_zsh[i], r_nbt)
                        yield S.dma("sp", lambda e, i=i: e.dma_start(out=PREP[g, h, i, :, :], in_=zsh[i][:]), reads=[r_zsh[i]], writes=[r_prep[g][h][i]])
                    yield S.op("act", lambda e: e.activation(out=kkt[:], in_=kr[:], func=AF.Identity, scale=pcol(P_KK, h)), reads=[r_zsh[1], r_const], writes=[r_kk])
                    yield S.op("act", lambda e: e.activation(out=tmp64[:], in_=kkt[:], func=AF.Square), reads=[r_kk], writes=[r_tmp])
                    yield S.pe_group([lambda e: e.matmul(pa_, lhsT=ones[0:64, 0:64], rhs=tmp64[:], start=True, stop=True)], reads=[r_tmp, r_const], writes=[rb0])
                    yield S.op("act", lambda e: e.activation(out=tmp64[:], in_=pa_, func=AF.Sqrt), reads=[rb0], writes=[r_tmp])
                    yield S.op("dve", lambda e: e.tensor_scalar(out=tmp64[:], in0=tmp64[:], scalar1=1e-12, scalar2=None, op0=ALU.max), reads=[r_tmp], writes=[r_tmp])
                    yield S.op("dve", lambda e: e.reciprocal(out=tmp64[:], in_=tmp64[:]), reads=[r_tmp], writes=[r_tmp])
                    yield S.op("dve", lambda e: e.tensor_tensor(out=kkt[:], in0=kkt[:], in1=tmp64[:], op=ALU.mult), reads=[r_tmp, r_kk], writes=[r_kk])
                    yield S.dma("sp", lambda e: e.dma_start(out=PREP[g, h, 3, :, :], in_=kkt[:]), reads=[r_kk], writes=[r_prep[g][h][3]])
                else:
                    for i in range(3):
                        yield S.dma("sp", lambda e, i=i: e.dma_start(out=zsh[i][:], in_=PREP[g, h, i, :, :]), reads=[r_prep[g][h][i]], writes=[r_zsh[i]])
                    yield S.dma("sp", lambda e: e.dma_start(out=kkt[:], in_=PREP[g, h, 3, :, :]), reads=[r_prep[g][h][3]], writes=[r_kk])
                yield S.pe_group([lambda e: e.matmul(pb_, lhsT=w2s[:, d, h * 64:(h + 1) * 64], rhs=txw[:], start=True, stop=True)], reads=[r_tx, r_lw], writes=[rb0])
                yield S.op("act", lambda e: e.activation(out=wdt[:], in_=pb_, func=AF.Sigmoid, bias=pcol(9 + d, h)), reads=[rb0, r_const], writes=[r_wd])
                yield S.op("act", lambda e: e.activation(out=wdt[:], in_=wdt[:], func=AF.Exp, scale=-EXPC), reads=[r_wd], writes=[r_wd])
                yield S.pe_group([lambda e: e.matmul(pc_, lhsT=a2s[:, d, h * 64:(h + 1) * 64], rhs=xat[:], start=True, stop=True)], reads=[r_tx, r_lw], writes=[rb1])
                yield S.op("act", lambda e: e.activation(out=alr[:], in_=pc_, func=AF.Sigmoid, bias=a0s[:, d, h:h + 1]), reads=[rb1, r_const], writes=[r_alr])
                yield S.op("act", lambda e: e.activation(out=kdt[:], in_=alr[:], func=AF.Identity, scale=pcol(P_KA, h), bias=pcol(P_1MKA, h)),
                           reads=[r_alr, r_const], writes=[r_kd])
                yield S.op("dve", lambda e: e.tensor_tensor(out=kdt[:], in0=kdt[:], in1=kr[:], op=ALU.mult), reads=[r_kd, r_zsh[1]], writes=[r_kd])
                yield S.op("dve", lambda e: e.scalar_tensor_tensor(out=rkd[:], in0=kdt[:], scalar=pcol(P_RK, h), in1=rr[:], op0=ALU.mult, op1=ALU.mult),
                           reads=[r_kd, r_zsh[0], r_const], writes=[r_rkd])
                for c in chunks:
                    yield S.op("dve", lambda e, c=c: e.tensor_tensor_scan(out=pin[:, c * C:(c + 1) * C], data0=wdt[:, c * C:(c + 1) * C], data1=zeros[0:64, 0:C], initial=1.0,
                                                                          op0=ALU.mult, op1=ALU.add), reads=[r_wd, r_const], writes=[r_p])
                yield S.op("dve", lambda e: e.reciprocal(out=tmp64[:], in_=wdt[:]), reads=[r_wd], writes=[r_tmp])
                yield S.op("dve", lambda e: e.tensor_tensor(out=pex[:], in0=pin[:], in1=tmp64[:], op=ALU.mult), reads=[r_p, r_tmp], writes=[r_p])
                if d == 0:
                    sincv, sexcv = pin, pex
                else:
                    sincv, sexcv = sinc, sexc
                    yield S.op("dve", lambda e: e.reciprocal(out=sinc[:], in_=pex[:]), reads=[r_p], writes=[r_s])
                    yield S.op("dve", lambda e: e.reciprocal(out=sexc[:], in_=pin[:]), reads=[r_p], writes=[r_s])
                    for c in chunks:
                        yield S.op("act", lambda e, c=c: e.activation(out=sinc[:, c * C:(c + 1) * C], in_=sinc[:, c * C:(c + 1) * C], func=AF.Identity,
                                                                      scale=pin[:, (c + 1) * C - 1:(c + 1) * C]), reads=[r_p, r_s], writes=[r_s])
                        yield S.op("act", lambda e, c=c: e.activation(out=sexc[:, c * C:(c + 1) * C], in_=sexc[:, c * C:(c + 1) * C], func=AF.Identity,
                                                                      scale=pin[:, (c + 1) * C - 1:(c + 1) * C]), reads=[r_p, r_s], writes=[r_s])
                yield S.op("dve", lambda e: e.reciprocal(out=rinc[:], in_=sincv[:]), reads=[r_s], writes=[r_s])
                for c in chunks:
                    yield S.op("dve", lambda e, c=c: e.scalar_tensor_tensor(out=RA(arT[:, c, 0, :]), in0=kkt[:, c * C:(c + 1) * C], scalar=-1.0, in1=sexcv[:, c * C:(c + 1) * C],
                                                                            op0=ALU.mult, op1=ALU.mult), reads=[r_kk, r_s], writes=[r_ar])
                    yield S.op("dve", lambda e, c=c: e.tensor_tensor(out=RA(arT[:, c, 1, :]), in0=rr[:, c * C:(c + 1) * C], in1=sincv[:, c * C:(c + 1) * C], op=ALU.mult),
                               reads=[r_zsh[0], r_s], writes=[r_ar])
                yield S.op("dve", lambda e: e.tensor_tensor(out=tmp64[:], in0=kkt[:], in1=alr[:], op=ALU.mult), reads=[r_kk, r_alr], writes=[r_tmp])
                yield S.op("dve", lambda e: e.tensor_tensor(out=RA(btT[:]), in0=tmp64[:], in1=rinc[:], op=ALU.mult), reads=[r_s, r_tmp], writes=[r_bk])
                yield S.op("dve", lambda e: e.tensor_tensor(out=RA(ktT[:]), in0=kdt[:], in1=rinc[:], op=ALU.mult), reads=[r_s, r_kd], writes=[r_bk])
                corder = chunks if d == 0 else chunks[::-1]
                Hv = Hs[:, h, :]
                rH = r_H[h]
                m4 = msk4[:, d, :, :].rearrange("p a t -> p (a t)")
                fns = []
                for c in chunks:
                    cs = slice(c * C, (c + 1) * C)
                    for i, src in enumerate([arT[:, c, 0, :], btT[:, cs], ktT[:, cs], vr[:, cs]]):
                        fns.append(lambda e, c=c, i=i, src=src: e.transpose(out=bk1[:, c * 256 + i * 64:c * 256 + (i + 1) * 64], in_=src, identity=ident[0:64, 0:64]))
                yield S.pe_group(fns, reads=[r_ar, r_bk, r_zsh[2], r_const], writes=[rb1])
                yield S.op("act", lambda e: e.activation(out=RA(tokm[:].rearrange("p c a k -> p (c a k)")), in_=bk1[:, :], func=AF.Copy), reads=[rb1], writes=[r_tokm])
                yield S.pe_group([(lambda e, c=c: e.matmul(bk0[:, c * 256:(c + 1) * 256], lhsT=RA(btT[:, c * C:(c + 1) * C]), rhs=RA(arv[:, c, :]), start=True, stop=True)) for c in chunks],
                                 reads=[r_bk, r_ar], writes=[rb0])
                yield S.op("dve", lambda e: e.tensor_tensor(out=RA(ALLA[:].rearrange("p c a t -> p (c a t)")), in0=bk0[:, :], in1=m4, op=ALU.mult), reads=[rb0, r_const], writes=[r_ALLA])
                yield S.pe_group([(lambda e, c=c: e.matmul(bk1[:, c * 256:(c + 1) * 256], lhsT=RA(ktT[:, c * C:(c + 1) * C]), rhs=RA(arv[:, c, :]), start=True, stop=True)) for c in chunks],
                                 reads=[r_bk, r_ar], writes=[rb1])
                yield S.op("dve", lambda e: e.tensor_tensor(out=RA(ALLB[:].rearrange("p c a t -> p (c a t)")), in0=bk1[:, :], in1=m4, op=ALU.mult), reads=[rb1, r_const], writes=[r_ALLB])
                yield S.pe_group([(lambda e, c=c: e.matmul(bk0[:, c * 128:(c + 1) * 128], lhsT=RA(arT[:, c, 0, :]), rhs=RA(btT[:, c * C:(c + 1) * C]), start=True, stop=True)) for c in chunks],
                                 reads=[r_bk, r_ar], writes=[rb0])
                yield S.op("dve", lambda e: e.tensor_tensor(out=R(LL[0][:, 1, :, :].rearrange("p c t -> p (c t)")), in0=bk0[:, 0:256], in1=msk2[:, d, :, :].rearrange("p a t -> p (a t)"), op=ALU.mult),
                           reads=[rb0, r_const], writes=[r_LL[0]])
                yield S.pe_group([(lambda e, c=c: e.matmul(bk1[:, c * 64:(c + 1) * 64], lhsT=RA(ALLB[:, c, 0, :]), rhs=RA(tokm[:, c, 3, :]), start=True, stop=True)) for c in chunks],
                                 reads=[r_ALLB, r_tokm], writes=[rb1])
                yield S.op("act", lambda e: e.activation(out=R(Xb[0][:, :, 64:128]), in_=bk1[:, 0:128].rearrange("p (c k) -> p c k", k=64), func=AF.Copy), reads=[rb1], writes=[r_X[0]])
                yield S.op("act", lambda e: e.activation(out=R(Xb[0][:, :, 0:64]), in_=tokm[:, :, 0, :], func=AF.Copy), reads=[r_tokm], writes=[r_X[0]])
                for j in range(7):
                    i0, i1 = j % 2, (j + 1) % 2

                    def LtA(c, j=j, i0=i0):
                        return ALLA[:, c, 0, :] if j == 0 else LL[i0][:, 0, c, :]

                    def LA(c, i0=i0):
                        return LL[i0][:, 1, c, :]
                    rLt = [r_ALLA, r_LL[0]] if j == 0 else [r_LL[i0]]
                    yield S.pe_group([(lambda e, c=c, LtA=LtA, i0=i0: e.matmul(bk0[:, c * 128:(c + 1) * 128], lhsT=R(LtA(c)), rhs=R(Xb[i0][:, c, :]), start=True, stop=True)) for c in chunks],
                                     reads=rLt + [r_X[i0]], writes=[rb0])
                    if j < 6:
                        fns = []
                        for c in chunks:
                            fns.append(lambda e, c=c, LtA=LtA, LA=LA: e.matmul(bk1[:, c * 128:(c + 1) * 128], lhsT=R(LA(c)), rhs=R(LtA(c)), start=True, stop=True))
                        for c in chunks:
                            fns.append(lambda e, c=c, LtA=LtA, LA=LA: e.matmul(bk1[:, 256 + c * 128:256 + (c + 1) * 128], lhsT=R(LtA(c)), rhs=R(LA(c)), start=True, stop=True))
                        yield S.pe_group(fns, reads=rLt + [r_LL[i0]], writes=[rb1])
                    if j < 6:
                        yield S.op("dve", lambda e, i0=i0, i1=i1: e.tensor_tensor(out=R(Xb[i1][:].rearrange("p c t -> p (c t)")), in0=bk0[:, 0:256], in1=Xb[i0][:].rearrange("p c t -> p (c t)"), op=ALU.add),
                                   reads=[rb0, r_X[i0]], writes=[r_X[i1]])
                    else:
                        yield S.op("dve", lambda e, i0=i0: e.tensor_tensor(out=RA(Xfin[:].rearrange("p c t -> p (c t)")), in0=bk0[:, 0:256], in1=Xb[i0][:].rearrange("p c t -> p (c t)"), op=ALU.add),
                                   reads=[rb0, r_X[i0]], writes=[r_Xfin])
                    if j < 6:
                        yield S.op("act", lambda e, i1=i1: e.activation(out=R(LL[i1][:].rearrange("p a c t -> p (a c t)")), in_=bk1[:, :], func=AF.Copy), reads=[rb1], writes=[r_LL[i1]])
                Xf = Xfin; rXf = r_Xfin
                fns = []
                for c in chunks:
                    fns.append(lambda e, c=c: e.matmul(bk0[0:64, c * 128:c * 128 + 64], lhsT=RA(Xf[:, c, 0:64]), rhs=RA(tokm[:, c, 1, :]), start=True, stop=True))
                    fns.append(lambda e, c=c: e.matmul(bk0[0:64, c * 128 + 64:c * 128 + 128], lhsT=RA(tokm[:, c, 1, :]), rhs=RA(Xf[:, c, 64:128]), start=True, stop=False))
                    fns.append(lambda e, c=c: e.matmul(bk0[0:64, c * 128 + 64:c * 128 + 128], lhsT=RA(tokm[:, c, 2, :]), rhs=RA(tokm[:, c, 3, :]), start=False, stop=True))
                    fns.append(lambda e, c=c: e.matmul(bk0[0:64, 256 + c * 128:256 + (c + 1) * 128], lhsT=RA(Xf[:, c, 0:64]), rhs=RA(ALLA[:, c, 1, :]), start=True, stop=True))
                yield S.pe_group(fns, reads=[rXf, r_tokm, r_ALLA], writes=[rb0])
                yield S.op("act", lambda e: e.activation(out=MN[:].rearrange("p c a k -> p (c a k)"), in_=bk0[0:64, 0:256], func=AF.Copy), reads=[rb0], writes=[r_MN])
                yield S.op("dve", lambda e: e.tensor_tensor(out=QT[:], in0=bk0[0:64, 256:512].rearrange("p (c t) -> p c t", t=128), in1=arT[:, :, 1, :], op=ALU.add),
                           reads=[rb0, r_ar], writes=[r_QT])
                for c in corder:
                    cs = slice(c * C, (c + 1) * C)
                    yield S.pe_group([lambda e, c=c: e.matmul(bk1[0:64, c * 128:(c + 1) * 128], lhsT=RA(Xf[:, c, 64:128]), rhs=RA(ALLA[:, c, 1, :]), start=True, stop=False),
                                      lambda e, c=c: e.matmul(bk1[0:64, c * 128:(c + 1) * 128], lhsT=RA(tokm[:, c, 3, :]), rhs=RA(ALLB[:, c, 1, :]), start=False, stop=False),
                                      lambda e, c=c: e.matmul(bk1[0:64, c * 128:(c + 1) * 128], lhsT=Hv, rhs=QT[:, c, :], start=False, stop=True),
                                      lambda e, c=c: e.matmul(bk1[0:64, 256:320], lhsT=MN[:, c, 0, :], rhs=Hv, start=True, stop=True)],
                                     reads=[rXf, r_tokm, r_ALLA, r_ALLB, r_QT, rH, r_MN], writes=[rb1])
                    pc = pin[:, (c + 1) * C - 1:(c + 1) * C]
                    yield S.op("dve", lambda e, c=c: e.tensor_tensor(out=Ht[:], in0=bk1[0:64, 256:320], in1=MN[:, c, 1, :], op=ALU.add), reads=[rb1, r_MN], writes=[r_Ht])
                    yield S.op("dve", lambda e: e.tensor_tensor(out=Ht[:], in0=Ht[:], in1=Hv, op=ALU.add), reads=[r_Ht, rH], writes=[r_Ht])
                    yield S.op("act", lambda e, pc=pc: e.activation(out=Hv, in_=Ht[:], func=AF.Identity, scale=pc), reads=[r_Ht, r_p], writes=[rH])
                yield S.op("act", lambda e: e.activation(out=yacc[:], in_=bk1[0:64, 0:256], func=AF.Copy), reads=[rb1], writes=[r_yacc])
                if not final:
                    yield S.dma("sp", lambda e: e.dma_start(out=YF[g, h, :, :], in_=yacc[:]), reads=[r_yacc], writes=[r_yf[g][h][0]])
                    yield S.dma("sp", lambda e: e.dma_start(out=RKF[g, h, :, :], in_=rkd[:]), reads=[r_rkd], writes=[r_yf[g][h][1]])
                else:
                    yield from head_final(g, h, lb)
                if g < NPG:
                    yield S.dma("sp", lambda e: e.dma_start(out=st_out[d, g, h, :, :], in_=Hs[:, h, :]), reads=[r_H[h]], is_output=True)

            return head_body

        streams = [make_stream(sid) for sid in range(KS)]

        def run_streams(gens, stagger=48):
            gens = list(gens)
            active = []
            for i, gen in enumerate(gens):
                if i > 0:
                    for _ in range(stagger):
                        for g_ in list(active):
                            try:
                                next(g_)
                            except StopIteration:
                                active.remove(g_)
                active.append(gen)
            while active:
                for g_ in list(active):
                    try:
                        next(g_)
                    except StopIteration:
                        active.remove(g_)

        def drain(gen):
            for _ in gen:
                pass

        def lora_prep(g, lb):
            is_grid = g >= NPG
            sq, st = g_seg(g)
            txw, xat, r_tx, sxg, r_sxg = txwL[lb], xatL[lb], r_txL[lb], sxgL[lb], r_sxgL[lb]
            for q in range(3):
                yield S.dma("sp", lambda e, q=q: e.dma_start(out=lin_big[:, q, :], in_=ZLP[sq][q, :, st:st + TP]), reads=[r_zpad], writes=[r_lin])
            for q in range(3):
                yield from shift_tile(lin_big[:, q, :], lsh[:, q, :], is_grid, 128, mus[:, q:q + 1], nbl, r_lin, r_lsh, r_nbl)
            yield S.op("act", lambda e: e.activation(out=txw[:], in_=lsh[0:64, 0, :], func=AF.Tanh), reads=[r_lsh], writes=[r_tx])
            yield S.op("act", lambda e: e.activation(out=xaf[:], in_=lsh[:, 0, :], func=AF.Copy), reads=[r_lsh], writes=[r_xaf])
            yield S.dma("sp", lambda e: e.dma_start(out=xat[:], in_=xaf[64:128, :]), reads=[r_xaf], writes=[r_tx])
            yield S.op("act", lambda e: e.activation(out=sxg[:, 0, :], in_=lsh[:, 1, :], func=AF.Sigmoid), reads=[r_lsh], writes=[r_sxg])
            yield S.op("act", lambda e: e.activation(out=sxg[0:32, 1, :], in_=lsh[0:32, 2, :], func=AF.Sigmoid), reads=[r_lsh], writes=[r_sxg])

        def rwkv_pass(d, final, order):
            lora_done = [-1]
            fin_cnt = [0] * len(order)

            def groups_done():
                k = 0
                while k < len(order) and fin_cnt[k] == KS:
                    k += 1
                return k

            def aux_gen():
                for n, g in enumerate(order):
                    while groups_done() < n - 1:
                        yield None
                    yield from lora_prep(g, n % 2)
                    lora_done[0] = n

            def stream_gen(sid):
                for n, g in enumerate(order):
                    while lora_done[0] < n:
                        yield None
                    lb = n % 2
                    for h in range(sid, NH, KS):
                        yield from streams[sid](g, d, final, h, lb)
                    fin_cnt[n] += 1

            run_streams([aux_gen()] + [stream_gen(sid) for sid in range(KS)])

        rwkv_pass(0, False, list(range(NG)))
        rwkv_pass(1, True, list(range(NG - 1, -1, -1)))

    with ExitStack() as stk:
        cur[0] = stk
        EA = dense_env()
        scond = sb("scond", [128, 16, 2]); r_scond = Res("scond")
        scondb = sb("scondb", [128, 16, 2], BF16)
        badas = sb("badas", [128, 144])
        S.dma("sp", lambda e: e.dma_start(out=scond[:], in_=condT[:, :, :]), writes=[r_scond])
        S.dma("sp", lambda e: e.dma_start(out=badas[:], in_=b_adaT[:, :]), writes=[r_const])
        S.op("act", lambda e: e.activation(out=scondb[:], in_=scond[:], func=AF.Silu), reads=[r_scond], writes=[r_scond])
        def ada_stage(i):
            for j in range(48 * i, 48 * (i + 1)):
                wv, rw = EA.load_w(w_ada[j, :, :, :], 128, 16, 128)
                pt, rp = next_pd()
                S.pe_group([(lambda e, k=k, wv=wv, pt=pt: e.matmul(pt[:, 0:2], lhsT=wv[:, k, :], rhs=scondb[:, k, :], start=(k == 0), stop=(k == 15)))
                            for k in range(16)], reads=[rw, r_scond], writes=[rp])
                S.op("dve", lambda e, j=j, pt=pt: e.tensor_scalar(out=mod[:, j, :], in0=pt[:, 0:2], scalar1=badas[:, j:j + 1], scalar2=None, op0=ALU.add),
                     reads=[rp, r_const], writes=[r_modS[i]])
            S.op("dve", lambda e: e.tensor_copy(out=modd[:, 3 * i + 0, :, :], in_=mod[:, (3 * i) * 16:(3 * i + 1) * 16, :]), reads=[r_modS[i]], writes=[r_modS[i]])
            S.op("dve", lambda e: e.tensor_scalar(out=modd[:, 3 * i + 1, :, :], in0=mod[:, (3 * i + 1) * 16:(3 * i + 2) * 16, :], scalar1=1.0, scalar2=None, op0=ALU.add),
                 reads=[r_modS[i]], writes=[r_modS[i]])
            S.op("dve", lambda e: e.tensor_scalar(out=modd[:, 3 * i + 2, :, :], in0=mod[:, (3 * i + 2) * 16:(3 * i + 3) * 16, :], scalar1=(1.0 if i == 1 else 0.5), scalar2=None, op0=ALU.mult),
                 reads=[r_modS[i]], writes=[r_modS[i]])

        ada_stage(0)

        stg = sb("stg", [128, TD]); r_stg = Res("stg")
        stgb = sb("stgb", [128, 4, TD], BF16); r_stgb = Res("stgb")
        vtm = sb("vtm", [128, DB]); r_vtm = Res("vtm")
        vnb = sb("vnb", [128, DB], BF16); r_vnb = Res("vnb")
        bnst = sb("bnst", [128, 2, 6]); bnag = sb("bnag", [128, 2]); r_bn = Res("bn")
        sgl = sb("sgl", [128, 2, DB]); r_sgl = Res("sgl")
        S.dma("sp", lambda e: e.dma_start(out=sgl[:], in_=sgln[:, :, :]), writes=[r_sgl])

        def phase_a(sg):
            cnd = sg_cnd(sg)
            segs = sg_segs(sg)
            xa_t, h_t, r_xa, r_h = EA.xa_t, EA.h_t, EA.r_xa, EA.r_h
            S.dma("sp", lambda e: e.dma_start(out=xa_t[:], in_=xT[sg, :, :, :]), writes=[r_xa])
            EA.modulate(0, cnd)
            EA.ffn(w_f1, w_f1o, 0, cnd)
            if sg == 0:
                ada_stage(1)
                ada_stage(2)
            EA.layernorm(0)
            S.dma("sp", lambda e: e.dma_start(out=X1[sg, :, :, :], in_=xa_t[:]), reads=[r_xa])
            EA.modulate(1, cnd)
            for q in range(24):
                def epi(pt, rp, q=q):
                    S.op("act", lambda e: e.activation(out=stg[:, :], in_=pt[:, :], func=AF.Copy), reads=[rp], writes=[r_stg])
                    for (sq, st, n, co) in segs:
                        for half in range(2):
                            dst = ZR[sq][2 * q + half, :, PADZ + st:PADZ + st + n]
                            S.dma("sp", lambda e, dst=dst, co=co, n=n, half=half: e.dma_start(out=dst, in_=stg[half * 64:(half + 1) * 64, co:co + n]), reads=[r_stg])
                EA.proj_fm(w_rkv, q, 128, epi)
            for q in range(3):
                def epi(pt, rp, q=q):
                    S.op("dve", lambda e: e.tensor_copy(out=stg[:, :], in_=pt[:, :]), reads=[rp], writes=[r_stg])
                    for (sq, st, n, co) in segs:
                        dst = ZLP[sq][q, :, PADZ + st:PADZ + st + n]
                        S.dma("sp", lambda e, dst=dst, co=co, n=n: e.dma_start(out=dst, in_=stg[:, co:co + n]), reads=[r_stg])
                EA.proj_fm(w_lora, q, 128, epi)
            for q in range(8):
                def epi(pt, rp, q=q):
                    S.op("act", lambda e: e.activation(out=stgb[:, q % 4, :], in_=pt[:, :], func=AF.Gelu), reads=[rp], writes=[r_stgb])
                    if q % 4 == 3:
                        S.dma("sp", lambda e: e.dma_start(out=UTs[sg, :, q - 3:q + 1, :], in_=stgb[:]), reads=[r_stgb])
                EA.proj_fm(w_u, q, 128, epi)
            for q in range(32):
                def epi(pt, rp, q=q):
                    S.op("act", lambda e: e.activation(out=stgb[:, q % 4, :], in_=pt[:, :], func=AF.Sigmoid), reads=[rp], writes=[r_stgb])
                    if q % 4 == 3:
                        S.dma("sp", lambda e: e.dma_start(out=GTs[sg, :, q - 3:q + 1, :], in_=stgb[:]), reads=[r_stgb])
                EA.proj_fm(w_gate, q, 128, epi)
            for tc in range(TD // 128):
                for cb in range(4):
                    wv, rw = EA.load_w(w_v[cb, :, :, :], 128, 16, 256)
                    pt, rp = next_pd()
                    S.pe_group([(lambda e, k=k, wv=wv, pt=pt, tc=tc: e.matmul(pt[:, 0:256], lhsT=h_t[:, k, tc * 128:(tc + 1) * 128], rhs=wv[:, k, :], start=(k == 0), stop=(k == 15)))
                                for k in range(16)], reads=[rw, r_h], writes=[rp])
                    S.op("act", lambda e, pt=pt, cb=cb: e.activation(out=vtm[:, cb * 256:(cb + 1) * 256], in_=pt[:, 0:256], func=AF.Gelu), reads=[rp], writes=[r_vtm])
                for cb in range(2):
                    S.op("dve", lambda e, cb=cb: e.bn_stats(out=bnst[:, cb, :], in_=vtm[:, cb * 512:(cb + 1) * 512]), reads=[r_vtm], writes=[r_bn])
                S.op("dve", lambda e: e.bn_aggr(out=bnag[:], in_=bnst[:].rearrange("p a b -> p (a b)")), reads=[r_bn], writes=[r_bn])
                S.op("dve", lambda e: e.tensor_scalar(out=bnag[:, 1:2], in0=bnag[:, 1:2], scalar1=LN_EPS, scalar2=None, op0=ALU.add), reads=[r_bn], writes=[r_bn])
                S.op("act", lambda e: e.activation(out=bnag[:, 1:2], in_=bnag[:, 1:2], func=AF.Sqrt), reads=[r_bn], writes=[r_bn])
                S.op("dve", lambda e: e.reciprocal(out=bnag[:, 1:2], in_=bnag[:, 1:2]), reads=[r_bn], writes=[r_bn])
                S.op("dve", lambda e: e.tensor_scalar(out=vtm[:], in0=vtm[:], scalar1=bnag[:, 0:1], scalar2=bnag[:, 1:2], op0=ALU.subtract, op1=ALU.mult),
                     reads=[r_bn, r_vtm], writes=[r_vtm])
                S.op("dve", lambda e: e.tensor_tensor(out=vtm[:], in0=vtm[:], in1=sgl[:, 0, :], op=ALU.mult), reads=[r_vtm, r_sgl], writes=[r_vtm])
                S.op("dve", lambda e: e.tensor_tensor(out=vnb[:], in0=vtm[:], in1=sgl[:, 1, :], op=ALU.add), reads=[r_vtm, r_sgl], writes=[r_vnb])
                S.dma("sp", lambda e, tc=tc: e.dma_start(out=VNs[sg, tc, :, :], in_=vnb[:]), reads=[r_vnb])

        for sg in range(NSG):
            phase_a(sg)
        S.barrier()
        cur[0] = None

    with ExitStack() as stk:
        cur[0] = stk
        rwkv_phase()
        S.barrier()
        cur[0] = None

    with ExitStack() as stk:
        cur[0] = stk
        EC = dense_env()
        outA = EC.GBIG[0:64, 0:NH * TD].rearrange("p (h t) -> p h t", t=TD)
        mrg = EC.GBIG[:, NH * TD:NH * TD + 16 * TD].rearrange("p (k t) -> p k t", t=TD)
        ut = EC.GBIG[:, 32 * TD:40 * TD].rearrange("p (k t) -> p k t", t=TD)
        r_G = EC.r_G
        gA = sb("gA", [128, TD], BF16); gB = sb("gB", [128, TD], BF16); r_gAB = Res("gAB")
        vn_in = sb("vn_in", [128, TD // 128, DB], BF16); r_vnin = Res("vnin")
        yB = sb("yB", [128, 8, TD], BF16); r_yB = Res("yB")
        sgws = sb("sgws", [128, 8, 128], BF16); r_sgw = Res("sgw")
        sgbs = sb("sgbs", [128, 8 * 128])
        S.dma("pool", lambda e: e.dma_start(out=sgws[:], in_=sgwT[:, :, :]), writes=[r_sgw])
        S.dma("sp", lambda e: e.dma_start(out=sgbs[:], in_=sgb[:, :]), writes=[r_sgw])
        mtmp = sb("mtmp", [128, TD]); r_mtmp = Res("mtmp")

        def phase_c(sg):
            cnd = sg_cnd(sg)
            xa_t, r_xa, sa_t, r_sa = EC.xa_t, EC.r_xa, EC.sa_t, EC.r_sa
            S.dma("sp", lambda e: e.dma_start(out=outA[:], in_=OA[sg, :, :, :]), writes=[r_G])
            S.dma("sp", lambda e: e.dma_start(out=ut[:], in_=UTs[sg, :, :, :]), writes=[r_G])
            S.dma("sp", lambda e: e.dma_start(out=vn_in[:], in_=VNs[sg, :, :, :].rearrange("t p c -> p t c")), writes=[r_vnin])
            for tc in range(TD // 128):
                for gg in range(8):
                    pt, rp = next_pd()
                    S.pe_group([lambda e, pt=pt, tc=tc, gg=gg: e.matmul(pt[:, 0:128], lhsT=vn_in[:, tc, gg * 128:(gg + 1) * 128], rhs=sgws[:, gg, :], start=True, stop=True)],
                               reads=[r_vnin, r_sgw], writes=[rp])
                    S.op("dve", lambda e, pt=pt, gg=gg: e.tensor_tensor(out=mtmp[:, 0:128], in0=pt[:, 0:128], in1=sgbs[:, gg * 128:(gg + 1) * 128], op=ALU.add),
                         reads=[rp, r_sgw], writes=[r_mtmp])
                    S.op("dve", lambda e, tc=tc, gg=gg: e.tensor_tensor(out=yB[:, gg, tc * 128:(tc + 1) * 128], in0=mtmp[:, 0:128], in1=ut[:, gg, tc * 128:(tc + 1) * 128], op=ALU.mult),
                         reads=[r_mtmp, r_G], writes=[r_yB])
            for dc in range(16):
                wfull, rwa = EC.load_w(w_pa[dc, :, :, :], 64, 16, 128)
                wpa_v = wfull
                wpb_v, rwb = EC.load_w(w_pb[dc, :, :, :], 128, 8, 128)
                S.dma("sp", lambda e, dc=dc: e.dma_start(out=gA[:], in_=GTs[sg, :, dc, :]), writes=[r_gAB])
                S.dma("sp", lambda e, dc=dc: e.dma_start(out=gB[:], in_=GTs[sg, :, 16 + dc, :]), writes=[r_gAB])
                pa, rpa = next_pd()
                pb, rpb = next_pd()
                S.pe_group([(lambda e, k=k, pa=pa, wpa_v=wpa_v: e.matmul(pa[:, :], lhsT=wpa_v[:, k, :], rhs=outA[:, k, :], start=(k == 0), stop=(k == 15))) for k in range(16)],
                           reads=[rwa, r_G], writes=[rpa])
                S.op("dve", lambda e, pa=pa: e.tensor_tensor(out=mtmp[:], in0=pa[:, :], in1=gA[:], op=ALU.mult), reads=[rpa, r_gAB], writes=[r_mtmp])
                S.pe_group([(lambda e, k=k, pb=pb, wpb_v=wpb_v: e.matmul(pb[:, :], lhsT=wpb_v[:, k, :], rhs=yB[:, k, :], start=(k == 0), stop=(k == 7))) for k in range(8)],
                           reads=[rwb, r_yB], writes=[rpb])
                S.op("dve", lambda e, pb=pb: e.tensor_tensor(out=sa_t[:], in0=pb[:, :], in1=gB[:], op=ALU.mult), reads=[rpb, r_gAB], writes=[r_sa])
                S.op("dve", lambda e, dc=dc: e.tensor_tensor(out=mrg[:, dc, :], in0=mtmp[:], in1=sa_t[:], op=ALU.add), reads=[r_mtmp, r_sa], writes=[r_G])
            S.dma("sp", lambda e: e.dma_start(out=xa_t[:], in_=X1[sg, :, :, :]), writes=[r_xa])
            for dc in range(16):
                wo, rwo = EC.load_w(w_o[dc, :, :, :], 128, 16, 128)
                po, rpo = next_pd()
                S.pe_group([(lambda e, k=k, wo=wo, po=po: e.matmul(po[:, :], lhsT=wo[:, k, :], rhs=mrg[:, k, :], start=(k == 0), stop=(k == 15))) for k in range(16)],
                           reads=[rwo, r_G], writes=[rpo])
                S.op("act", lambda e, dc=dc: e.activation(out=xa_t[:, dc, :], in_=xa_t[:, dc, :], func=AF.Identity, scale=ALPHA), reads=[r_xa], writes=[r_xa])
                S.op("dve", lambda e, dc=dc, po=po: e.scalar_tensor_tensor(out=xa_t[:, dc, :], in0=po[:, :], scalar=mcol(1, 2, dc, cnd), in1=xa_t[:, dc, :], op0=ALU.mult, op1=ALU.add),
                     reads=[rpo, r_xa, r_modS[1]], writes=[r_xa])
            EC.layernorm(1)
            EC.modulate(2, cnd)
            EC.ffn(w_f2, w_f2o, 2, cnd)
            EC.layernorm(2)
            S.dma("sp", lambda e: e.dma_start(out=yT[sg, :, :, :], in_=xa_t[:]), reads=[r_xa], is_output=True)

        for sg in range(NSG):
            phase_c(sg)
        cur[0] = None

    S.emit()
    return nc


def _chunk_w(W, cols=128):
    K, N = W.shape
    return np.ascontiguousarray(W.reshape(K // 128, 128, N // cols, cols).transpose(2, 1, 0, 3))


def _fm(x):
    T = x.shape[0]
    return np.ascontiguousarray(x.T.reshape(16, 128, T).transpose(1, 0, 2))


_NC_CACHE = {}


def kernel(x_prompt, x_sample, c, state_rwkv_fwd, state_rwkv_bwd, c_ctx, w_ada, b_ada, ln_g, ln_b,
           ffn_w_in, ffn_w_out, w_in, shift_mu, rw_w0, rw_w2, rw_a0, rw_a2, rw_g2, rw_k_k, rw_k_a,
           rw_r_k, rw_gn_g, rw_gn_b, sg_ln_g, sg_ln_b, sg_w, sg_b, w_pa, w_pb, w_o):
    f = lambda a: np.asarray(a, dtype=np.float32)
    x_prompt, x_sample, c, c_ctx = f(x_prompt), f(x_sample), f(c), f(c_ctx)
    w_in0 = f(w_in)[0]
    shared = {}
    shared["w_ada"] = _chunk_w(f(w_ada)[0])
    shared["b_adaT"] = np.ascontiguousarray(f(b_ada)[0].reshape(144, 128).T)
    lg, lb = f(ln_g)[0], f(ln_b)[0]
    shared["lnT"] = np.ascontiguousarray(np.concatenate([lg, lb], 0).reshape(6, 16, 128).transpose(2, 0, 1))
    for nm, i in (("w_f1", 0), ("w_f2", 1)):
        W = f(ffn_w_in)[0, i]
        Wc = _chunk_w(W)
        inter = np.empty_like(Wc)
        inter[0::2] = Wc[:43]
        inter[1::2] = Wc[43:]
        shared[nm] = inter
        Wo = f(ffn_w_out)[0, i]
        shared[nm + "o"] = np.ascontiguousarray(Wo.reshape(43, 128, 16, 128).transpose(2, 1, 0, 3))
    shared["w_rkv"] = _chunk_w(w_in0[:, :3072], 128)
    wl = np.zeros((2048, 384), np.float32)
    wl[:, :288] = w_in0[:, 3072:3360]
    shared["w_lora"] = _chunk_w(wl)
    shared["w_u"] = _chunk_w(w_in0[:, 3360:4384])
    shared["w_v"] = _chunk_w(w_in0[:, 4384:5408], 256)
    shared["w_gate"] = _chunk_w(w_in0[:, 5408:])
    mu = f(shift_mu)[0]
    p64 = np.zeros((64, 12, NH), np.float32)
    hp = lambda v: np.ascontiguousarray(v.reshape(NH, 64).T)
    p64[:, 0] = hp(mu[0:1024]); p64[:, 1] = hp(mu[1024:2048]); p64[:, 2] = hp(mu[2048:3072])
    kk_, ka_ = f(rw_k_k)[0], f(rw_k_a)[0]
    p64[:, 3] = hp(kk_); p64[:, 4] = hp(ka_)
    p64[:, 5] = hp(np.float32(1.0) - ka_)
    p64[:, 6] = hp(f(rw_r_k)[0].reshape(-1)); p64[:, 7] = hp(f(rw_gn_g)[0]); p64[:, 8] = hp(f(rw_gn_b)[0])
    p64[:, 9] = hp(f(rw_w0)[0, 0]); p64[:, 10] = hp(f(rw_w0)[0, 1])
    shared["p64"] = p64
    a0 = f(rw_a0)[0]
    shared["a0in"] = np.ascontiguousarray(np.stack([hp(a0[0]), hp(a0[1])], 1))
    shared["w2T"] = np.ascontiguousarray(f(rw_w2)[0].transpose(1, 0, 2))
    shared["a2T"] = np.ascontiguousarray(f(rw_a2)[0].transpose(1, 0, 2))
    g2 = f(rw_g2)[0]
    shared["g2a"] = np.ascontiguousarray(g2[:128]); shared["g2b"] = np.ascontiguousarray(g2[128:160])
    mul = np.zeros((384,), np.float32); mul[:288] = mu[3072:3360]
    shared["mu_l"] = np.ascontiguousarray(mul.reshape(3, 128).T)
    shared["sgln"] = np.ascontiguousarray(np.broadcast_to(np.stack([f(sg_ln_g)[0], f(sg_ln_b)[0]], 0)[None], (128, 2, DB)))
    shared["sgwT"] = np.ascontiguousarray(f(sg_w)[0].transpose(2, 0, 1))
    shared["sgb"] = np.ascontiguousarray(np.broadcast_to(f(sg_b)[0].reshape(1, -1), (128, 1024)))
    wpa = f(w_pa)[0]
    shared["w_pa"] = np.ascontiguousarray(wpa.reshape(NH, 64, 16, 128).transpose(2, 1, 0, 3))
    shared["w_pb"] = _chunk_w(f(w_pb)[0])
    shared["w_o"] = _chunk_w(f(w_o)[0])
    idx = np.arange(128)
    m = np.zeros((128, 2, 3, 128), np.float32)
    m[:, 0, 0] = (idx[:, None] < idx[None, :]); m[:, 0, 1] = (idx[:, None] <= idx[None, :]); m[:, 0, 2] = (idx[:, None] > idx[None, :])
    m[:, 1, 0] = (idx[:, None] > idx[None, :]); m[:, 1, 1] = (idx[:, None] >= idx[None, :]); m[:, 1, 2] = (idx[:, None] < idx[None, :])
    shared["masks"] = m
    shared["ident_in"] = np.eye(128, dtype=np.float32)

    sf, sbw = f(state_rwkv_fwd), f(state_rwkv_bwd)
    in_maps = []
    for core in range(8):
        b = core % 2
        xs = []
        for gi in range(2):
            xs.append(_fm(x_prompt[4 * core + 2 * gi: 4 * core + 2 * gi + 2].reshape(TD, D)))
        for gi in range(4):
            xs.append(_fm(x_sample[b, gi * TD:(gi + 1) * TD]))
        mp = dict(shared)
        mp["xT"] = np.ascontiguousarray(np.stack(xs, 0))
        cond = np.stack([c_ctx, c[b]], 1)
        mp["condT"] = np.ascontiguousarray(cond.reshape(16, 128, 2).transpose(1, 0, 2))
        h0 = np.stack([sf[b, 0].transpose(0, 2, 1), sbw[b, 0].transpose(0, 2, 1)], 0)
        mp["h0in"] = np.ascontiguousarray(h0)
        in_maps.append(mp)

    if "nc" not in _NC_CACHE:
        _NC_CACHE["nc"] = build_program()
    nc = _NC_CACHE["nc"]
    res = run_bass_kernel_spmd(nc, in_maps, core_ids=list(range(8)))

    def unfm(y):
        return y.transpose(1, 0, 2).reshape(D, -1).T

    y_prompt = np.zeros((32, 256, D), np.float32)
    y_sample = np.zeros((2, 2048, D), np.float32)
    nsf = np.zeros((32, 1, NH, HD, HD), np.float32)
    nsb = np.zeros((32, 1, NH, HD, HD), np.float32)
    for core in range(8):
        r = res.results[core]
        yt = np.asarray(r["yT"])
        for gi in range(2):
            y_prompt[4 * core + 2 * gi: 4 * core + 2 * gi + 2] = unfm(yt[gi]).reshape(2, 256, D)
        if core < 2:
            for gi in range(4):
                y_sample[core, gi * TD:(gi + 1) * TD] = unfm(yt[2 + gi])
        so = np.asarray(r["st_out"])
        nsf[4 * core:4 * core + 4, 0] = so[0].transpose(0, 1, 3, 2)
        nsb[4 * core:4 * core + 4, 0] = so[1].transpose(0, 1, 3, 2)
    return (y_prompt, y_sample, nsf, nsb)
```

```python
import math
from contextlib import ExitStack
from types import SimpleNamespace
import numpy as np
import concourse.bass as bass
import concourse.mybir as mybir
from concourse.bass_utils import run_bass_kernel_spmd

F32 = mybir.dt.float32
F32R = mybir.dt.float32r
BF16 = mybir.dt.bfloat16
AF = mybir.ActivationFunctionType
ALU = mybir.AluOpType

D = 2048; DFF = 5504; DA = 1024; HD = 64; NH = 16; DB = 1024; NSH = 3360; NIN = 9504
LW = 64; LA = 64; LG = 160
ALPHA = 2.0 ** 0.25
LN_EPS = 1e-5; GN_EPS = 64e-5
TG = 256
NG = 12
NPG = 4
TD = 512
NSG = 6
C = 128
PADZ = 64
EXPC = math.exp(-0.5)


class Res:
    __slots__ = ("name", "w", "readers", "excl")

    def __init__(self, name="", excl=False):
        self.name = name
        self.w = None
        self.readers = {}
        self.excl = excl


class Sched:
    ENGS = ("pe", "act", "dve", "pool", "sp")

    def __init__(self, nc, n_dma_sems=40):
        self.nc = nc
        self.prog = {e: [] for e in self.ENGS}
        self.sems = {e: nc.alloc_semaphore(f"s_{e}") for e in self.ENGS}
        self.count = {e: 0 for e in self.ENGS}
        self.known = {e: {} for e in self.ENGS}
        self.dma_sems = []
        for i in range(n_dma_sems):
            k = f"d{i}"
            self.sems[k] = nc.alloc_semaphore(f"s_{k}")
            self.dma_sems.append(k)
        self.dma_cnt = {k: 0 for k in self.dma_sems}
        self.dma_rr = 0
        self.out_tokens = []

    def _deps(self, eng, reads, writes):
        deps = {}
        ex = [r for r in reads if r.excl]
        if ex:
            writes = list(writes) + ex

        def add(tok):
            if tok is not None and deps.get(tok[0], 0) < tok[1]:
                deps[tok[0]] = tok[1]

        for r in reads:
            add(r.w)
        for w in writes:
            add(w.w)
            for k, v in w.readers.items():
                add((k, v))
        waits = []
        kn = self.known[eng]
        for k, v in deps.items():
            if k == eng and eng == "pe":
                continue
            if kn.get(k, 0) < v:
                kn[k] = v
                waits.append((k, v))
        return waits

    def _mark(self, tok, reads, writes):
        k, v = tok
        ex = [r for r in reads if r.excl]
        if ex:
            writes = list(writes) + ex
        for r in reads:
            if r.readers.get(k, 0) < v:
                r.readers[k] = v
        for w in writes:
            w.w = tok
            w.readers = {}

    def op(self, eng, fn, reads=(), writes=()):
        waits = self._deps(eng, reads, writes)
        self.count[eng] += 1
        tok = (eng, self.count[eng])
        self.prog[eng].append((waits, fn, (eng, 1)))
        self._mark(tok, reads, writes)
        return tok

    def pe_group(self, fns, reads=(), writes=()):
        waits = self._deps("pe", reads, writes)
        self.count["pe"] += 1
        tok = ("pe", self.count["pe"])
        n = len(fns)
        for i, fn in enumerate(fns):
            self.prog["pe"].append((waits if i == 0 else [], fn, ("pe", 1) if i == n - 1 else None))
        self._mark(tok, reads, writes)
        return tok

    def dma(self, eng, fn, reads=(), writes=(), is_output=False):
        k = self.dma_sems[self.dma_rr % len(self.dma_sems)]
        self.dma_rr += 1
        waits = self._deps(eng, reads, writes)
        prev = 16 * self.dma_cnt[k]
        kn = self.known[eng]
        if prev > 0 and kn.get(k, 0) < prev:
            kn[k] = prev
            waits.append((k, prev))
        self.dma_cnt[k] += 1
        tok = (k, 16 * self.dma_cnt[k])
        self.prog[eng].append((waits, fn, (k, 16)))
        self._mark(tok, reads, writes)
        if is_output:
            self.out_tokens.append(tok)
        return tok

    def barrier(self):
        tg = {e: self.count[e] for e in self.ENGS if self.count[e] > 0}
        tg.update({k: 16 * c for k, c in self.dma_cnt.items() if c > 0})
        for e in self.ENGS:
            waits = []
            kn = self.known[e]
            for k, v in tg.items():
                if k == e:
                    continue
                if kn.get(k, 0) < v:
                    kn[k] = v
                    waits.append((k, v))
            if waits:
                self.prog[e].append((waits, None, None))

    def emit(self):
        nc = self.nc
        fin = {}
        for k, v in self.out_tokens:
            fin[k] = max(fin.get(k, 0), v)
        final_waits = list(fin.items())
        names = {"pe": "tensor", "act": "scalar", "dve": "vector", "pool": "gpsimd", "sp": "sync"}
        with nc.Block() as block:
            for e in self.ENGS:
                prog = self.prog[e]
                if not prog and not (e == "sp" and final_waits):
                    continue

                def body(engine, prog=prog, e=e):
                    for waits, fn, inc in prog:
                        for k, v in waits:
                            engine.wait_ge(self.sems[k], v)
                        if fn is None:
                            continue
                        ins = fn(engine)
                        if inc is not None:
                            ins.then_inc(self.sems[inc[0]], inc[1])
                    if e == "sp":
                        for k, v in final_waits:
                            engine.wait_ge(self.sems[k], v)

                getattr(block, names[e])(body)


class Prog:
    pass


def build_program():
    nc = bass.Bass("TRN2", target_bir_lowering=False)
    S = Sched(nc)
    cur = [None]

    def din(name, shape, dt=F32):
        return nc.dram_tensor(name, list(shape), dt, kind="ExternalInput").ap()

    def dout(name, shape, dt=F32):
        return nc.dram_tensor(name, list(shape), dt, kind="ExternalOutput").ap()

    def dscr(name, shape, dt=F32):
        return nc.dram_tensor(name, list(shape), dt, kind="Internal").ap()

    uid = [0]

    def sb(name, shape, dt=F32):
        uid[0] += 1
        name = f"{name}_{uid[0]}"
        if cur[0] is not None:
            return cur[0].enter_context(nc.sbuf_tensor(name, list(shape), dt))
        return nc.alloc_sbuf_tensor(name, list(shape), dt)

    xT = din("xT", [NSG, 128, 16, TD])
    condT = din("condT", [128, 16, 2])
    h0in = din("h0in", [2, NH, HD, HD])
    w_ada = din("w_ada", [144, 128, 16, 128])
    b_adaT = din("b_adaT", [128, 144])
    lnT = din("lnT", [128, 6, 16])
    w_f1 = din("w_f1", [86, 128, 16, 128])
    w_f1o = din("w_f1o", [16, 128, 43, 128])
    w_f2 = din("w_f2", [86, 128, 16, 128])
    w_f2o = din("w_f2o", [16, 128, 43, 128])
    w_rkv = din("w_rkv", [24, 128, 16, 128])
    w_lora = din("w_lora", [3, 128, 16, 128])
    w_u = din("w_u", [8, 128, 16, 128])
    w_v = din("w_v", [4, 128, 16, 256])
    w_gate = din("w_gate", [32, 128, 16, 128])
    p64 = din("p64", [64, 12, NH])
    NP64 = 12
    w2T = din("w2T", [64, 2, DA])
    a2T = din("a2T", [64, 2, DA])
    g2a = din("g2a", [128, DA])
    g2b = din("g2b", [32, DA])
    mu_l = din("mu_l", [128, 3])
    sgln = din("sgln", [128, 2, DB])
    sgwT = din("sgwT", [128, 8, 128])
    sgb = din("sgb", [128, 8 * 128])
    w_pa = din("w_pa", [16, 64, 16, 128])
    w_pb = din("w_pb", [16, 128, 8, 128])
    w_o = din("w_o", [16, 128, 16, 128])
    masks = din("masks", [128, 2, 3, 128])
    ident_in = din("ident_in", [128, 128])

    yT = dout("yT", [NSG, 128, 16, TD])
    st_out = dout("st_out", [2, 4, NH, HD, HD])

    a0in = din("a0in", [64, 2, NH])
    X1 = dscr("X1", [NSG, 128, 16, TD])
    ZR = [dscr(f"ZR{s}", [48, 64, (256 if s < 4 else 2048) + 2 * PADZ]) for s in range(5)]
    ZLP = [dscr(f"ZLP{s}", [3, 128, (256 if s < 4 else 2048) + 2 * PADZ]) for s in range(5)]
    GTs = dscr("GTs", [NSG, 128, 32, TD], BF16)
    UTs = dscr("UTs", [NSG, 128, 8, TD], BF16)
    VNs = dscr("VNs", [NSG, 4, 128, DB], BF16)
    OA = dscr("OA", [NSG, 64, NH, TD], BF16)
    PREP = dscr("PREP", [NG, NH, 4, 64, TG])
    YF = dscr("YF", [NG, NH, 64, TG])
    RKF = dscr("RKF", [NG, NH, 64, TG])

    ident = sb("ident", [128, 128]); r_const = Res("const")
    ones = sb("ones", [128, 128])
    onesb = sb("onesb", [128, 128], BF16)
    zeros = sb("zeros", [128, 128])
    msk = sb("msk", [128, 2, 3, 128])
    p64s = sb("p64s", [64, NP64, NH])
    lns = sb("lns", [128, 6, 16])
    mus = sb("mus", [128, 3])
    S.dma("sp", lambda e: e.dma_start(out=ident[:], in_=ident_in[:, :]), writes=[r_const])
    S.dma("sp", lambda e: e.dma_start(out=msk[:], in_=masks[:, :, :, :]), writes=[r_const])
    S.dma("sp", lambda e: e.dma_start(out=p64s[:], in_=p64[:, :, :]), writes=[r_const])
    S.dma("sp", lambda e: e.dma_start(out=lns[:], in_=lnT[:, :, :]), writes=[r_const])
    S.dma("sp", lambda e: e.dma_start(out=mus[:], in_=mu_l[:, :]), writes=[r_const])
    S.op("dve", lambda e: e.memset(ones[:], 1.0), writes=[r_const])
    S.op("dve", lambda e: e.memset(onesb[:], 1.0), writes=[r_const])
    S.op("dve", lambda e: e.memset(zeros[:], 0.0), writes=[r_const])

    r_zpad = Res("zpad")
    for s in range(5):
        L = 256 if s < 4 else 2048
        for side in (0, 1):
            off = 0 if side == 0 else PADZ + L
            for q in range(48):
                S.dma("sp", lambda e, s=s, q=q, off=off: e.dma_start(out=ZR[s][q, :, off:off + PADZ], in_=zeros[0:64, 0:PADZ]),
                      reads=[r_const], writes=[r_zpad])
            for q in range(3):
                S.dma("sp", lambda e, s=s, q=q, off=off: e.dma_start(out=ZLP[s][q, :, off:off + PADZ], in_=zeros[:, 0:PADZ]),
                      reads=[r_const], writes=[r_zpad])

    pd = [nc.alloc_psum_tensor(f"pd{i}", [128, 512], F32) for i in range(4)]
    r_pd = [Res(f"pd{i}", excl=True) for i in range(4)]
    pr = [nc.alloc_psum_tensor(f"pr{i}", [128, 512], F32) for i in range(4)]
    pdi = [0]

    def next_pd():
        i = pdi[0] % 4
        pdi[0] += 1
        return pd[i], r_pd[i]

    mod = sb("mod", [128, 144, 2]); r_modS = [Res(f"mod{i}") for i in range(3)]
    modd = sb("modd", [128, 9, 16, 2])
    a0s = sb("a0s", [64, 2, NH])
    S.dma("sp", lambda e: e.dma_start(out=a0s[:], in_=a0in[:, :, :]), writes=[r_const])

    def mcol(i, which, kc, cnd):
        return modd[:, 3 * i + which, kc, cnd:cnd + 1]

    def sg_cnd(sg):
        return 0 if sg < 2 else 1

    def sg_segs(sg):
        if sg < 2:
            return [(2 * sg, 0, 256, 0), (2 * sg + 1, 0, 256, 256)]
        return [(4, (sg - 2) * TD, TD, 0)]

    def g_seg(g):
        if g < NPG:
            return (g, 0)
        return (4, (g - NPG) * TG)

    def dense_env():
        NSLOT = 4
        wslot = [sb(f"wslot{i}", [128, 43 * 128], BF16) for i in range(NSLOT)]
        r_wslot = [Res(f"wslot{i}") for i in range(NSLOT)]
        wsi = [0]
        xa_t = sb("xa_t", [128, 16, TD]); r_xa = Res("xa")
        h_t = sb("h_t", [128, 16, TD], BF16); r_h = Res("h")
        GBIG = sb("GBIG", [128, 43 * TD], BF16); r_G = Res("G")
        G_t = GBIG[:].rearrange("p (k t) -> p k t", t=TD)
        sq_t = [sb(f"sq_t{i}", [128, TD]) for i in range(2)]; r_sq = [Res(f"sq{i}") for i in range(2)]
        sa_t = sb("sa_t", [128, TD]); r_sa = Res("sa")
        st_mean = sb("st_mean", [128, TD]); st_rstd = sb("st_rstd", [128, TD]); st_tmp = sb("st_tmp", [128, TD]); r_st = Res("st")

        def load_w(src_ap, kp, kc, cols):
            i = wsi[0] % NSLOT
            wsi[0] += 1
            view = wslot[i][0:kp, 0:kc * cols].rearrange("p (k c) -> p k c", c=cols)
            S.dma("pool", lambda e: e.dma_start(out=view, in_=src_ap), writes=[r_wslot[i]])
            return view, r_wslot[i]

        def modulate(i, cnd):
            for kc in range(16):
                S.op("act", lambda e, kc=kc: e.activation(out=h_t[:, kc, :], in_=xa_t[:, kc, :], func=AF.Identity,
                                                          scale=mcol(i, 1, kc, cnd), bias=mcol(i, 0, kc, cnd)),
                     reads=[r_xa, r_modS[i]], writes=[r_h])

        def layernorm(li):
            x, rx = xa_t, r_xa
            p1, rp1 = next_pd()
            p2, rp2 = next_pd()
            S.pe_group([(lambda e, kc=kc: e.matmul(p1[:, :], lhsT=ones[:, :], rhs=x[:, kc, :], start=(kc == 0), stop=(kc == 15))) for kc in range(16)],
                       reads=[rx, r_const], writes=[rp1])
            for kc in range(16):
                i = kc % 2
                S.op("act", lambda e, kc=kc, i=i: e.activation(out=sq_t[i][:], in_=x[:, kc, :], func=AF.Square), reads=[rx], writes=[r_sq[i]])
                S.pe_group([lambda e, kc=kc, i=i: e.matmul(p2[:, :], lhsT=ones[:, :], rhs=sq_t[i][:], start=(kc == 0), stop=(kc == 15))],
                           reads=[r_sq[i], r_const], writes=[rp2])
            S.op("dve", lambda e: e.tensor_scalar(out=st_mean[:], in0=p1[:, :], scalar1=1.0 / D, scalar2=None, op0=ALU.mult), reads=[rp1], writes=[r_st])
            S.op("dve", lambda e: e.tensor_tensor(out=st_tmp[:], in0=st_mean[:], in1=st_mean[:], op=ALU.mult), reads=[r_st], writes=[r_st])
            S.op("dve", lambda e: e.scalar_tensor_tensor(out=st_tmp[:], in0=p2[:, :], scalar=1.0 / D, in1=st_tmp[:], op0=ALU.mult, op1=ALU.subtract),
                 reads=[rp2, r_st], writes=[r_st])
            S.op("dve", lambda e: e.tensor_scalar(out=st_tmp[:], in0=st_tmp[:], scalar1=LN_EPS, scalar2=None, op0=ALU.add), reads=[r_st], writes=[r_st])
            S.op("act", lambda e: e.activation(out=st_tmp[:], in_=st_tmp[:], func=AF.Sqrt), reads=[r_st], writes=[r_st])
            S.op("dve", lambda e: e.reciprocal(out=st_rstd[:], in_=st_tmp[:]), reads=[r_st], writes=[r_st])
            for kc in range(16):
                S.op("dve", lambda e, kc=kc: e.tensor_tensor(out=x[:, kc, :], in0=x[:, kc, :], in1=st_mean[:], op=ALU.subtract), reads=[rx, r_st], writes=[rx])
                S.op("dve", lambda e, kc=kc: e.tensor_tensor(out=x[:, kc, :], in0=x[:, kc, :], in1=st_rstd[:], op=ALU.mult), reads=[r_st, rx], writes=[rx])
                S.op("act", lambda e, kc=kc: e.activation(out=x[:, kc, :], in_=x[:, kc, :], func=AF.Identity,
                                                          scale=lns[:, li, kc:kc + 1], bias=lns[:, 3 + li, kc:kc + 1]), reads=[rx, r_const], writes=[rx])

        def ffn(w_in_d, w_out_d, stage, cnd):
            x, rx = xa_t, r_xa
            for j in range(43):
                wa, rwa = load_w(w_in_d[2 * j, :, :, :], 128, 16, 128)
                wb, rwb = load_w(w_in_d[2 * j + 1, :, :, :], 128, 16, 128)
                pa, rpa = next_pd()
                pb, rpb = next_pd()
                S.pe_group([(lambda e, k=k, wa=wa, pa=pa: e.matmul(pa[:, :], lhsT=wa[:, k, :], rhs=h_t[:, k, :], start=(k == 0), stop=(k == 15))) for k in range(16)],
                           reads=[rwa, r_h], writes=[rpa])
                S.pe_group([(lambda e, k=k, wb=wb, pb=pb: e.matmul(pb[:, :], lhsT=wb[:, k, :], rhs=h_t[:, k, :], start=(k == 0), stop=(k == 15))) for k in range(16)],
                           reads=[rwb, r_h], writes=[rpb])
                S.op("act", lambda e, pa=pa: e.activation(out=sa_t[:], in_=pa[:, :], func=AF.Silu), reads=[rpa], writes=[r_sa])
                S.op("dve", lambda e, pb=pb, j=j: e.tensor_tensor(out=G_t[:, j, :], in0=sa_t[:], in1=pb[:, :], op=ALU.mult), reads=[r_sa, rpb], writes=[r_G])
            for dc in range(16):
                wo, rwo = load_w(w_out_d[dc, :, :, :], 128, 43, 128)
                po, rpo = next_pd()
                S.pe_group([(lambda e, k=k, wo=wo, po=po: e.matmul(po[:, :], lhsT=wo[:, k, :], rhs=G_t[:, k, :], start=(k == 0), stop=(k == 42))) for k in range(43)],
                           reads=[rwo, r_G], writes=[rpo])
                S.op("act", lambda e, dc=dc: e.activation(out=x[:, dc, :], in_=x[:, dc, :], func=AF.Identity, scale=ALPHA), reads=[rx], writes=[rx])
                S.op("dve", lambda e, dc=dc, po=po: e.scalar_tensor_tensor(out=x[:, dc, :], in0=po[:, :], scalar=mcol(stage, 2, dc, cnd), in1=x[:, dc, :],
                                                                          op0=ALU.mult, op1=ALU.add), reads=[rpo, rx, r_modS[stage]], writes=[rx])

        def proj_fm(w_d, q, ncols, epi):
            wv, rw = load_w(w_d[q, :, :, :], 128, 16, ncols)
            pt, rp = next_pd()
            S.pe_group([(lambda e, k=k, wv=wv, pt=pt: e.matmul(pt[0:ncols, :], lhsT=wv[:, k, :], rhs=h_t[:, k, :], start=(k == 0), stop=(k == 15))) for k in range(16)],
                       reads=[rw, r_h], writes=[rp])
            epi(pt, rp)

        return SimpleNamespace(**locals())

    def rwkv_phase():
        KS = 4
        TP = TG + 2 * PADZ
        NCH = TG // C
        RW = F32
        lin_big = sb("lin_big", [128, 3, TP]); r_lin = Res("lin")
        lsh = sb("lsh", [128, 3, TG]); r_lsh = Res("lsh")
        nbl = sb("nbl", [128, TG]); r_nbl = Res("nbl")
        xaf = sb("xaf", [128, TG], BF16); r_xaf = Res("xaf")
        txwL = [sb("txw", [64, TG], BF16) for _ in range(2)]; xatL = [sb("xat", [64, TG], BF16) for _ in range(2)]; r_txL = [Res("tx0"), Res("tx1")]
        sxgL = [sb("sxg", [128, 2, TG], BF16) for _ in range(2)]; r_sxgL = [Res("sxg0"), Res("sxg1")]
        w2s = sb("w2s", [64, 2, DA], BF16); a2s = sb("a2s", [64, 2, DA], BF16); g2as = sb("g2as", [128, DA], BF16); g2bs = sb("g2bs", [32, DA], BF16); r_lw = Res("lw")
        S.dma("pool", lambda e: e.dma_start(out=w2s[:], in_=w2T[:, :, :]), writes=[r_lw])
        S.dma("pool", lambda e: e.dma_start(out=a2s[:], in_=a2T[:, :, :]), writes=[r_lw])
        S.dma("pool", lambda e: e.dma_start(out=g2as[:], in_=g2a[:, :]), writes=[r_lw])
        S.dma("pool", lambda e: e.dma_start(out=g2bs[:], in_=g2b[:, :]), writes=[r_lw])
        Hs = sb("Hs", [64, NH, 64]); r_H = [Res(f"H{h}") for h in range(NH)]
        r_prep = [[[Res() for i in range(4)] for h in range(NH)] for g in range(NG)]
        r_yf = [[[Res() for i in range(2)] for h in range(NH)] for g in range(NG)]
        P_MUR, P_KK, P_KA, P_1MKA, P_RK, P_GNG, P_GNB = 0, 3, 4, 5, 6, 7, 8

        def pcol(idx, h):
            return p64s[:, idx, h:h + 1]

        def R(ap):
            return ap

        def RA(ap):
            return ap.bitcast(F32R)

        banks = [(pr[0], pr[1]), (pr[2], pr[3]), (pd[0], pd[1]), (pd[2], pd[3])]
        msk4 = sb("msk4", [128, 2, 4, 128]); msk2 = sb("msk2", [128, 2, 2, 128])
        for dd in range(2):
            for a_ in range(4):
                S.op("dve", lambda e, dd=dd, a_=a_: e.tensor_copy(out=msk4[:, dd, a_, :], in_=msk[:, dd, a_ % 2, :]), reads=[r_const], writes=[r_const])
            for a_ in range(2):
                S.op("dve", lambda e, dd=dd, a_=a_: e.tensor_copy(out=msk2[:, dd, a_, :], in_=msk[:, dd, 2, :]), reads=[r_const], writes=[r_const])

        def shift_tile(src, dst, is_grid, n_part, mu_ap, nb, rsrc, rdst, rnb):
            c0 = PADZ
            n = TG
            if is_grid:
                yield S.op("dve", lambda e: e.tensor_tensor(out=nb[0:n_part, 0:n], in0=src[0:n_part, c0 - 64:c0 - 64 + n], in1=src[0:n_part, c0 + 64:c0 + 64 + n], op=ALU.add),
                           reads=[rsrc], writes=[rnb])
                nb3 = nb[0:n_part, 0:n].rearrange("p (r c) -> p r c", c=64)
                s3 = src[0:n_part, c0:c0 + n].rearrange("p (r c) -> p r c", c=64)
                yield S.op("dve", lambda e: e.tensor_tensor(out=nb3[:, :, 1:64], in0=nb3[:, :, 1:64], in1=s3[:, :, 0:63], op=ALU.add), reads=[rsrc, rnb], writes=[rnb])
                yield S.op("dve", lambda e: e.tensor_tensor(out=nb3[:, :, 0:63], in0=nb3[:, :, 0:63], in1=s3[:, :, 1:64], op=ALU.add), reads=[rsrc, rnb], writes=[rnb])
                sc = 0.25
            else:
                yield S.op("dve", lambda e: e.tensor_tensor(out=nb[0:n_part, 0:n], in0=src[0:n_part, c0 - 1:c0 - 1 + n], in1=src[0:n_part, c0 + 1:c0 + 1 + n], op=ALU.add),
                           reads=[rsrc], writes=[rnb])
                sc = 0.5
            yield S.op("dve", lambda e: e.scalar_tensor_tensor(out=nb[0:n_part, 0:n], in0=nb[0:n_part, 0:n], scalar=sc, in1=src[0:n_part, c0:c0 + n], op0=ALU.mult, op1=ALU.subtract),
                       reads=[rsrc, rnb], writes=[rnb])
            yield S.op("dve", lambda e: e.scalar_tensor_tensor(out=dst[0:n_part, 0:n], in0=nb[0:n_part, 0:n], scalar=mu_ap, in1=src[0:n_part, c0:c0 + n], op0=ALU.mult, op1=ALU.add),
                       reads=[rsrc, rnb, r_const], writes=[rdst])

        def make_stream(sid):
            bk0, bk1 = banks[sid]
            rb0, rb1 = Res(f"bk0_{sid}", excl=True), Res(f"bk1_{sid}", excl=True)
            zin_big = [sb(f"zinb{i}", [64, TP]) for i in range(3)]; r_zin = [Res() for i in range(3)]
            zsh = [sb(f"zsh{i}", [64, TG]) for i in range(3)]; r_zsh = [Res() for i in range(3)]
            kkt = sb("kkt", [64, TG]); r_kk = Res()
            wdt = sb("wdt", [64, TG]); r_wd = Res()
            alr = sb("alr", [64, TG]); r_alr = Res()
            kdt = sb("kdt", [64, TG]); r_kd = Res()
            pin = sb("pin", [64, TG]); pex = sb("pex", [64, TG]); r_p = Res()
            sinc = sb("sinc", [64, TG]); sexc = sb("sexc", [64, TG]); rinc = sb("rinc", [64, TG]); r_s = r_p
            tmp64 = sb("tmp64", [64, TG]); r_tmp = Res()
            nbt, r_nbt = tmp64, r_tmp
            arT = sb("arT", [64, NCH, 2, C]); r_ar = Res()
            btT = sb("btT", [64, TG]); ktT = sb("ktT", [64, TG]); r_bk = Res()
            yacc = sb("yacc", [64, TG]); r_yacc = Res()
            oA = sb("oA", [64, TG], BF16); r_oA = Res()
            rkd = sb("rkd", [64, TG]); r_rkd = Res()
            yf_in, rk_in, r_yfin = sinc, sexc, r_s
            gn1, gn2, r_gn = pex, rinc, r_s
            ALLA = sb("ALLA", [128, NCH, 2, 128], RW); r_ALLA = Res()
            ALLB = sb("ALLB", [128, NCH, 2, 128], RW); r_ALLB = Res()
            LL = [sb(f"LL{i}", [128, 2, NCH, 128], RW) for i in range(2)]; r_LL = [Res() for i in range(2)]
            Xb = [sb(f"X{i}", [128, NCH, 128], RW) for i in range(2)]; r_X = [Res() for i in range(2)]
            Xfin = sb("Xfin", [128, NCH, 128], RW); r_Xfin = Res()
            tokm = sb("tokm", [128, NCH, 4, 64], RW); r_tokm = Res()
            MN = sb("MN", [64, NCH, 2, 64]); r_MN = Res()
            QT = sb("QT", [64, NCH, 128]); r_QT = Res()
            Ht = sb("Ht", [64, 64]); r_Ht = Res()
            arv = arT[:].rearrange("p c two t -> p c (two t)")
            rr, kr, vr = zsh

            def head_final(g, h, lb):
                sxg, r_sxg = sxgL[lb], r_sxgL[lb]
                yield S.dma("sp", lambda e: e.dma_start(out=yf_in[:], in_=YF[g, h, :, :]), reads=[r_yf[g][h][0]], writes=[r_yfin])
                yield S.dma("sp", lambda e: e.dma_start(out=rk_in[:], in_=RKF[g, h, :, :]), reads=[r_yf[g][h][1]], writes=[r_yfin])
                yield S.op("dve", lambda e: e.tensor_tensor(out=yacc[:], in0=yacc[:], in1=yf_in[:], op=ALU.add), reads=[r_yfin, r_yacc], writes=[r_yacc])
                yield S.op("dve", lambda e: e.tensor_tensor(out=rkd[:], in0=rkd[:], in1=rk_in[:], op=ALU.add), reads=[r_yfin, r_rkd], writes=[r_rkd])
                p1, p2 = bk0[0:64, 0:TG], bk0[0:64, TG:2 * TG]
                yield S.pe_group([lambda e: e.matmul(p1, lhsT=ones[0:64, 0:64], rhs=yacc[:], start=True, stop=True)], reads=[r_yacc, r_const], writes=[rb0])
                yield S.op("act", lambda e: e.activation(out=gn1[:], in_=yacc[:], func=AF.Square), reads=[r_yacc], writes=[r_gn])
                yield S.pe_group([lambda e: e.matmul(p2, lhsT=ones[0:64, 0:64], rhs=gn1[:], start=True, stop=True)], reads=[r_gn, r_const], writes=[rb0])
                yield S.op("dve", lambda e: e.tensor_scalar(out=gn1[:], in0=p1, scalar1=1.0 / 64, scalar2=None, op0=ALU.mult), reads=[rb0], writes=[r_gn])
                yield S.op("dve", lambda e: e.tensor_tensor(out=gn2[:], in0=gn1[:], in1=gn1[:], op=ALU.mult), reads=[r_gn], writes=[r_gn])
                yield S.op("dve", lambda e: e.scalar_tensor_tensor(out=gn2[:], in0=p2, scalar=1.0 / 64, in1=gn2[:], op0=ALU.mult, op1=ALU.subtract), reads=[rb0, r_gn], writes=[r_gn])
                yield S.op("dve", lambda e: e.tensor_scalar(out=gn2[:], in0=gn2[:], scalar1=GN_EPS, scalar2=None, op0=ALU.add), reads=[r_gn], writes=[r_gn])
                yield S.op("act", lambda e: e.activation(out=gn2[:], in_=gn2[:], func=AF.Sqrt), reads=[r_gn], writes=[r_gn])
                yield S.op("dve", lambda e: e.reciprocal(out=gn2[:], in_=gn2[:]), reads=[r_gn], writes=[r_gn])
                yield S.op("dve", lambda e: e.tensor_tensor(out=yacc[:], in0=yacc[:], in1=gn1[:], op=ALU.subtract), reads=[r_gn, r_yacc], writes=[r_yacc])
                yield S.op("dve", lambda e: e.tensor_tensor(out=yacc[:], in0=yacc[:], in1=gn2[:], op=ALU.mult), reads=[r_gn, r_yacc], writes=[r_yacc])
                yield S.op("act", lambda e: e.activation(out=yacc[:], in_=yacc[:], func=AF.Identity, scale=pcol(P_GNG, h), bias=pcol(P_GNB, h)), reads=[r_yacc, r_const], writes=[r_yacc])
                p3, p4 = bk1[0:64, 0:TG], bk1[0:64, TG:2 * TG]
                yield S.pe_group([lambda e: e.matmul(p3, lhsT=ones[0:64, 0:64], rhs=rkd[:], start=True, stop=True)], reads=[r_rkd, r_const], writes=[rb1])
                yield S.op("dve", lambda e: e.tensor_tensor(out=gn1[:], in0=p3, in1=vr[:], op=ALU.mult), reads=[rb1, r_zsh[2]], writes=[r_gn])
                yield S.op("dve", lambda e: e.tensor_tensor(out=yacc[:], in0=yacc[:], in1=gn1[:], op=ALU.add), reads=[r_gn, r_yacc], writes=[r_yacc])
                yield S.pe_group([lambda e: e.matmul(p4, lhsT=g2as[:, h * 64:(h + 1) * 64], rhs=sxg[:, 0, :], start=True, stop=False),
                                  lambda e: e.matmul(p4, lhsT=g2bs[:, h * 64:(h + 1) * 64], rhs=sxg[0:32, 1, :], start=False, stop=True)],
                                 reads=[r_sxg, r_lw], writes=[rb1])
                yield S.op("dve", lambda e: e.tensor_tensor(out=oA[:], in0=yacc[:], in1=p4, op=ALU.mult), reads=[rb1, r_yacc], writes=[r_oA])
                yield S.dma("sp", lambda e: e.dma_start(out=OA[g // 2, :, h, (g % 2) * TG:(g % 2 + 1) * TG], in_=oA[:]), reads=[r_oA])

            def head_body(g, d, final, h, lb):
                is_grid = g >= NPG
                sq, st = g_seg(g)
                chunks = list(range(NCH))
                txw, xat, r_tx = txwL[lb], xatL[lb], r_txL[lb]
                if g < NPG:
                    yield S.op("dve", lambda e: e.memset(Hs[:, h, :], 0.0), writes=[r_H[h]])
                elif (d == 0 and g == NPG) or (d == 1 and g == NG - 1):
                    yield S.dma("sp", lambda e: e.dma_start(out=Hs[:, h, :], in_=h0in[d, h, :, :]), writes=[r_H[h]])
                pa_, pb_, pc_ = bk0[0:64, 0:TG], bk0[0:64, TG:2 * TG], bk1[0:64, 0:TG]
                if d == 0:
                    for i in range(3):
                        yield S.dma("sp", lambda e, i=i: e.dma_start(out=zin_big[i][:], in_=ZR[sq][i * NH + h, :, st:st + TP]), reads=[r_zpad], writes=[r_zin[i]])
                    for i in range(3):
                        yield from shift_tile(zin_big[i], zsh[i], is_grid, 64, pcol(P_MUR + i, h), nbt, r_zin[i], r_zsh[i], r_nbt)
                        yield S.dma("sp", lambda e, i=i: e.dma_start(out=PREP[g, h, i, :, :], in_=zsh[i][:]), reads=[r_zsh[i]], writes=[r_prep[g][h][i]])
                    yield S.op("act", lambda e: e.activation(out=kkt[:], in_=kr[:], func=AF.Identity, scale=pcol(P_KK, h)), reads=[r_zsh[1], r_const], writes=[r_kk])
                    yield S.op("act", lambda e: e.activation(out=tmp64[:], in_=kkt[:], func=AF.Square), reads=[r_kk], writes=[r_tmp])
                    yield S.pe_group([lambda e: e.matmul(pa_, lhsT=ones[0:64, 0:64], rhs=tmp64[:], start=True, stop=True)], reads=[r_tmp, r_const], writes=[rb0])
                    yield S.op("act", lambda e: e.activation(out=tmp64[:], in_=pa_, func=AF.Sqrt), reads=[rb0], writes=[r_tmp])
                    yield S.op("dve", lambda e: e.tensor_scalar(out=tmp64[:], in0=tmp64[:], scalar1=1e-12, scalar2=None, op0=ALU.max), reads=[r_tmp], writes=[r_tmp])
                    yield S.op("dve", lambda e: e.reciprocal(out=tmp64[:], in_=tmp64[:]), reads=[r_tmp], writes=[r_tmp])
                    yield S.op("dve", lambda e: e.tensor_tensor(out=kkt[:], in0=kkt[:], in1=tmp64[:], op=ALU.mult), reads=[r_tmp, r_kk], writes=[r_kk])
                    yield S.dma("sp", lambda e: e.dma_start(out=PREP[g, h, 3, :, :], in_=kkt[:]), reads=[r_kk], writes=[r_prep[g][h][3]])
                else:
                    for i in range(3):
                        yield S.dma("sp", lambda e, i=i: e.dma_start(out=zsh[i][:], in_=PREP[g, h, i, :, :]), reads=[r_prep[g][h][i]], writes=[r_zsh[i]])
                    yield S.dma("sp", lambda e: e.dma_start(out=kkt[:], in_=PREP[g, h, 3, :, :]), reads=[r_prep[g][h][3]], writes=[r_kk])
                yield S.pe_group([lambda e: e.matmul(pb_, lhsT=w2s[:, d, h * 64:(h + 1) * 64], rhs=txw[:], start=True, stop=True)], reads=[r_tx, r_lw], writes=[rb0])
                yield S.op("act", lambda e: e.activation(out=wdt[:], in_=pb_, func=AF.Sigmoid, bias=pcol(9 + d, h)), reads=[rb0, r_const], writes=[r_wd])
                yield S.op("act", lambda e: e.activation(out=wdt[:], in_=wdt[:], func=AF.Exp, scale=-EXPC), reads=[r_wd], writes=[r_wd])
                yield S.pe_group([lambda e: e.matmul(pc_, lhsT=a2s[:, d, h * 64:(h + 1) * 64], rhs=xat[:], start=True, stop=True)], reads=[r_tx, r_lw], writes=[rb1])
                yield S.op("act", lambda e: e.activation(out=alr[:], in_=pc_, func=AF.Sigmoid, bias=a0s[:, d, h:h + 1]), reads=[rb1, r_const], writes=[r_alr])
                yield S.op("act", lambda e: e.activation(out=kdt[:], in_=alr[:], func=AF.Identity, scale=pcol(P_KA, h), bias=pcol(P_1MKA, h)),
                           reads=[r_alr, r_const], writes=[r_kd])
                yield S.op("dve", lambda e: e.tensor_tensor(out=kdt[:], in0=kdt[:], in1=kr[:], op=ALU.mult), reads=[r_kd, r_zsh[1]], writes=[r_kd])
                yield S.op("dve", lambda e: e.scalar_tensor_tensor(out=rkd[:], in0=kdt[:], scalar=pcol(P_RK, h), in1=rr[:], op0=ALU.mult, op1=ALU.mult),
                           reads=[r_kd, r_zsh[0], r_const], writes=[r_rkd])
                for c in chunks:
                    yield S.op("dve", lambda e, c=c: e.tensor_tensor_scan(out=pin[:, c * C:(c + 1) * C], data0=wdt[:, c * C:(c + 1) * C], data1=zeros[0:64, 0:C], initial=1.0,
                                                                          op0=ALU.mult, op1=ALU.add), reads=[r_wd, r_const], writes=[r_p])
                yield S.op("dve", lambda e: e.reciprocal(out=tmp64[:], in_=wdt[:]), reads=[r_wd], writes=[r_tmp])
                yield S.op("dve", lambda e: e.tensor_tensor(out=pex[:], in0=pin[:], in1=tmp64[:], op=ALU.mult), reads=[r_p, r_tmp], writes=[r_p])
                if d == 0:
                    sincv, sexcv = pin, pex
                else:
                    sincv, sexcv = sinc, sexc
                    yield S.op("dve", lambda e: e.reciprocal(out=sinc[:], in_=pex[:]), reads=[r_p], writes=[r_s])
                    yield S.op("dve", lambda e: e.reciprocal(out=sexc[:], in_=pin[:]), reads=[r_p], writes=[r_s])
                    for c in chunks:
                        yield S.op("act", lambda e, c=c: e.activation(out=sinc[:, c * C:(c + 1) * C], in_=sinc[:, c * C:(c + 1) * C], func=AF.Identity,
                                                                      scale=pin[:, (c + 1) * C - 1:(c + 1) * C]), reads=[r_p, r_s], writes=[r_s])
                        yield S.op("act", lambda e, c=c: e.activation(out=sexc[:, c * C:(c + 1) * C], in_=sexc[:, c * C:(c + 1) * C], func=AF.Identity,
                                                                      scale=pin[:, (c + 1) * C - 1:(c + 1) * C]), reads=[r_p, r_s], writes=[r_s])
                yield S.op("dve", lambda e: e.reciprocal(out=rinc[:], in_=sincv[:]), reads=[r_s], writes=[r_s])
                for c in chunks:
                    yield S.op("dve", lambda e, c=c: e.scalar_tensor_tensor(out=RA(arT[:, c, 0, :]), in0=kkt[:, c * C:(c + 1) * C], scalar=-1.0, in1=sexcv[:, c * C:(c + 1) * C],
                                                                            op0=ALU.mult, op1=ALU.mult), reads=[r_kk, r_s], writes=[r_ar])
                    yield S.op("dve", lambda e, c=c: e.tensor_tensor(out=RA(arT[:, c, 1, :]), in0=rr[:, c * C:(c + 1) * C], in1=sincv[:, c * C:(c + 1) * C], op=ALU.mult),
                               reads=[r_zsh[0], r_s], writes=[r_ar])
                yield S.op("dve", lambda e: e.tensor_tensor(out=tmp64[:], in0=kkt[:], in1=alr[:], op=ALU.mult), reads=[r_kk, r_alr], writes=[r_tmp])
                yield S.op("dve", lambda e: e.tensor_tensor(out=RA(btT[:]), in0=tmp64[:], in1=rinc[:], op=ALU.mult), reads=[r_s, r_tmp], writes=[r_bk])
                yield S.op("dve", lambda e: e.tensor_tensor(out=RA(ktT[:]), in0=kdt[:], in1=rinc[:], op=ALU.mult), reads=[r_s, r_kd], writes=[r_bk])
                corder = chunks if d == 0 else chunks[::-1]
                Hv = Hs[:, h, :]
                rH = r_H[h]
                m4 = msk4[:, d, :, :].rearrange("p a t -> p (a t)")
                fns = []
                for c in chunks:
                    cs = slice(c * C, (c + 1) * C)
                    for i, src in enumerate([arT[:, c, 0, :], btT[:, cs], ktT[:, cs], vr[:, cs]]):
                        fns.append(lambda e, c=c, i=i, src=src: e.transpose(out=bk1[:, c * 256 + i * 64:c * 256 + (i + 1) * 64], in_=src, identity=ident[0:64, 0:64]))
                yield S.pe_group(fns, reads=[r_ar, r_bk, r_zsh[2], r_const], writes=[rb1])
                yield S.op("act", lambda e: e.activation(out=RA(tokm[:].rearrange("p c a k -> p (c a k)")), in_=bk1[:, :], func=AF.Copy), reads=[rb1], writes=[r_tokm])
                yield S.pe_group([(lambda e, c=c: e.matmul(bk0[:, c * 256:(c + 1) * 256], lhsT=RA(btT[:, c * C:(c + 1) * C]), rhs=RA(arv[:, c, :]), start=True, stop=True)) for c in chunks],
                                 reads=[r_bk, r_ar], writes=[rb0])
                yield S.op("dve", lambda e: e.tensor_tensor(out=RA(ALLA[:].rearrange("p c a t -> p (c a t)")), in0=bk0[:, :], in1=m4, op=ALU.mult), reads=[rb0, r_const], writes=[r_ALLA])
                yield S.pe_group([(lambda e, c=c: e.matmul(bk1[:, c * 256:(c + 1) * 256], lhsT=RA(ktT[:, c * C:(c + 1) * C]), rhs=RA(arv[:, c, :]), start=True, stop=True)) for c in chunks],
                                 reads=[r_bk, r_ar], writes=[rb1])
                yield S.op("dve", lambda e: e.tensor_tensor(out=RA(ALLB[:].rearrange("p c a t -> p (c a t)")), in0=bk1[:, :], in1=m4, op=ALU.mult), reads=[rb1, r_const], writes=[r_ALLB])
                yield S.pe_group([(lambda e, c=c: e.matmul(bk0[:, c * 128:(c + 1) * 128], lhsT=RA(arT[:, c, 0, :]), rhs=RA(btT[:, c * C:(c + 1) * C]), start=True, stop=True)) for c in chunks],
                                 reads=[r_bk, r_ar], writes=[rb0])
                yield S.op("dve", lambda e: e.tensor_tensor(out=R(LL[0][:, 1, :, :].rearrange("p c t -> p (c t)")), in0=bk0[:, 0:256], in1=msk2[:, d, :, :].rearrange("p a t -> p (a t)"), op=ALU.mult),
                           reads=[rb0, r_const], writes=[r_LL[0]])
                yield S.pe_group([(lambda e, c=c: e.matmul(bk1[:, c * 64:(c + 1) * 64], lhsT=RA(ALLB[:, c, 0, :]), rhs=RA(tokm[:, c, 3, :]), start=True, stop=True)) for c in chunks],
                                 reads=[r_ALLB, r_tokm], writes=[rb1])
                yield S.op("act", lambda e: e.activation(out=R(Xb[0][:, :, 64:128]), in_=bk1[:, 0:128].rearrange("p (c k) -> p c k", k=64), func=AF.Copy), reads=[rb1], writes=[r_X[0]])
                yield S.op("act", lambda e: e.activation(out=R(Xb[0][:, :, 0:64]), in_=tokm[:, :, 0, :], func=AF.Copy), reads=[r_tokm], writes=[r_X[0]])
                for j in range(7):
                    i0, i1 = j % 2, (j + 1) % 2

                    def LtA(c, j=j, i0=i0):
                        return ALLA[:, c, 0, :] if j == 0 else LL[i0][:, 0, c, :]

                    def LA(c, i0=i0):
                        return LL[i0][:, 1, c, :]
                    rLt = [r_ALLA, r_LL[0]] if j == 0 else [r_LL[i0]]
                    yield S.pe_group([(lambda e, c=c, LtA=LtA, i0=i0: e.matmul(bk0[:, c * 128:(c + 1) * 128], lhsT=R(LtA(c)), rhs=R(Xb[i0][:, c, :]), start=True, stop=True)) for c in chunks],
                                     reads=rLt + [r_X[i0]], writes=[rb0])
                    if j < 6:
                        fns = []
                        for c in chunks:
                            fns.append(lambda e, c=c, LtA=LtA, LA=LA: e.matmul(bk1[:, c * 128:(c + 1) * 128], lhsT=R(LA(c)), rhs=R(LtA(c)), start=True, stop=True))
                        for c in chunks:
                            fns.append(lambda e, c=c, LtA=LtA, LA=LA: e.matmul(bk1[:, 256 + c * 128:256 + (c + 1) * 128], lhsT=R(LtA(c)), rhs=R(LA(c)), start=True, stop=True))
                        yield S.pe_group(fns, reads=rLt + [r_LL[i0]], writes=[rb1])
                    if j < 6:
                        yield S.op("dve", lambda e, i0=i0, i1=i1: e.tensor_tensor(out=R(Xb[i1][:].rearrange("p c t -> p (c t)")), in0=bk0[:, 0:256], in1=Xb[i0][:].rearrange("p c t -> p (c t)"), op=ALU.add),
                                   reads=[rb0, r_X[i0]], writes=[r_X[i1]])
                    else:
                        yield S.op("dve", lambda e, i0=i0: e.tensor_tensor(out=RA(Xfin[:].rearrange("p c t -> p (c t)")), in0=bk0[:, 0:256], in1=Xb[i0][:].rearrange("p c t -> p (c t)"), op=ALU.add),
                                   reads=[rb0, r_X[i0]], writes=[r_Xfin])
                    if j < 6:
                        yield S.op("act", lambda e, i1=i1: e.activation(out=R(LL[i1][:].rearrange("p a c t -> p (a c t)")), in_=bk1[:, :], func=AF.Copy), reads=[rb1], writes=[r_LL[i1]])
                Xf = Xfin; rXf = r_Xfin
                fns = []
                for c in chunks:
                    fns.append(lambda e, c=c: e.matmul(bk0[0:64, c * 128:c * 128 + 64], lhsT=RA(Xf[:, c, 0:64]), rhs=RA(tokm[:, c, 1, :]), start=True, stop=True))
                    fns.append(lambda e, c=c: e.matmul(bk0[0:64, c * 128 + 64:c * 128 + 128], lhsT=RA(tokm[:, c, 1, :]), rhs=RA(Xf[:, c, 64:128]), start=True, stop=False))
                    fns.append(lambda e, c=c: e.matmul(bk0[0:64, c * 128 + 64:c * 128 + 128], lhsT=RA(tokm[:, c, 2, :]), rhs=RA(tokm[:, c, 3, :]), start=False, stop=True))
                    fns.append(lambda e, c=c: e.matmul(bk0[0:64, 256 + c * 128:256 + (c + 1) * 128], lhsT=RA(Xf[:, c, 0:64]), rhs=RA(ALLA[:, c, 1, :]), start=True, stop=True))
                yield S.pe_group(fns, reads=[rXf, r_tokm, r_ALLA], writes=[rb0])
                yield S.op("act", lambda e: e.activation(out=MN[:].rearrange("p c a k -> p (c a k)"), in_=bk0[0:64, 0:256], func=AF.Copy), reads=[rb0], writes=[r_MN])
                yield S.op("dve", lambda e: e.tensor_tensor(out=QT[:], in0=bk0[0:64, 256:512].rearrange("p (c t) -> p c t", t=128), in1=arT[:, :, 1, :], op=ALU.add),
                           reads=[rb0, r_ar], writes=[r_QT])
                for c in corder:
                    cs = slice(c * C, (c + 1) * C)
                    yield S.pe_group([lambda e, c=c: e.matmul(bk1[0:64, c * 128:(c + 1) * 128], lhsT=RA(Xf[:, c, 64:128]), rhs=RA(ALLA[:, c, 1, :]), start=True, stop=False),
                                      lambda e, c=c: e.matmul(bk1[0:64, c * 128:(c + 1) * 128], lhsT=RA(tokm[:, c, 3, :]), rhs=RA(ALLB[:, c, 1, :]), start=False, stop=False),
                                      lambda e, c=c: e.matmul(bk1[0:64, c * 128:(c + 1) * 128], lhsT=Hv, rhs=QT[:, c, :], start=False, stop=True),
                                      lambda e, c=c: e.matmul(bk1[0:64, 256:320], lhsT=MN[:, c, 0, :], rhs=Hv, start=True, stop=True)],
                                     reads=[rXf, r_tokm, r_ALLA, r_ALLB, r_QT, rH, r_MN], writes=[rb1])
                    pc = pin[:, (c + 1) * C - 1:(c + 1) * C]
                    yield S.op("dve", lambda e, c=c: e.tensor_tensor(out=Ht[:], in0=bk1[0:64, 256:320], in1=MN[:, c, 1, :], op=ALU.add), reads=[rb1, r_MN], writes=[r_Ht])
                    yield S.op("dve", lambda e: e.tensor_tensor(out=Ht[:], in0=Ht[:], in1=Hv, op=ALU.add), reads=[r_Ht, rH], writes=[r_Ht])
                    yield S.op("act", lambda e, pc=pc: e.activation(out=Hv, in_=Ht[:], func=AF.Identity, scale=pc), reads=[r_Ht, r_p], writes=[rH])
                yield S.op("act", lambda e: e.activation(out=yacc[:], in_=bk1[0:64, 0:256], func=AF.Copy), reads=[rb1], writes=[r_yacc])
                if not final:
                    yield S.dma("sp", lambda e: e.dma_start(out=YF[g, h, :, :], in_=yacc[:]), reads=[r_yacc], writes=[r_yf[g][h][0]])
                    yield S.dma("sp", lambda e: e.dma_start(out=RKF[g, h, :, :], in_=rkd[:]), reads=[r_rkd], writes=[r_yf[g][h][1]])
                else:
                    yield from head_final(g, h, lb)
                if g < NPG:
                    yield S.dma("sp", lambda e: e.dma_start(out=st_out[d, g, h, :, :], in_=Hs[:, h, :]), reads=[r_H[h]], is_output=True)

            return head_body

        streams = [make_stream(sid) for sid in range(KS)]

        def run_streams(gens, stagger=48):
            gens = list(gens)
            active = []
            for i, gen in enumerate(gens):
                if i > 0:
                    for _ in range(stagger):
                        for g_ in list(active):
                            try:
                                next(g_)
                            except StopIteration:
                                active.remove(g_)
                active.append(gen)
            while active:
                for g_ in list(active):
                    try:
                        next(g_)
                    except StopIteration:
                        active.remove(g_)

        def drain(gen):
            for _ in gen:
                pass

        def lora_prep(g, lb):
            is_grid = g >= NPG
            sq, st = g_seg(g)
            txw, xat, r_tx, sxg, r_sxg = txwL[lb], xatL[lb], r_txL[lb], sxgL[lb], r_sxgL[lb]
            for q in range(3):
                yield S.dma("sp", lambda e, q=q: e.dma_start(out=lin_big[:, q, :], in_=ZLP[sq][q, :, st:st + TP]), reads=[r_zpad], writes=[r_lin])
            for q in range(3):
                yield from shift_tile(lin_big[:, q, :], lsh[:, q, :], is_grid, 128, mus[:, q:q + 1], nbl, r_lin, r_lsh, r_nbl)
            yield S.op("act", lambda e: e.activation(out=txw[:], in_=lsh[0:64, 0, :], func=AF.Tanh), reads=[r_lsh], writes=[r_tx])
            yield S.op("act", lambda e: e.activation(out=xaf[:], in_=lsh[:, 0, :], func=AF.Copy), reads=[r_lsh], writes=[r_xaf])
            yield S.dma("sp", lambda e: e.dma_start(out=xat[:], in_=xaf[64:128, :]), reads=[r_xaf], writes=[r_tx])
            yield S.op("act", lambda e: e.activation(out=sxg[:, 0, :], in_=lsh[:, 1, :], func=AF.Sigmoid), reads=[r_lsh], writes=[r_sxg])
            yield S.op("act", lambda e: e.activation(out=sxg[0:32, 1, :], in_=lsh[0:32, 2, :], func=AF.Sigmoid), reads=[r_lsh], writes=[r_sxg])

        def rwkv_pass(d, final, order):
            lora_done = [-1]
            fin_cnt = [0] * len(order)

            def groups_done():
                k = 0
                while k < len(order) and fin_cnt[k] == KS:
                    k += 1
                return k

            def aux_gen():
                for n, g in enumerate(order):
                    while groups_done() < n - 1:
                        yield None
                    yield from lora_prep(g, n % 2)
                    lora_done[0] = n

            def stream_gen(sid):
                for n, g in enumerate(order):
                    while lora_done[0] < n:
                        yield None
                    lb = n % 2
                    for h in range(sid, NH, KS):
                        yield from streams[sid](g, d, final, h, lb)
                    fin_cnt[n] += 1

            run_streams([aux_gen()] + [stream_gen(sid) for sid in range(KS)])

        rwkv_pass(0, False, list(range(NG)))
        rwkv_pass(1, True, list(range(NG - 1, -1, -1)))

    with ExitStack() as stk:
        cur[0] = stk
        EA = dense_env()
        scond = sb("scond", [128, 16, 2]); r_scond = Res("scond")
        scondb = sb("scondb", [128, 16, 2], BF16)
        badas = sb("badas", [128, 144])
        S.dma("sp", lambda e: e.dma_start(out=scond[:], in_=condT[:, :, :]), writes=[r_scond])
        S.dma("sp", lambda e: e.dma_start(out=badas[:], in_=b_adaT[:, :]), writes=[r_const])
        S.op("act", lambda e: e.activation(out=scondb[:], in_=scond[:], func=AF.Silu), reads=[r_scond], writes=[r_scond])
        def ada_stage(i):
            for j in range(48 * i, 48 * (i + 1)):
                wv, rw = EA.load_w(w_ada[j, :, :, :], 128, 16, 128)
                pt, rp = next_pd()
                S.pe_group([(lambda e, k=k, wv=wv, pt=pt: e.matmul(pt[:, 0:2], lhsT=wv[:, k, :], rhs=scondb[:, k, :], start=(k == 0), stop=(k == 15)))
                            for k in range(16)], reads=[rw, r_scond], writes=[rp])
                S.op("dve", lambda e, j=j, pt=pt: e.tensor_scalar(out=mod[:, j, :], in0=pt[:, 0:2], scalar1=badas[:, j:j + 1], scalar2=None, op0=ALU.add),
                     reads=[rp, r_const], writes=[r_modS[i]])
            S.op("dve", lambda e: e.tensor_copy(out=modd[:, 3 * i + 0, :, :], in_=mod[:, (3 * i) * 16:(3 * i + 1) * 16, :]), reads=[r_modS[i]], writes=[r_modS[i]])
            S.op("dve", lambda e: e.tensor_scalar(out=modd[:, 3 * i + 1, :, :], in0=mod[:, (3 * i + 1) * 16:(3 * i + 2) * 16, :], scalar1=1.0, scalar2=None, op0=ALU.add),
                 reads=[r_modS[i]], writes=[r_modS[i]])
            S.op("dve", lambda e: e.tensor_scalar(out=modd[:, 3 * i + 2, :, :], in0=mod[:, (3 * i + 2) * 16:(3 * i + 3) * 16, :], scalar1=(1.0 if i == 1 else 0.5), scalar2=None, op0=ALU.mult),
                 reads=[r_modS[i]], writes=[r_modS[i]])

        ada_stage(0)

        stgL = [sb("stg", [128, TD]) for _ in range(2)]; r_stgL = [Res("stg0"), Res("stg1")]
        stgb = sb("stgb", [128, 4, TD], BF16); r_stgb = Res("stgb")
        vtm = sb("vtm", [128, DB]); r_vtm = Res("vtm")
        vnb = sb("vnb", [128, DB], BF16); r_vnb = Res("vnb")
        bnst = sb("bnst", [128, 2, 6]); bnag = sb("bnag", [128, 2]); r_bn = Res("bn")
        sgl = sb("sgl", [128, 2, DB]); r_sgl = Res("sgl")
        S.dma("sp", lambda e: e.dma_start(out=sgl[:], in_=sgln[:, :, :]), writes=[r_sgl])

        def phase_a(sg):
            cnd = sg_cnd(sg)
            segs = sg_segs(sg)
            xa_t, h_t, r_xa, r_h = EA.xa_t, EA.h_t, EA.r_xa, EA.r_h
            S.dma("sp", lambda e: e.dma_start(out=xa_t[:], in_=xT[sg, :, :, :]), writes=[r_xa])
            EA.modulate(0, cnd)
            EA.ffn(w_f1, w_f1o, 0, cnd)
            if sg == 0:
                ada_stage(1)
                ada_stage(2)
            EA.layernorm(0)
            S.dma("sp", lambda e: e.dma_start(out=X1[sg, :, :, :], in_=xa_t[:]), reads=[r_xa])
            EA.modulate(1, cnd)
            for q in range(24):
                def epi(pt, rp, q=q):
                    stg, r_stg = stgL[q % 2], r_stgL[q % 2]
                    S.op("act", lambda e: e.activation(out=stg[:, :], in_=pt[:, :], func=AF.Copy), reads=[rp], writes=[r_stg])
                    for (sq, st, n, co) in segs:
                        for half in range(2):
                            dst = ZR[sq][2 * q + half, :, PADZ + st:PADZ + st + n]
                            S.dma("sp", lambda e, dst=dst, co=co, n=n, half=half: e.dma_start(out=dst, in_=stg[half * 64:(half + 1) * 64, co:co + n]), reads=[r_stg])
                EA.proj_fm(w_rkv, q, 128, epi)
            for q in range(3):
                def epi(pt, rp, q=q):
                    stg, r_stg = stgL[q % 2], r_stgL[q % 2]
                    S.op("dve", lambda e: e.tensor_copy(out=stg[:, :], in_=pt[:, :]), reads=[rp], writes=[r_stg])
                    for (sq, st, n, co) in segs:
                        dst = ZLP[sq][q, :, PADZ + st:PADZ + st + n]
                        S.dma("sp", lambda e, dst=dst, co=co, n=n: e.dma_start(out=dst, in_=stg[:, co:co + n]), reads=[r_stg])
                EA.proj_fm(w_lora, q, 128, epi)
            for q in range(8):
                def epi(pt, rp, q=q):
                    S.op("act", lambda e: e.activation(out=stgb[:, q % 4, :], in_=pt[:, :], func=AF.Gelu), reads=[rp], writes=[r_stgb])
                    if q % 4 == 3:
                        S.dma("sp", lambda e: e.dma_start(out=UTs[sg, :, q - 3:q + 1, :], in_=stgb[:]), reads=[r_stgb])
                EA.proj_fm(w_u, q, 128, epi)
            for q in range(32):
                def epi(pt, rp, q=q):
                    S.op("act", lambda e: e.activation(out=stgb[:, q % 4, :], in_=pt[:, :], func=AF.Sigmoid), reads=[rp], writes=[r_stgb])
                    if q % 4 == 3:
                        S.dma("sp", lambda e: e.dma_start(out=GTs[sg, :, q - 3:q + 1, :], in_=stgb[:]), reads=[r_stgb])
                EA.proj_fm(w_gate, q, 128, epi)
            for tc in range(TD // 128):
                for cb in range(4):
                    wv, rw = EA.load_w(w_v[cb, :, :, :], 128, 16, 256)
                    pt, rp = next_pd()
                    S.pe_group([(lambda e, k=k, wv=wv, pt=pt, tc=tc: e.matmul(pt[:, 0:256], lhsT=h_t[:, k, tc * 128:(tc + 1) * 128], rhs=wv[:, k, :], start=(k == 0), stop=(k == 15)))
                                for k in range(16)], reads=[rw, r_h], writes=[rp])
                    S.op("act", lambda e, pt=pt, cb=cb: e.activation(out=vtm[:, cb * 256:(cb + 1) * 256], in_=pt[:, 0:256], func=AF.Gelu), reads=[rp], writes=[r_vtm])
                for cb in range(2):
                    S.op("dve", lambda e, cb=cb: e.bn_stats(out=bnst[:, cb, :], in_=vtm[:, cb * 512:(cb + 1) * 512]), reads=[r_vtm], writes=[r_bn])
                S.op("dve", lambda e: e.bn_aggr(out=bnag[:], in_=bnst[:].rearrange("p a b -> p (a b)")), reads=[r_bn], writes=[r_bn])
                S.op("dve", lambda e: e.tensor_scalar(out=bnag[:, 1:2], in0=bnag[:, 1:2], scalar1=LN_EPS, scalar2=None, op0=ALU.add), reads=[r_bn], writes=[r_bn])
                S.op("act", lambda e: e.activation(out=bnag[:, 1:2], in_=bnag[:, 1:2], func=AF.Sqrt), reads=[r_bn], writes=[r_bn])
                S.op("dve", lambda e: e.reciprocal(out=bnag[:, 1:2], in_=bnag[:, 1:2]), reads=[r_bn], writes=[r_bn])
                S.op("dve", lambda e: e.tensor_scalar(out=vtm[:], in0=vtm[:], scalar1=bnag[:, 0:1], scalar2=bnag[:, 1:2], op0=ALU.subtract, op1=ALU.mult),
                     reads=[r_bn, r_vtm], writes=[r_vtm])
                S.op("dve", lambda e: e.tensor_tensor(out=vtm[:], in0=vtm[:], in1=sgl[:, 0, :], op=ALU.mult), reads=[r_vtm, r_sgl], writes=[r_vtm])
                S.op("dve", lambda e: e.tensor_tensor(out=vnb[:], in0=vtm[:], in1=sgl[:, 1, :], op=ALU.add), reads=[r_vtm, r_sgl], writes=[r_vnb])
                S.dma("sp", lambda e, tc=tc: e.dma_start(out=VNs[sg, tc, :, :], in_=vnb[:]), reads=[r_vnb])

        for sg in range(NSG):
            phase_a(sg)
        S.barrier()
        cur[0] = None

    with ExitStack() as stk:
        cur[0] = stk
        rwkv_phase()
        S.barrier()
        cur[0] = None

    with ExitStack() as stk:
        cur[0] = stk
        EC = dense_env()
        outA = EC.GBIG[0:64, 0:NH * TD].rearrange("p (h t) -> p h t", t=TD)
        mrg = EC.GBIG[:, NH * TD:NH * TD + 16 * TD].rearrange("p (k t) -> p k t", t=TD)
        ut = EC.GBIG[:, 32 * TD:40 * TD].rearrange("p (k t) -> p k t", t=TD)
        r_G = EC.r_G
        gA = sb("gA", [128, TD], BF16); gB = sb("gB", [128, TD], BF16); r_gAB = Res("gAB")
        vn_in = sb("vn_in", [128, TD // 128, DB], BF16); r_vnin = Res("vnin")
        yB = sb("yB", [128, 8, TD], BF16); r_yB = Res("yB")
        sgws = sb("sgws", [128, 8, 128], BF16); r_sgw = Res("sgw")
        sgbs = sb("sgbs", [128, 8 * 128])
        S.dma("pool", lambda e: e.dma_start(out=sgws[:], in_=sgwT[:, :, :]), writes=[r_sgw])
        S.dma("sp", lambda e: e.dma_start(out=sgbs[:], in_=sgb[:, :]), writes=[r_sgw])
        mtmp = sb("mtmp", [128, TD]); r_mtmp = Res("mtmp")

        def phase_c(sg):
            cnd = sg_cnd(sg)
            xa_t, r_xa, sa_t, r_sa = EC.xa_t, EC.r_xa, EC.sa_t, EC.r_sa
            S.dma("sp", lambda e: e.dma_start(out=outA[:], in_=OA[sg, :, :, :]), writes=[r_G])
            S.dma("sp", lambda e: e.dma_start(out=ut[:], in_=UTs[sg, :, :, :]), writes=[r_G])
            S.dma("sp", lambda e: e.dma_start(out=vn_in[:], in_=VNs[sg, :, :, :].rearrange("t p c -> p t c")), writes=[r_vnin])
            for tc in range(TD // 128):
                for gg in range(8):
                    pt, rp = next_pd()
                    S.pe_group([lambda e, pt=pt, tc=tc, gg=gg: e.matmul(pt[:, 0:128], lhsT=vn_in[:, tc, gg * 128:(gg + 1) * 128], rhs=sgws[:, gg, :], start=True, stop=True)],
                               reads=[r_vnin, r_sgw], writes=[rp])
                    S.op("dve", lambda e, pt=pt, gg=gg: e.tensor_tensor(out=mtmp[:, 0:128], in0=pt[:, 0:128], in1=sgbs[:, gg * 128:(gg + 1) * 128], op=ALU.add),
                         reads=[rp, r_sgw], writes=[r_mtmp])
                    S.op("dve", lambda e, tc=tc, gg=gg: e.tensor_tensor(out=yB[:, gg, tc * 128:(tc + 1) * 128], in0=mtmp[:, 0:128], in1=ut[:, gg, tc * 128:(tc + 1) * 128], op=ALU.mult),
                         reads=[r_mtmp, r_G], writes=[r_yB])
            for dc in range(16):
                wfull, rwa = EC.load_w(w_pa[dc, :, :, :], 64, 16, 128)
                wpa_v = wfull
                wpb_v, rwb = EC.load_w(w_pb[dc, :, :, :], 128, 8, 128)
                S.dma("sp", lambda e, dc=dc: e.dma_start(out=gA[:], in_=GTs[sg, :, dc, :]), writes=[r_gAB])
                S.dma("sp", lambda e, dc=dc: e.dma_start(out=gB[:], in_=GTs[sg, :, 16 + dc, :]), writes=[r_gAB])
                pa, rpa = next_pd()
                pb, rpb = next_pd()
                S.pe_group([(lambda e, k=k, pa=pa, wpa_v=wpa_v: e.matmul(pa[:, :], lhsT=wpa_v[:, k, :], rhs=outA[:, k, :], start=(k == 0), stop=(k == 15))) for k in range(16)],
                           reads=[rwa, r_G], writes=[rpa])
                S.op("dve", lambda e, pa=pa: e.tensor_tensor(out=mtmp[:], in0=pa[:, :], in1=gA[:], op=ALU.mult), reads=[rpa, r_gAB], writes=[r_mtmp])
                S.pe_group([(lambda e, k=k, pb=pb, wpb_v=wpb_v: e.matmul(pb[:, :], lhsT=wpb_v[:, k, :], rhs=yB[:, k, :], start=(k == 0), stop=(k == 7))) for k in range(8)],
                           reads=[rwb, r_yB], writes=[rpb])
                S.op("dve", lambda e, pb=pb: e.tensor_tensor(out=sa_t[:], in0=pb[:, :], in1=gB[:], op=ALU.mult), reads=[rpb, r_gAB], writes=[r_sa])
                S.op("dve", lambda e, dc=dc: e.tensor_tensor(out=mrg[:, dc, :], in0=mtmp[:], in1=sa_t[:], op=ALU.add), reads=[r_mtmp, r_sa], writes=[r_G])
            S.dma("sp", lambda e: e.dma_start(out=xa_t[:], in_=X1[sg, :, :, :]), writes=[r_xa])
            for dc in range(16):
                wo, rwo = EC.load_w(w_o[dc, :, :, :], 128, 16, 128)
                po, rpo = next_pd()
                S.pe_group([(lambda e, k=k, wo=wo, po=po: e.matmul(po[:, :], lhsT=wo[:, k, :], rhs=mrg[:, k, :], start=(k == 0), stop=(k == 15))) for k in range(16)],
                           reads=[rwo, r_G], writes=[rpo])
                S.op("act", lambda e, dc=dc: e.activation(out=xa_t[:, dc, :], in_=xa_t[:, dc, :], func=AF.Identity, scale=ALPHA), reads=[r_xa], writes=[r_xa])
                S.op("dve", lambda e, dc=dc, po=po: e.scalar_tensor_tensor(out=xa_t[:, dc, :], in0=po[:, :], scalar=mcol(1, 2, dc, cnd), in1=xa_t[:, dc, :], op0=ALU.mult, op1=ALU.add),
                     reads=[rpo, r_xa, r_modS[1]], writes=[r_xa])
            EC.layernorm(1)
            EC.modulate(2, cnd)
            EC.ffn(w_f2, w_f2o, 2, cnd)
            EC.layernorm(2)
            S.dma("sp", lambda e: e.dma_start(out=yT[sg, :, :, :], in_=xa_t[:]), reads=[r_xa], is_output=True)

        for sg in range(NSG):
            phase_c(sg)
        cur[0] = None

    S.emit()
    return nc


def _chunk_w(W, cols=128):
    K, N = W.shape
    return np.ascontiguousarray(W.reshape(K // 128, 128, N // cols, cols).transpose(2, 1, 0, 3))


def _fm(x):
    T = x.shape[0]
    return np.ascontiguousarray(x.T.reshape(16, 128, T).transpose(1, 0, 2))


_NC_CACHE = {}


def kernel(x_prompt, x_sample, c, state_rwkv_fwd, state_rwkv_bwd, c_ctx, w_ada, b_ada, ln_g, ln_b,
           ffn_w_in, ffn_w_out, w_in, shift_mu, rw_w0, rw_w2, rw_a0, rw_a2, rw_g2, rw_k_k, rw_k_a,
           rw_r_k, rw_gn_g, rw_gn_b, sg_ln_g, sg_ln_b, sg_w, sg_b, w_pa, w_pb, w_o):
    f = lambda a: np.asarray(a, dtype=np.float32)
    x_prompt, x_sample, c, c_ctx = f(x_prompt), f(x_sample), f(c), f(c_ctx)
    w_in0 = f(w_in)[0]
    shared = {}
    shared["w_ada"] = _chunk_w(f(w_ada)[0])
    shared["b_adaT"] = np.ascontiguousarray(f(b_ada)[0].reshape(144, 128).T)
    lg, lb = f(ln_g)[0], f(ln_b)[0]
    shared["lnT"] = np.ascontiguousarray(np.concatenate([lg, lb], 0).reshape(6, 16, 128).transpose(2, 0, 1))
    for nm, i in (("w_f1", 0), ("w_f2", 1)):
        W = f(ffn_w_in)[0, i]
        Wc = _chunk_w(W)
        inter = np.empty_like(Wc)
        inter[0::2] = Wc[:43]
        inter[1::2] = Wc[43:]
        shared[nm] = inter
        Wo = f(ffn_w_out)[0, i]
        shared[nm + "o"] = np.ascontiguousarray(Wo.reshape(43, 128, 16, 128).transpose(2, 1, 0, 3))
    shared["w_rkv"] = _chunk_w(w_in0[:, :3072], 128)
    wl = np.zeros((2048, 384), np.float32)
    wl[:, :288] = w_in0[:, 3072:3360]
    shared["w_lora"] = _chunk_w(wl)
    shared["w_u"] = _chunk_w(w_in0[:, 3360:4384])
    shared["w_v"] = _chunk_w(w_in0[:, 4384:5408], 256)
    shared["w_gate"] = _chunk_w(w_in0[:, 5408:])
    mu = f(shift_mu)[0]
    p64 = np.zeros((64, 12, NH), np.float32)
    hp = lambda v: np.ascontiguousarray(v.reshape(NH, 64).T)
    p64[:, 0] = hp(mu[0:1024]); p64[:, 1] = hp(mu[1024:2048]); p64[:, 2] = hp(mu[2048:3072])
    kk_, ka_ = f(rw_k_k)[0], f(rw_k_a)[0]
    p64[:, 3] = hp(kk_); p64[:, 4] = hp(ka_)
    p64[:, 5] = hp(np.float32(1.0) - ka_)
    p64[:, 6] = hp(f(rw_r_k)[0].reshape(-1)); p64[:, 7] = hp(f(rw_gn_g)[0]); p64[:, 8] = hp(f(rw_gn_b)[0])
    p64[:, 9] = hp(f(rw_w0)[0, 0]); p64[:, 10] = hp(f(rw_w0)[0, 1])
    shared["p64"] = p64
    a0 = f(rw_a0)[0]
    shared["a0in"] = np.ascontiguousarray(np.stack([hp(a0[0]), hp(a0[1])], 1))
    shared["w2T"] = np.ascontiguousarray(f(rw_w2)[0].transpose(1, 0, 2))
    shared["a2T"] = np.ascontiguousarray(f(rw_a2)[0].transpose(1, 0, 2))
    g2 = f(rw_g2)[0]
    shared["g2a"] = np.ascontiguousarray(g2[:128]); shared["g2b"] = np.ascontiguousarray(g2[128:160])
    mul = np.zeros((384,), np.float32); mul[:288] = mu[3072:3360]
    shared["mu_l"] = np.ascontiguousarray(mul.reshape(3, 128).T)
    shared["sgln"] = np.ascontiguousarray(np.broadcast_to(np.stack([f(sg_ln_g)[0], f(sg_ln_b)[0]], 0)[None], (128, 2, DB)))
    shared["sgwT"] = np.ascontiguousarray(f(sg_w)[0].transpose(2, 0, 1))
    shared["sgb"] = np.ascontiguousarray(np.broadcast_to(f(sg_b)[0].reshape(1, -1), (128, 1024)))
    wpa = f(w_pa)[0]
    shared["w_pa"] = np.ascontiguousarray(wpa.reshape(NH, 64, 16, 128).transpose(2, 1, 0, 3))
    shared["w_pb"] = _chunk_w(f(w_pb)[0])
    shared["w_o"] = _chunk_w(f(w_o)[0])
    idx = np.arange(128)
    m = np.zeros((128, 2, 3, 128), np.float32)
    m[:, 0, 0] = (idx[:, None] < idx[None, :]); m[:, 0, 1] = (idx[:, None] <= idx[None, :]); m[:, 0, 2] = (idx[:, None] > idx[None, :])
    m[:, 1, 0] = (idx[:, None] > idx[None, :]); m[:, 1, 1] = (idx[:, None] >= idx[None, :]); m[:, 1, 2] = (idx[:, None] < idx[None, :])
    shared["masks"] = m
    shared["ident_in"] = np.eye(128, dtype=np.float32)

    sf, sbw = f(state_rwkv_fwd), f(state_rwkv_bwd)
    in_maps = []
    for core in range(8):
        b = core % 2
        xs = []
        for gi in range(2):
            xs.append(_fm(x_prompt[4 * core + 2 * gi: 4 * core + 2 * gi + 2].reshape(TD, D)))
        for gi in range(4):
            xs.append(_fm(x_sample[b, gi * TD:(gi + 1) * TD]))
        mp = dict(shared)
        mp["xT"] = np.ascontiguousarray(np.stack(xs, 0))
        cond = np.stack([c_ctx, c[b]], 1)
        mp["condT"] = np.ascontiguousarray(cond.reshape(16, 128, 2).transpose(1, 0, 2))
        h0 = np.stack([sf[b, 0].transpose(0, 2, 1), sbw[b, 0].transpose(0, 2, 1)], 0)
        mp["h0in"] = np.ascontiguousarray(h0)
        in_maps.append(mp)

    if "nc" not in _NC_CACHE:
        _NC_CACHE["nc"] = build_program()
    nc = _NC_CACHE["nc"]
    res = run_bass_kernel_spmd(nc, in_maps, core_ids=list(range(8)))

    def unfm(y):
        return y.transpose(1, 0, 2).reshape(D, -1).T

    y_prompt = np.zeros((32, 256, D), np.float32)
    y_sample = np.zeros((2, 2048, D), np.float32)
    nsf = np.zeros((32, 1, NH, HD, HD), np.float32)
    nsb = np.zeros((32, 1, NH, HD, HD), np.float32)
    for core in range(8):
        r = res.results[core]
        yt = np.asarray(r["yT"])
        for gi in range(2):
            y_prompt[4 * core + 2 * gi: 4 * core + 2 * gi + 2] = unfm(yt[gi]).reshape(2, 256, D)
        if core < 2:
            for gi in range(4):
                y_sample[core, gi * TD:(gi + 1) * TD] = unfm(yt[2 + gi])
        so = np.asarray(r["st_out"])
        nsf[4 * core:4 * core + 4, 0] = so[0].transpose(0, 1, 3, 2)
        nsb[4 * core:4 * core + 4, 0] = so[1].transpose(0, 1, 3, 2)
    return (y_prompt, y_sample, nsf, nsb)
```
